# Optimizing a Trainium2 kernel written in Bass

```python
import math
import jax, jax.numpy as jnp
from jax import lax
import numpy as np

D_MODEL = 1024
BATCH = 16
SEQ = 256
DEPTH = 4
DEC_BATCH = 2
DEC_SEQ = 1024
PAST_LEN = 256

GRID_W = 64
BLOCK = 128
GROUP_W = D_MODEL // 4
A_HEADS = 4
A_HD = GROUP_W // A_HEADS
B_HEADS = 4
B_KV = 2
B_HD = GROUP_W // B_HEADS
B_GROUP = B_HEADS // B_KV
WINDOW = 128
C_HEADS = 4
C_HD = GROUP_W // C_HEADS
C_SUB = C_HD // 2
D_GROUPS = 4
D_GW = GROUP_W // D_GROUPS
POOL_SIZES = (2, 4, 8, 16)
PROJ_WIDTHS = (GROUP_W, GROUP_W,
               B_HEADS * B_HD, B_KV * B_HD, B_KV * B_HD,
               GROUP_W, GROUP_W, GROUP_W,
               GROUP_W)
P_COLS = sum(PROJ_WIDTHS)
PROJ_SPLITS = [int(s) for s in np.cumsum(PROJ_WIDTHS)[:-1]]
PK_HEADS = 8
N_KEYS = 128
N_EXPERTS = N_KEYS * N_KEYS
D_KEY = 128
PK_TOPK = 16
ROPE_BASE = 10000.0
ALPHA = (2 * DEPTH) ** 0.25
BETA = (8 * DEPTH) ** -0.25
LN_EPS = 1e-5

kernel_name = 'hybrid_flow_prefix_step'


def layer_norm(x):
    xf = x.astype(jnp.float32)
    mu = jnp.mean(xf, axis=-1, keepdims=True)
    var = jnp.mean(jnp.square(xf - mu), axis=-1, keepdims=True)
    return ((xf - mu) * lax.rsqrt(var + LN_EPS)).astype(x.dtype)


def rms_norm(x, g):
    xf = x.astype(jnp.float32)
    return (xf * lax.rsqrt(jnp.mean(xf * xf, axis=-1, keepdims=True) + LN_EPS)).astype(x.dtype) * g


def axial_rope(n_tok, dim):
    rows = n_tok // GRID_W
    row = jnp.repeat(jnp.arange(rows, dtype=jnp.float32), GRID_W)
    col = jnp.tile(jnp.arange(GRID_W, dtype=jnp.float32), rows)
    nf = dim // 4
    inv = ROPE_BASE ** (-jnp.arange(nf, dtype=jnp.float32) / nf)
    ar = row[:, None] * inv
    ac = col[:, None] * inv
    ang = jnp.concatenate([ar, ar, ac, ac], axis=-1)
    return jnp.cos(ang), jnp.sin(ang)


def apply_rope(x, rope):
    cos, sin = rope
    shp = (cos.shape[0],) + (1,) * (x.ndim - 3) + (cos.shape[1],)
    cos = cos.reshape(shp)
    sin = sin.reshape(shp)
    x1, x2, x3, x4 = jnp.split(x, 4, axis=-1)
    rot = jnp.concatenate([-x2, x1, -x4, x3], axis=-1)
    return (x * cos + rot * sin).astype(x.dtype)


def sweep_query_blocks(fn, q):
    b, t = q.shape[:2]
    qb = jnp.moveaxis(q.reshape((b, t // BLOCK, BLOCK) + q.shape[2:]), 1, 0)
    out = jnp.moveaxis(lax.map(fn, qb), 0, 1)
    return out.reshape((b, t) + out.shape[3:])


def softmax_with_sink(logits, sink):
    full = jnp.concatenate([logits, jnp.broadcast_to(sink, logits.shape[:-1] + (1,))], axis=-1)
    return jax.nn.softmax(full, axis=-1)[..., :-1]


def chunk_gating(u, v, w_s, b_s):
    b, s, h, d = v.shape
    vc = v.reshape(b, s // BLOCK, BLOCK, h, d)
    mixed = jnp.einsum('hpq,bnqhd->bnphd', w_s, vc) + b_s.T[None, None, :, :, None]
    return u * mixed.reshape(b, s, h, d)


def window_attn_context(q, k, v, sink):
    scale = B_HD ** -0.5
    sink = sink.astype(jnp.float32).reshape(B_KV, B_GROUP, 1, 1)

    def blk(qb):
        qg = qb.reshape(qb.shape[0], BLOCK, B_KV, B_GROUP, B_HD)
        s = jnp.einsum('bqkgd,bskd->bkgqs', qg, k).astype(jnp.float32) * scale
        p = softmax_with_sink(s, sink).astype(v.dtype)
        return jnp.einsum('bkgqs,bskd->bqkgd', p, v).reshape(qb.shape)

    return sweep_query_blocks(blk, q)


def window_attn_latent(q, k, v, k_ctx, v_ctx, sink):
    b, t = q.shape[:2]
    nb = t // BLOCK
    scale = B_HD ** -0.5
    qb = q.reshape(b, nb, BLOCK, B_KV, B_GROUP, B_HD)

    def band(x):
        xb = x.reshape(b, nb, BLOCK, B_KV, B_HD)
        xp = jnp.pad(xb, ((0, 0), (1, 1), (0, 0), (0, 0), (0, 0)))
        return jnp.concatenate([xp[:, :-2], xp[:, 1:-1], xp[:, 2:]], axis=2)

    kb, vb = band(k), band(v)
    qpos = jnp.arange(nb)[:, None] * BLOCK + jnp.arange(BLOCK)[None, :]
    kpos = (jnp.arange(nb)[:, None] - 1) * BLOCK + jnp.arange(3 * BLOCK)[None, :]
    rel = kpos[:, None, :] - qpos[:, :, None]
    valid = (jnp.abs(rel) <= WINDOW) & (kpos[:, None, :] >= 0) & (kpos[:, None, :] < t)
    s_loc = jnp.einsum('bnqkgd,bnskd->bnkgqs', qb, kb).astype(jnp.float32) * scale
    s_loc = jnp.where(valid[None, :, None, None], s_loc, -jnp.inf)
    s_ctx = jnp.einsum('bnqkgd,bskd->bnkgqs', qb, k_ctx).astype(jnp.float32) * scale
    p = softmax_with_sink(jnp.concatenate([s_loc, s_ctx], axis=-1),
                          sink.astype(jnp.float32).reshape(B_KV, B_GROUP, 1, 1)).astype(v.dtype)
    o = (jnp.einsum('bnkgqs,bnskd->bnqkgd', p[..., :3 * BLOCK], vb)
         + jnp.einsum('bnkgqs,bskd->bnqkgd', p[..., 3 * BLOCK:], v_ctx))
    return o.reshape(b, t, B_HEADS, B_HD)


def diff_attn(q, k, v, lam):
    scale = C_SUB ** -0.5

    def blk(qb):
        s = jnp.einsum('bqhjd,bshjd->bhjqs', qb, k).astype(jnp.float32) * scale
        p = jax.nn.softmax(s, axis=-1)
        w = p[:, :, 0] - lam * p[:, :, 1]
        return jnp.einsum('bhqs,bshd->bqhd', w.astype(v.dtype), v)

    return sweep_query_blocks(blk, q)


def multiscale_pool(z, w_pool, pool_scale):
    b, s, _ = z.shape
    zf = z.astype(jnp.float32)
    cs = jnp.concatenate([jnp.zeros((b, 1, GROUP_W), jnp.float32), jnp.cumsum(zf, axis=1)], axis=1)
    t = jnp.arange(s)
    outs = []
    for g, w in enumerate(POOL_SIZES):
        lo = jnp.clip(t - w // 2, 0, s)
        hi = jnp.clip(t + w // 2, 0, s)
        c0, c1 = g * D_GW, (g + 1) * D_GW
        mean = (cs[:, hi, c0:c1] - cs[:, lo, c0:c1]) / (hi - lo).astype(jnp.float32)[None, :, None]
        outs.append(mean - zf[:, :, c0:c1])
    pooled = jnp.stack(outs, axis=2)
    y = jnp.einsum('bsgc,gcd->bsgd', pooled, w_pool).reshape(b, s, GROUP_W)
    return (y * pool_scale).astype(z.dtype)


def peer(x, w_q, sub_keys, u_tab, v_tab):
    b, s, d = x.shape
    xt = x.reshape(-1, d)
    nt = xt.shape[0]
    q = (xt @ w_q).reshape(nt, PK_HEADS, 2, D_KEY // 2)
    sc = jnp.einsum('thjd,jnd->thjn', q, sub_keys).astype(jnp.float32)
    sv, si = lax.top_k(sc, PK_TOPK)
    cand = sv[:, :, 0, :, None] + sv[:, :, 1, None, :]
    cv, ci = lax.top_k(cand.reshape(nt, PK_HEADS, PK_TOPK * PK_TOPK), PK_TOPK)
    i1 = jnp.take_along_axis(si[:, :, 0], ci // PK_TOPK, axis=-1)
    i2 = jnp.take_along_axis(si[:, :, 1], ci % PK_TOPK, axis=-1)
    idx = i1 * N_KEYS + i2
    gate = jax.nn.softmax(cv, axis=-1)

    def blk(args):
        xb, ib, gb = args
        ue = jnp.take(u_tab, ib, axis=0)
        act = jax.nn.gelu(jnp.einsum('thkd,td->thk', ue, xb).astype(jnp.float32))
        ve = jnp.take(v_tab, ib, axis=0)
        return jnp.einsum('thk,thkd->td', (gb * act).astype(xb.dtype), ve)

    nblk = nt // BLOCK
    out = lax.map(blk, (xt.reshape(nblk, BLOCK, d),
                        idx.reshape(nblk, BLOCK, PK_HEADS, PK_TOPK),
                        gate.reshape(nblk, BLOCK, PK_HEADS, PK_TOPK)))
    return out.reshape(b, s, d)


def trunk_layer(x, mod, w_in, w_out, chunk_w, chunk_b, sink, lam, subln_g, lam_init,
                pool_w, pool_scale, ln_g, ln_b, peer_wq, peer_keys, peer_u, peer_v, latent=None):
    b, s, _ = x.shape
    sh1, sc1, g1, sh2, sc2, g2 = jnp.split(mod, 6, axis=-1)
    h = layer_norm(x) * (1 + sc1) + sh1
    a_u, a_v, bq, bk, bv, cq, ck, cv, dz = jnp.split(h @ w_in, PROJ_SPLITS, axis=-1)
    u = jax.nn.gelu(a_u).reshape(b, s, A_HEADS, A_HD)
    v = jax.nn.gelu(a_v).reshape(b, s, A_HEADS, A_HD)
    o_a = chunk_gating(u, v, chunk_w, chunk_b).reshape(b, s, GROUP_W)
    bq = bq.reshape(b, s, B_HEADS, B_HD)
    bk = bk.reshape(b, s, B_KV, B_HD)
    bv = bv.reshape(b, s, B_KV, B_HD)
    cq = cq.reshape(b, s, C_HEADS, 2, C_SUB)
    ck = ck.reshape(b, s, C_HEADS, 2, C_SUB)
    cv = cv.reshape(b, s, C_HEADS, C_HD)
    if latent is None:
        o_b = window_attn_context(bq, bk, bv, sink)
        o_c = diff_attn(cq, ck, cv, lam)
    else:
        rope_b, rope_c, kb_ctx, vb_ctx, kc_ctx, vc_ctx = latent
        o_b = window_attn_latent(apply_rope(bq, rope_b), apply_rope(bk, rope_b), bv, kb_ctx, vb_ctx, sink)
        o_c = diff_attn(apply_rope(cq, rope_c),
                        jnp.concatenate([kc_ctx, apply_rope(ck, rope_c)], axis=1),
                        jnp.concatenate([vc_ctx, cv], axis=1), lam)
    o_c = rms_norm(o_c, subln_g) * (1.0 - lam_init)
    o_d = multiscale_pool(dz, pool_w, pool_scale)
    o = jnp.concatenate([o_a, o_b.reshape(b, s, GROUP_W), o_c.reshape(b, s, GROUP_W), o_d], axis=-1) @ w_out
    x = layer_norm(ALPHA * x + g1 * o) * ln_g[0] + ln_b[0]
    h = layer_norm(x) * (1 + sc2) + sh2
    f = peer(h, peer_wq, peer_keys, peer_u, peer_v)
    x = layer_norm(ALPHA * x + g2 * f) * ln_g[1] + ln_b[1]
    return x, (bk, bv, ck, cv)


def setup_inputs(seed: int = 0) -> dict:
    key = jax.random.key(seed)
    ks = jax.random.split(key, 26)

    def nrm(k, shape, s):
        return jax.random.normal(k, shape, jnp.float32) * s

    return {
        'x_prompt': nrm(ks[0], (BATCH, SEQ, D_MODEL), 1.0),
        'x_sample': nrm(ks[1], (DEC_BATCH, DEC_SEQ, D_MODEL), 1.0),
        'c': nrm(ks[2], (DEC_BATCH, D_MODEL), 1.0),
        'cache_win_k': nrm(ks[3], (DEC_BATCH, DEPTH, PAST_LEN, B_KV, B_HD), 1.0),
        'cache_win_v': nrm(ks[4], (DEC_BATCH, DEPTH, PAST_LEN, B_KV, B_HD), 1.0),
        'cache_diff_k': nrm(ks[5], (DEC_BATCH, DEPTH, PAST_LEN, C_HEADS, 2, C_SUB), 1.0),
        'cache_diff_v': nrm(ks[6], (DEC_BATCH, DEPTH, PAST_LEN, C_HEADS, C_HD), 1.0),
        'c_ctx': nrm(ks[7], (D_MODEL,), 1.0),
        'w_mod': nrm(ks[8], (DEPTH, D_MODEL, 6 * D_MODEL), 0.5 * D_MODEL ** -0.5),
        'b_mod': nrm(ks[9], (DEPTH, 6 * D_MODEL), 0.01),
        'w_in': nrm(ks[10], (DEPTH, D_MODEL, P_COLS), D_MODEL ** -0.5),
        'w_out': nrm(ks[11], (DEPTH, D_MODEL, D_MODEL), BETA * D_MODEL ** -0.5),
        'chunk_w': nrm(ks[12], (DEPTH, A_HEADS, BLOCK, BLOCK), BLOCK ** -0.5),
        'chunk_b': 1.0 + nrm(ks[13], (DEPTH, A_HEADS, BLOCK), 0.01),
        'win_sink': nrm(ks[14], (DEPTH, B_HEADS), 0.5),
        'diff_lam_q': nrm(ks[15], (DEPTH, 2, C_SUB), 0.1),
        'diff_lam_k': nrm(ks[16], (DEPTH, 2, C_SUB), 0.1),
        'diff_subln_g': 1.0 + nrm(ks[17], (DEPTH, C_HD), 0.02),
        'pool_w': nrm(ks[18], (DEPTH, D_GROUPS, D_GW, D_GW), D_GW ** -0.5),
        'pool_scale': 1.0 + nrm(ks[19], (DEPTH, GROUP_W), 0.02),
        'ln_g': 1.0 + nrm(ks[20], (DEPTH, 2, D_MODEL), 0.02),
        'ln_b': nrm(ks[21], (DEPTH, 2, D_MODEL), 0.01),
        'peer_wq': nrm(ks[22], (DEPTH, D_MODEL, PK_HEADS * D_KEY), D_MODEL ** -0.5),
        'peer_keys': nrm(ks[23], (DEPTH, 2, N_KEYS, D_KEY // 2), (D_KEY // 2) ** -0.5),
        'peer_u': nrm(ks[24], (DEPTH, N_EXPERTS, D_MODEL), D_MODEL ** -0.5),
        'peer_v': nrm(ks[25], (DEPTH, N_EXPERTS, D_MODEL), BETA),
    }


def reference(x_prompt, x_sample, c, cache_win_k, cache_win_v, cache_diff_k, cache_diff_v, c_ctx,
              w_mod, b_mod, w_in, w_out, chunk_w, chunk_b, win_sink, diff_lam_q, diff_lam_k,
              diff_subln_g, pool_w, pool_scale, ln_g, ln_b, peer_wq, peer_keys, peer_u, peer_v):
    n_lat = x_sample.shape[1]
    rope_b = axial_rope(n_lat, B_HD)
    rope_c = axial_rope(n_lat, C_SUB)
    y_p, y_s = x_prompt, x_sample
    kbs, vbs, kcs, vcs = [], [], [], []
    for l in range(DEPTH):
        lam_init = 0.8 - 0.6 * math.exp(-0.3 * l)
        lam = (jnp.exp(jnp.sum(diff_lam_q[l, 0] * diff_lam_k[l, 0]).astype(jnp.float32))
               - jnp.exp(jnp.sum(diff_lam_q[l, 1] * diff_lam_k[l, 1]).astype(jnp.float32)) + lam_init)
        shared = (w_in[l], w_out[l], chunk_w[l], chunk_b[l], win_sink[l], lam, diff_subln_g[l], lam_init,
                  pool_w[l], pool_scale[l], ln_g[l], ln_b[l], peer_wq[l], peer_keys[l], peer_u[l], peer_v[l])
        mod_ctx = (jax.nn.silu(c_ctx) @ w_mod[l] + b_mod[l])[None, None, :]
        y_p, (kb, vb, kc, vc) = trunk_layer(y_p, mod_ctx, *shared)
        kbs.append(kb)
        vbs.append(vb)
        kcs.append(kc)
        vcs.append(vc)
        mod_lat = (jax.nn.silu(c) @ w_mod[l] + b_mod[l])[:, None, :]
        y_s, _ = trunk_layer(y_s, mod_lat, *shared,
                             latent=(rope_b, rope_c, cache_win_k[:, l], cache_win_v[:, l],
                                     cache_diff_k[:, l], cache_diff_v[:, l]))
    new_win_k = jnp.stack(kbs, axis=1)
    new_win_v = jnp.stack(vbs, axis=1)
    new_diff_k = jnp.stack(kcs, axis=1)
    new_diff_v = jnp.stack(vcs, axis=1)
    return (y_p, y_sample_out := y_s, new_win_k, new_win_v, new_diff_k, new_diff_v)
```

```python
import math
import numpy as np
import concourse.bass as bass
import concourse.mybir as mybir
from concourse.bass_utils import run_bass_kernel_spmd

F32 = mybir.dt.float32
BF16 = mybir.dt.bfloat16
I32 = mybir.dt.int32
U32 = mybir.dt.uint32
AF = mybir.ActivationFunctionType
ALU = mybir.AluOpType
AX = mybir.AxisListType

DEPTH = 4
D = 1024
NT = 12
NEXP = 16384
ALPHA = (2 * DEPTH) ** 0.25
LN_EPS = 1e-5
NEG = -30000.0
NGB = 8
DEBUG = {}
CFG = {"depth": DEPTH, "stop": None, "nexp": NEXP}


class _Stop(Exception):
    pass


C_AU, C_AV, C_BQ, C_BK, C_BV, C_CQ, C_CK, C_CV, C_DZ = 0, 256, 512, 768, 896, 1024, 1280, 1536, 1792


class Sched:
    ENG = ("pe", "act", "dve", "pool", "sp")

    def __init__(self, n_dma_sems):
        self.q = {e: [] for e in self.ENG}
        self.cnt = {e: 0 for e in self.ENG}
        self.seen = {e: {} for e in self.ENG}
        self.last_w = {}
        self.readers = {}
        self.dma_n = [0] * n_dma_sems
        self.rr = 0
        self.n_general = n_dma_sems

    def _collect(self, eng, reads, writes, extra=(), raw=None):
        deps = {}
        own = ("e", eng)
        raw = set(reads if raw is None else raw)

        def add(d, same_ok):
            if d is None:
                return
            s, v = d
            if s == own and not same_ok:
                return
            if deps.get(s, 0) < v:
                deps[s] = v
        same_raw = eng != "pe"
        for k in reads:
            add(self.last_w.get(k), same_raw)
        for k in writes:
            add(self.last_w.get(k), same_raw and k in raw)
            for d in self.readers.get(k, {}).items():
                add(d, False)
        for d in extra:
            add(d, True)
        waits = []
        sn = self.seen[eng]
        for s, v in deps.items():
            if sn.get(s, 0) < v:
                sn[s] = v
                waits.append((s, v))
        return waits

    def _commit(self, me, reads, writes):
        for k in writes:
            self.last_w[k] = me
            self.readers[k] = {}
        for k in reads:
            r = self.readers.setdefault(k, {})
            if r.get(me[0], 0) < me[1]:
                r[me[0]] = me[1]

    @staticmethod
    def _norm(reads, writes):
        ps = [k for k in reads if k.startswith("ps")]
        if ps:
            reads = [k for k in reads if not k.startswith("ps")]
            writes = list(writes) + [k for k in ps if k not in writes]
        return reads, writes

    def op(self, eng, fn, reads=(), writes=(), extra=()):
        raw = list(reads)
        reads, writes = self._norm(reads, writes)
        waits = self._collect(eng, reads, writes, extra, raw)
        self.cnt[eng] += 1
        me = (("e", eng), self.cnt[eng])
        self.q[eng].append((waits, fn, me, 1))
        self._commit(me, reads, writes)

    def snapshot(self):
        snap = [(("e", e), self.cnt[e]) for e in self.ENG if self.cnt[e]]
        snap += [(("d", i), 16 * n) for i, n in enumerate(self.dma_n) if n]
        return snap

    def dma(self, eng, fn, reads=(), writes=(), sem=None):
        raw = list(reads)
        reads, writes = self._norm(reads, writes)
        if sem is None:
            sem = self.rr
            self.rr = (self.rr + 1) % self.n_general
        prev = (("d", sem), 16 * self.dma_n[sem]) if self.dma_n[sem] else None
        waits = self._collect(eng, reads, writes, extra=(prev,), raw=raw)
        self.dma_n[sem] += 1
        me = (("d", sem), 16 * self.dma_n[sem])
        self.q[eng].append((waits, fn, me, 16))
        self._commit(me, reads, writes)

    def final_waits(self, eng):
        waits = []
        for i, n in enumerate(self.dma_n):
            if n:
                waits.append((("d", i), 16 * n))
        for e in self.ENG:
            if e != eng and self.cnt[e]:
                waits.append((("e", e), self.cnt[e]))
        return waits


def _rope_tables(dim):
    n_tok, grid = 1024, 64
    rows = n_tok // grid
    row = np.repeat(np.arange(rows, dtype=np.float32), grid)
    col = np.tile(np.arange(grid, dtype=np.float32), rows)
    nf = dim // 4
    inv = (np.float32(10000.0) ** (-np.arange(nf, dtype=np.float32) / np.float32(nf))).astype(np.float32)
    ar = row[:, None] * inv
    ac = col[:, None] * inv
    ang = np.concatenate([ar, ar, ac, ac], axis=-1).astype(np.float32)
    return np.cos(ang).astype(np.float32), np.sin(ang).astype(np.float32)


def _rot_matrix(dim, reps):
    q = dim // 4
    r = np.zeros((dim, dim), np.float32)
    for a in range(q):
        r[q + a, a] = -1.0
        r[a, q + a] = 1.0
        r[3 * q + a, 2 * q + a] = -1.0
        r[2 * q + a, 3 * q + a] = 1.0
    out = np.zeros((dim * reps, dim * reps), np.float32)
    for i in range(reps):
        out[i * dim:(i + 1) * dim, i * dim:(i + 1) * dim] = r
    return out


def _pool_mats():
    S = 384
    out = np.zeros((5, 4, 128, 128), np.float32)
    for g, w in enumerate((2, 4, 8, 16)):
        def mat(seq_lo, seq_hi):
            m = np.zeros((S, S), np.float32)
            for t in range(128, 256):
                lo = min(max(t - w // 2, seq_lo), seq_hi)
                hi = min(max(t + w // 2, seq_lo), seq_hi)
                m[t, lo:hi] = 1.0 / float(hi - lo)
                m[t, t] -= 1.0
            return m
        mid = mat(0, S)
        first = mat(128, S)
        last = mat(0, 256)
        out[0, g] = first[128:256, 128:256].T
        out[1, g] = mid[128:256, 128:256].T
        out[2, g] = last[128:256, 128:256].T
        out[3, g] = mid[128:256, 0:128].T
        out[4, g] = mid[128:256, 256:384].T
    return out


def _band_bias():
    qi = np.arange(128)[:, None]
    ki = np.arange(128)[None, :]
    prev = np.where(ki >= qi, 0.0, NEG).astype(np.float32)
    nxt = np.where(ki <= qi, 0.0, NEG).astype(np.float32)
    z = np.zeros((128, 128), np.float32)
    full = np.full((128, 128), NEG, np.float32)
    return np.stack([np.concatenate([full, z, nxt], 1), np.concatenate([prev, z, nxt], 1),
                     np.concatenate([prev, z, full], 1)], 0)


def build_program():
    nc = bass.Bass("TRN2", target_bir_lowering=False)

    def din(name, shape, dt=F32):
        return nc.dram_tensor(name, list(shape), dt, kind="ExternalInput")

    def dout(name, shape, dt=F32):
        return nc.dram_tensor(name, list(shape), dt, kind="ExternalOutput")

    d_x = din("xin", [NT, 128, D]).ap()
    d_cvec = din("cvec", [128, 2, 8]).ap()
    d_cwkT = din("cwkT", [DEPTH, 128, 256]).ap()
    d_cwv = din("cwv", [DEPTH, 256, 128]).ap()
    d_cdkT = din("cdkT", [DEPTH, 64, 4, 256]).ap()
    d_cdv = din("cdv", [DEPTH, 256, 256]).ap()
    d_wmod = din("w_mod", [CFG["depth"], D, 6 * D]).ap()
    h_bmod = din("b_mod", [DEPTH, 6 * D])
    d_win = din("w_in", [CFG["depth"], D, 2048]).ap()
    d_wout = din("w_out", [CFG["depth"], D, D]).ap()
    d_cwT = din("chunk_wT", [DEPTH, 128, 4, 128]).ap()
    d_cbT = din("chunk_bT", [DEPTH, 128, 4]).ap()
    h_sink = din("win_sink", [DEPTH, 4])
    h_lamq = din("lam_q", [DEPTH, 64])
    h_lamk = din("lam_k", [DEPTH, 64])
    h_subg = din("subln_g", [DEPTH, 64])
    d_poolw = din("pool_wr", [DEPTH, 64, 4, 64]).ap()
    h_pscale = din("pool_scale", [DEPTH, 256])
    h_lng = din("ln_g", [DEPTH, 2, D])
    h_lnb = din("ln_b", [DEPTH, 2, D])
    d_wq = din("peer_wq", [CFG["depth"], D, D]).ap()
    d_kb = din("peer_kb", [DEPTH, 128, 256]).ap()
    d_puv = din("peer_uv", [CFG["depth"] * CFG["nexp"], 2 * D]).ap()
    d_ident = din("c_ident", [128, 128]).ap()
    d_rb = din("c_rb", [128, 128]).ap()
    d_rc = din("c_rc", [64, 64]).ap()
    d_cosb = din("c_cosb", [128, 1024]).ap()
    d_sinb = din("c_sinb", [128, 1024]).ap()
    d_cosc = din("c_cosc", [64, 1024]).ap()
    d_sinc = din("c_sinc", [64, 1024]).ap()
    d_band = din("c_band", [3, 128, 384]).ap()
    d_poolat = din("c_poolat", [128, 5, 4, 128]).ap()

    o_y = dout("y", [NT, 128, D]).ap()
    o_wk = dout("o_wk", [2, DEPTH, 256, 128]).ap()
    o_wv = dout("o_wv", [2, DEPTH, 256, 128]).ap()
    o_dk = dout("o_dk", [2, DEPTH, 256, 256]).ap()
    o_dv = dout("o_dv", [2, DEPTH, 256, 256]).ap()
    dbg_out = {}
    for name, shp in DEBUG.items():
        dbg_out[name] = dout("dbg_" + name, shp).ap()

    def bcast_row(h, off, n, parts=128):
        return bass.AP(h, off, [[0, parts], [1, n]])

    def sb(name, shape, dt):
        return nc.alloc_sbuf_tensor(name, list(shape), dt)

    x = sb("x", [128, NT, D], F32)
    mod = sb("mod", [128, 2, 3072], F32)
    reg1 = sb("reg1", [128, 24576], BF16)
    A16N, A32N = 21760, 6784
    a16 = sb("a16", [128, A16N], BF16)
    a32 = sb("a32", [128, A32N], F32)
    idx_i = sb("idx_i", [128, 3, 128], I32)
    si_u = sb("si_u", [128, 4, 16], U32)
    ident_f = sb("ident_f", [128, 128], F32)
    ident_b = sb("ident_b", [128, 128], BF16)
    rb_f = sb("rb_f", [128, 128], F32)
    rc_f = sb("rc_f", [64, 64], F32)
    band = sb("band", [128, 3, 384], BF16)
    poolat = sb("poolat", [128, 5, 4, 128], BF16)
    silu_b = sb("silu_b", [128, 2, 8], BF16)
    cvec = sb("cvec_sb", [128, 2, 8], F32)
    cwT = sb("cwT", [128, 4, 128], BF16)
    cbT = sb("cbT", [128, 4], F32)
    sink = sb("sink", [128, 4], F32)
    lamt = sb("lamt", [128, 2, 64], F32)
    lamv = sb("lamv", [128, 8], F32)
    subg = sb("subg", [128, 64], F32)
    poolw = sb("poolw", [64, 4, 64], BF16)
    pscale = sb("pscale", [128, 256], F32)
    kb_b = sb("kb_b", [128, 256], BF16)
    st6 = sb("st6", [128, 2, 6], F32)
    sm = sb("sm", [128, 32], F32)

    psA = nc.alloc_psum_tensor("psA", [128, 2048], F32)
    psB = nc.alloc_psum_tensor("psB", [128, 1024], F32)
    psC = nc.alloc_psum_tensor("psC", [128, 512], F32)
    psT = nc.alloc_psum_tensor("psT", [128, 1024], BF16)

    class Carver:
        def __init__(self, t, n):
            self.t, self.n, self.o = t, n, 0

        def take(self, size, pat=None, parts=128, **kw):
            assert self.o + size <= self.n, (self.o, size, self.n)
            ap = self.t[0:parts, self.o:self.o + size]
            self.o += size
            if pat:
                ap = ap.rearrange(pat, **kw)
            return ap

    w_in_sb = reg1[:, 0:16384].rearrange("p (k n) -> p k n", n=2048)
    w_out_sb = reg1[:, 16384:24576].rearrange("p (k n) -> p k n", n=1024)
    wq_sb = reg1[:, 0:8192].rearrange("p (k n) -> p k n", n=1024)
    gbuf = [reg1[:, 8192 + i * 2048: 8192 + (i + 1) * 2048] for i in range(NGB)]

    c16 = Carver(a16, A16N)
    h_bf = c16.take(1024)
    hT = c16.take(1024, "p (k t) -> p k t", t=128)
    z_all = c16.take(2048, "p (b c) -> p b c", c=256)
    kTb_all = c16.take(1280)
    vb_all = c16.take(1280, "p (b c) -> p b c", c=128)
    kTc_all = c16.take(5120, "p (h s) -> p h s", parts=64, s=1280)
    vc_all = c16.take(2560, "p (b c) -> p b c", c=256)
    ua = c16.take(512)
    qTb = c16.take(256, "p (c t) -> p c t", t=128)
    qTc = c16.take(512, "p (c t) -> p c t", parts=64, t=128)
    w_bf = c16.take(1280)
    wT = c16.take(1280, "p (b q) -> p b q", q=128)
    o_cat = c16.take(1024)
    pooledT = c16.take(512, "p (g t) -> p g t", parts=64, t=128)
    wmb = [c16.take(1024, "p (k n) -> p k n", n=128) for _ in range(2)]
    c32 = Carver(a32, A32N)
    scr0 = c32.take(1280)
    scr1 = c32.take(1408)
    csB = c32.take(256, "p (a t) -> p a t", t=128)
    csC = c32.take(256, "p (a t) -> p a t", parts=64, t=128)
    bmb = [c32.take(128) for _ in range(2)]
    assert c32.o == 3456
    lnp = c32.take(2048, "p (a n) -> p a n", n=1024)
    p16 = Carver(a16, A16N)
    h2 = [p16.take(1024) for _ in range(2)]
    h2T = p16.take(1024, "p (k t) -> p k t", t=128)
    qT_sb = p16.take(1024, "p (c t) -> p c t", t=128)
    junk = p16.take(1024)
    w_pb = [p16.take(128) for _ in range(2)]
    BIGN = 64 * 65
    bigf = [p16.take(BIGN) for _ in range(2)]
    p_wmb = [p16.take(1024, "p (k n) -> p k n", n=128) for _ in range(2)]
    p32 = Carver(a32, A32N)
    e_scr0 = p32.take(1024)
    e_scr1 = p32.take(256)
    sc_sb = p32.take(512)
    scr_mr = p32.take(256)
    sv = p32.take(64, "p (g k) -> p g k", k=16)
    sif = p32.take(64, "p (g k) -> p g k", k=16)
    cand = p32.take(512, "p (h j) -> p h j", j=256)
    eid = p32.take(512, "p (h j) -> p h j", j=256)
    cvv = p32.take(128, "p (h k) -> p h k", k=16)
    gat = p32.take(128, "p (h k) -> p h k", k=16)
    assert p32.o == 3456
    p32.o = 5504
    idxf = p32.take(128)
    gT = [p32.take(128) for _ in range(3)]
    aT = [p32.take(256, "p (a t) -> p a t", t=128) for _ in range(2)]
    p_bmb = [p32.take(128) for _ in range(2)]

    n_general = 24
    REGS = {}
    S = Sched(n_general + 2 * NGB + 4)
    SEM_U = [n_general + i for i in range(NGB)]
    SEM_V = [n_general + NGB + i for i in range(NGB)]

    def V(fn, r=(), w=()):
        S.op("dve", fn, r, w)

    def ACT(fn, r=(), w=()):
        S.op("act", fn, r, w)

    def PE(fn, r=(), w=()):
        S.op("pe", fn, r, w)

    def POOL(fn, r=(), w=()):
        S.op("pool", fn, r, w)

    def DMA(eng, out, in_, r=(), w=(), sem=None):
        S.dma(eng, lambda e: e.dma_start(out=out, in_=in_), r, w, sem)

    def barrier():
        snap = S.snapshot()
        for e in S.ENG:
            S.op(e, lambda en: en.nop(), extra=snap)
        S.readers = {}
        S.last_w = {}

    DMA("sp", ident_f[:], d_ident, w=["ident_f"])
    DMA("sp", rb_f[:], d_rb, w=["rb_f"])
    DMA("sp", rc_f[:], d_rc, w=["rc_f"])
    DMA("pool", band[:], d_band.rearrange("v q k -> q v k"), w=["band"])
    DMA("sp", cvec[:], d_cvec, w=["cvec"])
    DMA("pool", ident_b[:], d_ident, w=["ident_b"])
    DMA("pool", poolat[:], d_poolat, w=["poolat"])
    for t in range(NT):
        DMA("sp", x[:, t, :], d_x[t], w=[f"x{t}"])
    ACT(lambda e: e.activation(out=silu_b[:], in_=cvec[:], func=AF.Silu), ["cvec"], ["silu_b"])
    for i in range(2):
        V(lambda e, i=i: e.memset(bigf[i], 0.0), [], [f"big{i}"])

    def ln_stats(src_ap, src_keys, tag):
        V(lambda e: e.bn_stats(out=st6[:, 0, :], in_=src_ap[:, 0:512]), src_keys, ["st6a"])
        V(lambda e: e.bn_stats(out=st6[:, 1, :], in_=src_ap[:, 512:1024]), src_keys, ["st6b"])
        V(lambda e: e.bn_aggr(out=sm[:, 0:2], in_=st6[:].rearrange("p a b -> p (a b)")), ["st6a", "st6b"], ["sm_mv"])
        V(lambda e: e.tensor_scalar(out=sm[:, 2:3], in0=sm[:, 1:2], scalar1=LN_EPS, scalar2=None, op0=ALU.add),
          ["sm_mv"], ["sm_rstd"])
        ACT(lambda e: e.activation(out=sm[:, 2:3], in_=sm[:, 2:3], func=AF.Sqrt), ["sm_rstd"], ["sm_rstd"])
        V(lambda e: e.reciprocal(out=sm[:, 2:3], in_=sm[:, 2:3]), ["sm_rstd"], ["sm_rstd"])
        V(lambda e: e.scalar_tensor_tensor(out=sm[:, 3:4], in0=sm[:, 0:1], scalar=-1.0, in1=sm[:, 2:3],
                                           op0=ALU.mult, op1=ALU.mult), ["sm_mv", "sm_rstd"], ["sm_nmr"])
        return sm[:, 2:3], sm[:, 3:4]

    def ln_mod(t, which, sh_off, sc_off, dst_bf, dst_key, tmp, tmp_key):
        rstd, nmr = ln_stats(x[:, t, :], [f"x{t}"], "a")
        ACT(lambda e: e.activation(out=tmp[:, 0:1024], in_=x[:, t, :], func=AF.Identity, bias=nmr, scale=rstd),
            [f"x{t}", "sm_rstd", "sm_nmr"], [tmp_key])
        V(lambda e: e.tensor_tensor(out=tmp[:, 0:1024], in0=tmp[:, 0:1024], in1=mod[:, which, sc_off:sc_off + 1024],
                                    op=ALU.mult), [tmp_key, "mod"], [tmp_key])
        V(lambda e: e.tensor_tensor(out=dst_bf, in0=tmp[:, 0:1024], in1=mod[:, which, sh_off:sh_off + 1024],
                                    op=ALU.add), [tmp_key, "mod"], [dst_key])

    def transpose8(src_bf, src_key, dst, dst_key):
        def tr(e):
            ins = None
            for k in range(8):
                ins = e.transpose(out=psT[:, k * 128:(k + 1) * 128], in_=src_bf[:, k * 128:(k + 1) * 128],
                                  identity=ident_b[:])
            return ins
        PE(tr, [src_key, "ident_b"], ["psT"])
        ACT(lambda e: e.copy(out=dst.rearrange("p k t -> p (k t)"), in_=psT[:, :]), ["psT"], [dst_key])

    def mod_half(l, half, wb, wb_key, bb, bb_key):
        for j in range(24):
            c0 = half * 3072 + j * 128
            b = j % 2
            DMA("pool", wb[b], d_wmod[l][:, c0:c0 + 128].rearrange("(k p) n -> p k n", p=128), w=[f"{wb_key}{b}"])
            DMA("sp", bb[b], bcast_row(h_bmod, l * 6144 + c0, 128), w=[f"{bb_key}{b}"])
            for which in range(2):
                def mm(e, which=which, b=b):
                    ins = None
                    for k in range(8):
                        ins = e.matmul(psA[:, which * 512: which * 512 + 128],
                                       lhsT=silu_b[:, which, k:k + 1].to_broadcast([128, 128]),
                                       rhs=wb[b][:, k, :], start=(k == 0), stop=(k == 7))
                    return ins
                PE(mm, ["silu_b", f"{wb_key}{b}"], [f"psA{which}"])
                addc = 1.0 if 8 <= j < 16 else 0.0
                V(lambda e, which=which, b=b, j=j, addc=addc: e.scalar_tensor_tensor(
                    out=mod[:, which, j * 128:(j + 1) * 128], in0=psA[:, which * 512: which * 512 + 128],
                    scalar=addc, in1=bb[b], op0=ALU.add, op1=ALU.add),
                  [f"psA{which}", f"{bb_key}{b}"], ["mod"])

    def dbg(name, ap, keys):
        if name in dbg_out:
            DMA("pool", dbg_out[name], ap, r=keys)

    def attn_pv(blocks, vfn, vkey, out_ps, out_key, first_group_start=True):
        n = len(blocks)
        for r0 in range(0, n, 8):
            grp = list(range(r0, min(n, r0 + 8)))

            def tr(e, grp=grp, r0=r0):
                ins = None
                for n_ in grp:
                    ins = e.transpose(out=psT[:, (n_ - r0) * 128:(n_ - r0 + 1) * 128],
                                      in_=w_bf[:, n_ * 128:(n_ + 1) * 128], identity=ident_b[:])
                return ins
            PE(tr, ["w_bf", "ident_b"], ["psT"])
            ACT(lambda e, grp=grp, r0=r0: e.copy(out=wT[:, r0:r0 + len(grp), :].rearrange("p b q -> p (b q)"),
                                                 in_=psT[:, 0:len(grp) * 128]), ["psT"], ["wT"])

        def pv(e):
            ins = None
            for n_, jb in enumerate(blocks):
                ins = e.matmul(out_ps, lhsT=wT[:, n_, :], rhs=vfn(jb), start=(n_ == 0), stop=(n_ == n - 1))
            return ins
        PE(pv, ["wT", vkey], [out_key])

    def post_ln(t, src0, src0_key, src1, src1_key, which, tmp, tmp_key):
        V(lambda e: e.tensor_tensor(out=tmp[:, 0:512], in0=src0, in1=mod[:, which, 2048:2560], op=ALU.mult),
          [src0_key, "mod"], [tmp_key])
        V(lambda e: e.tensor_tensor(out=tmp[:, 512:1024], in0=src1, in1=mod[:, which, 2560:3072], op=ALU.mult),
          [src1_key, "mod"], [tmp_key])
        V(lambda e: e.scalar_tensor_tensor(out=tmp[:, 0:1024], in0=x[:, t, :], scalar=ALPHA, in1=tmp[:, 0:1024],
                                           op0=ALU.mult, op1=ALU.add), [f"x{t}", tmp_key], [tmp_key])
        rstd, nmr = ln_stats(tmp, [tmp_key], "p")
        ACT(lambda e: e.activation(out=tmp[:, 0:1024], in_=tmp[:, 0:1024], func=AF.Identity, bias=nmr, scale=rstd),
            [tmp_key, "sm_rstd", "sm_nmr"], [tmp_key])
        V(lambda e: e.tensor_tensor(out=tmp[:, 0:1024], in0=tmp[:, 0:1024], in1=lnp[:, 0, :], op=ALU.mult),
          [tmp_key, "lnp"], [tmp_key])
        V(lambda e: e.tensor_tensor(out=x[:, t, :], in0=tmp[:, 0:1024], in1=lnp[:, 1, :], op=ALU.add),
          [tmp_key, "lnp"], [f"x{t}"])

    def ck(name):
        if CFG["stop"] == name:
            raise _Stop()

    try:
      ck("init")
      for l in range(CFG["depth"]):
          lam_init = 0.8 - 0.6 * math.exp(-0.3 * l)
          for k in range(8):
              DMA("pool", w_in_sb[:, k, :], d_win[l][k * 128:(k + 1) * 128, :], w=["w_in"])
          for k in range(0, 8, 2):
              DMA("pool", w_out_sb[:, k:k + 2, :],
                  d_wout[l][k * 128:(k + 2) * 128, :].rearrange("(k p) n -> p k n", p=128), w=["w_out"])
          DMA("pool", cwT[:], d_cwT[l], w=["cwT"])
          DMA("pool", poolw[:], d_poolw[l], w=["poolw"])
          DMA("pool", kb_b[:], d_kb[l], w=["kb_b"])
          DMA("sp", cbT[:], d_cbT[l], w=["cbT"])
          DMA("sp", sink[:], bcast_row(h_sink, l * 4, 4), w=["sink"])
          DMA("sp", lamt[:, 0, :], bcast_row(h_lamq, l * 64, 64), w=["lamt"])
          DMA("sp", lamt[:, 1, :], bcast_row(h_lamk, l * 64, 64), w=["lamt"])
          DMA("sp", subg[:], bcast_row(h_subg, l * 64, 64), w=["subg"])
          DMA("sp", pscale[:], bcast_row(h_pscale, l * 256, 256), w=["pscale"])
          DMA("sp", lnp[:, 0, :], bcast_row(h_lng, (l * 2 + 0) * D, D), w=["lnp"])
          DMA("sp", lnp[:, 1, :], bcast_row(h_lnb, (l * 2 + 0) * D, D), w=["lnp"])
          for j in range(2):
              V(lambda e, j=j: e.tensor_tensor(out=lamt[:, 0, j * 32:(j + 1) * 32], in0=lamt[:, 0, j * 32:(j + 1) * 32],
                                               in1=lamt[:, 1, j * 32:(j + 1) * 32], op=ALU.mult), ["lamt"], ["lamt"])
              V(lambda e, j=j: e.reduce_sum(out=lamv[:, j:j + 1], in_=lamt[:, 0, j * 32:(j + 1) * 32], axis=AX.X),
                ["lamt"], ["lamv"])
          ACT(lambda e: e.activation(out=lamv[:, 2:4], in_=lamv[:, 0:2], func=AF.Exp), ["lamv"], ["lamve"])
          V(lambda e, lam_init=lam_init: e.scalar_tensor_tensor(out=lamv[:, 5:6], in0=lamv[:, 3:4], scalar=-lam_init, in1=lamv[:, 2:3],
                                             op0=ALU.add, op1=ALU.subtract), ["lamve"], ["nlam"])
          V(lambda e, lam_init=lam_init: e.tensor_scalar(out=subg[:], in0=subg[:], scalar1=1.0 - lam_init, scalar2=None, op0=ALU.mult),
            ["subg"], ["subg"])
          mod_half(l, 0, wmb, "wmb", bmb, "bmb")
          if l == 0:
              dbg("mod0", mod[:, 0, :], ["mod"])
          ck("mod")

          seqs = [(0, 2, False), (2, 2, False), (4, 8, True)]
          for (t0, nb, lat) in seqs:
              which = 1 if lat else 0
              nkb = nb + (2 if lat else 0)
              if lat:
                  DMA("pool", kTb_all[:, 1024:1280], d_cwkT[l], w=["kTb_all"])
                  DMA("pool", vb_all[:, 8:10, :], d_cwv[l].rearrange("(b p) c -> p b c", p=128), w=["vb_all"])
                  DMA("pool", kTc_all[:, :, 1024:1280], d_cdkT[l], w=["kTc_all"])
                  DMA("pool", vc_all[:, 8:10, :], d_cdv[l].rearrange("(b p) c -> p b c", p=128), w=["vc_all"])

              def rope(src_ps, src_key, n_ch, parts, tmp32, tmp_key, cs, cskey, rmat, rkey, dst, dst_key, lat=lat):
                  if not lat:
                      ACT(lambda e: e.copy(out=dst, in_=src_ps), [src_key], [dst_key])
                      return
                  V(lambda e: e.tensor_copy(out=tmp32, in_=src_ps), [src_key], [tmp_key])

                  def mm(e):
                      ins = None
                      for c in range(n_ch):
                          ins = e.matmul(src_ps[:, c, :], lhsT=rmat, rhs=tmp32[:, c, :], start=True, stop=True)
                      return ins
                  PE(mm, [tmp_key, rkey], [src_key])
                  V(lambda e: e.tensor_tensor(out=tmp32, in0=tmp32, in1=cs[:, 0:1, :].to_broadcast([parts, n_ch, 128]),
                                              op=ALU.mult), [tmp_key, cskey], [tmp_key])
                  V(lambda e: e.tensor_tensor(out=src_ps, in0=src_ps, in1=cs[:, 1:2, :].to_broadcast([parts, n_ch, 128]),
                                              op=ALU.mult), [src_key, cskey], [src_key])
                  V(lambda e: e.tensor_tensor(out=dst, in0=tmp32, in1=src_ps, op=ALU.add), [tmp_key, src_key], [dst_key])

              def load_cs(i):
                  tk = i * 128
                  DMA("sp", csB[:, 0, :], d_cosb[:, tk:tk + 128], w=["csB"])
                  DMA("sp", csB[:, 1, :], d_sinb[:, tk:tk + 128], w=["csB"])
                  DMA("sp", csC[:, 0, :], d_cosc[:, tk:tk + 128], w=["csC"])
                  DMA("sp", csC[:, 1, :], d_sinc[:, tk:tk + 128], w=["csC"])

              for i in range(nb):
                  t = t0 + i
                  ln_mod(t, which, 0, 1024, h_bf, "h_bf", scr0, "scr0")
                  ck("ln")
                  transpose8(h_bf, "h_bf", hT, "hT")
                  ck("tr")
                  if lat:
                      load_cs(i)

                  def tm(e):
                      ins = None
                      for (c_lo, n, po) in ((C_BK, 256, 0), (C_CK, 256, 512), (C_CV, 512, 1024)):
                          for k in range(8):
                              ins = e.matmul(psA[:, po:po + n], lhsT=hT[:, k, :], rhs=w_in_sb[:, k, c_lo:c_lo + n],
                                             start=(k == 0), stop=(k == 7))
                      return ins
                  PE(tm, ["hT", "w_in"], ["psA0", "psA1", "psA2"])

                  def fm(e):
                      ins = None
                      for k in range(8):
                          ins = e.matmul(psC[:, 0:128], lhsT=w_in_sb[:, k, C_BK:C_BK + 128], rhs=hT[:, k, :],
                                         start=(k == 0), stop=(k == 7))
                      for c in range(4):
                          for k in range(8):
                              ins = e.matmul(psB[0:64, c * 128:(c + 1) * 128],
                                             lhsT=w_in_sb[:, k, C_CK + c * 64:C_CK + (c + 1) * 64], rhs=hT[:, k, :],
                                             start=(k == 0), stop=(k == 7))
                      return ins
                  PE(fm, ["hT", "w_in"], ["psC", "psB0"])
                  ck("mm")
                  ACT(lambda e, i=i: e.copy(out=vb_all[:, i, :], in_=psA[:, 128:256]), ["psA0"], ["vb_all"])
                  ACT(lambda e, i=i: e.copy(out=vc_all[:, i, :], in_=psA[:, 1024:1280]), ["psA2"], ["vc_all"])
                  ACT(lambda e, i=i: e.copy(out=z_all[:, i, :], in_=psA[:, 1280:1536]), ["psA2"], ["z_all"])
                  if not lat:
                      sq = t0 // 2
                      V(lambda e: e.tensor_copy(out=scr1[:, 0:256], in_=psA[:, 0:256]), ["psA0"], ["scr1o"])
                      V(lambda e: e.tensor_copy(out=scr1[:, 256:512], in_=psA[:, 512:768]), ["psA1"], ["scr1o"])
                      V(lambda e: e.tensor_copy(out=scr1[:, 512:768], in_=psA[:, 1024:1280]), ["psA2"], ["scr1o"])
                      r0 = i * 128
                      DMA("sp", o_wk[sq, l, r0:r0 + 128, :], scr1[:, 0:128], r=["scr1o"])
                      DMA("sp", o_wv[sq, l, r0:r0 + 128, :], scr1[:, 128:256], r=["scr1o"])
                      DMA("sp", o_dk[sq, l, r0:r0 + 128, :], scr1[:, 256:512], r=["scr1o"])
                      DMA("sp", o_dv[sq, l, r0:r0 + 128, :], scr1[:, 512:768], r=["scr1o"])
                  ck("ev")
                  tk = i * 128
                  fmB32 = scr1[:, 768:896].rearrange("p (c t) -> p c t", t=128)
                  fmC32 = scr1[0:64, 896:1408].rearrange("p (c t) -> p c t", t=128)
                  rope(psC[:, 0:128].rearrange("p (c t) -> p c t", t=128), "psC", 1, 128, fmB32, "scr1o", csB, "csB",
                       rb_f[:], "rb_f", kTb_all[:, tk:tk + 128].rearrange("p (c t) -> p c t", t=128), "kTb_all")
                  rope(psB[0:64, 0:512].rearrange("p (c t) -> p c t", t=128), "psB0", 4, 64, fmC32, "scr1o", csC, "csC",
                       rc_f[:], "rc_f", kTc_all[:, :, tk:tk + 128], "kTc_all")
              if l == 0 and t0 == 0:
                  dbg("kTb", kTb_all[:, 0:256], ["kTb_all"])
                  dbg("z", z_all[:, 0:2, :], ["z_all"])
              ck("p1")

              for i in range(nb):
                  t = t0 + i
                  ln_mod(t, which, 0, 1024, h_bf, "h_bf", scr0, "scr0")
                  transpose8(h_bf, "h_bf", hT, "hT")
                  if lat:
                      load_cs(i)

                  def tmA(e):
                      ins = None
                      for k in range(8):
                          ins = e.matmul(psA[:, 0:512], lhsT=hT[:, k, :], rhs=w_in_sb[:, k, 0:512],
                                         start=(k == 0), stop=(k == 7))
                      return ins
                  PE(tmA, ["hT", "w_in"], ["psA0"])

                  def fmq(e):
                      ins = None
                      for g in range(2):
                          for kv in range(2):
                              c0 = C_BQ + (kv * 2 + g) * 64
                              for k in range(8):
                                  ins = e.matmul(psC[kv * 64:(kv + 1) * 64, g * 128:(g + 1) * 128],
                                                 lhsT=w_in_sb[:, k, c0:c0 + 64], rhs=hT[:, k, :],
                                                 start=(k == 0), stop=(k == 7))
                      for c in range(4):
                          for k in range(8):
                              ins = e.matmul(psB[0:64, c * 128:(c + 1) * 128],
                                             lhsT=w_in_sb[:, k, C_CQ + c * 64:C_CQ + (c + 1) * 64], rhs=hT[:, k, :],
                                             start=(k == 0), stop=(k == 7))
                      return ins
                  PE(fmq, ["hT", "w_in"], ["psC", "psB0"])
                  ACT(lambda e: e.activation(out=ua, in_=psA[:, 0:512], func=AF.Gelu_apprx_tanh), ["psA0"], ["ua"])
                  fmB32 = scr1[:, 768:1024].rearrange("p (c t) -> p c t", t=128)
                  fmC32 = scr1[0:64, 0:512].rearrange("p (c t) -> p c t", t=128)
                  rope(psC[:, 0:256].rearrange("p (c t) -> p c t", t=128), "psC", 2, 128, fmB32, "scr1o", csB, "csB",
                       rb_f[:], "rb_f", qTb, "qTb")
                  rope(psB[0:64, 0:512].rearrange("p (c t) -> p c t", t=128), "psB0", 4, 64, fmC32, "scr1o", csC, "csC",
                       rc_f[:], "rc_f", qTc, "qTc")

                  def mmA(e):
                      ins = None
                      for hh in range(4):
                          ins = e.matmul(psA[:, 1536 + hh * 64:1536 + (hh + 1) * 64], lhsT=cwT[:, hh, :],
                                         rhs=ua[:, 256 + hh * 64:256 + (hh + 1) * 64], start=True, stop=True)
                      return ins
                  PE(mmA, ["ua", "cwT"], ["psA3"])
                  for hh in range(4):
                      V(lambda e, hh=hh: e.scalar_tensor_tensor(
                          out=o_cat[:, hh * 64:(hh + 1) * 64], in0=psA[:, 1536 + hh * 64:1536 + (hh + 1) * 64],
                          scalar=cbT[:, hh:hh + 1], in1=ua[:, hh * 64:(hh + 1) * 64], op0=ALU.add, op1=ALU.mult),
                        ["psA3", "cbT", "ua"], ["o_cat"])

                  rels = []
                  if i > 0:
                      rels.append((i - 1, 3))
                  rels.append((i, 0 if i == 0 else (2 if i == nb - 1 else 1)))
                  if i < nb - 1:
                      rels.append((i + 1, 4))

                  def mmD(e, rels=rels):
                      ins = None
                      for g in range(4):
                          for n_, (j, v) in enumerate(rels):
                              ins = e.matmul(psB[0:64, 512 + g * 128:512 + (g + 1) * 128],
                                             lhsT=z_all[:, j, g * 64:(g + 1) * 64], rhs=poolat[:, v, g, :],
                                             start=(n_ == 0), stop=(n_ == len(rels) - 1))
                      return ins
                  PE(mmD, ["z_all", "poolat"], ["psB1"])
                  ACT(lambda e: e.copy(out=pooledT.rearrange("p g t -> p (g t)"), in_=psB[0:64, 512:1024]),
                      ["psB1"], ["pooledT"])

                  def mmD2(e):
                      ins = None
                      for g in range(4):
                          ins = e.matmul(psA[:, 1792 + g * 64:1792 + (g + 1) * 64], lhsT=pooledT[:, g, :],
                                         rhs=poolw[:, g, :], start=True, stop=True)
                      return ins
                  PE(mmD2, ["pooledT", "poolw"], ["psA3"])
                  V(lambda e: e.tensor_tensor(out=o_cat[:, 768:1024], in0=psA[:, 1792:2048], in1=pscale[:], op=ALU.mult),
                    ["psA3", "pscale"], ["o_cat"])

                  if lat:
                      kblocks = [min(max(j, 0), nb - 1) for j in (i - 1, i, i + 1)] + [8, 9]
                      bvar = 0 if i == 0 else (2 if i == nb - 1 else 1)
                  else:
                      kblocks = [0, 1]
                      bvar = 0
                  NKB = len(kblocks) * 128
                  for hh in range(4):
                      kv, g = hh // 2, hh % 2

                      def mmS(e, kv=kv, g=g, kblocks=kblocks):
                          ins = None
                          for n_, jb in enumerate(kblocks):
                              ins = e.matmul(psA[:, n_ * 128:(n_ + 1) * 128], lhsT=qTb[kv * 64:(kv + 1) * 64, g, :],
                                             rhs=kTb_all[kv * 64:(kv + 1) * 64, jb * 128:(jb + 1) * 128],
                                             start=True, stop=True)
                          return ins
                      PE(mmS, ["qTb", "kTb_all"], ["psA0", "psA1"])
                      if lat:
                          V(lambda e, bvar=bvar: e.scalar_tensor_tensor(out=scr0[:, 0:384], in0=psA[:, 0:384], scalar=0.125,
                                                                        in1=band[:, bvar, :], op0=ALU.mult, op1=ALU.add),
                            ["psA0", "band"], ["scr0"])
                          V(lambda e: e.tensor_scalar(out=scr0[:, 384:640], in0=psA[:, 384:640], scalar1=0.125,
                                                      scalar2=None, op0=ALU.mult), ["psA0", "psA1"], ["scr0"])
                      else:
                          V(lambda e: e.tensor_scalar(out=scr0[:, 0:256], in0=psA[:, 0:256], scalar1=0.125, scalar2=None,
                                                      op0=ALU.mult), ["psA0"], ["scr0"])
                      V(lambda e, NKB=NKB: e.reduce_max(out=sm[:, 8:9], in_=scr0[:, 0:NKB], axis=AX.X), ["scr0"], ["sm8"])
                      V(lambda e, hh=hh: e.tensor_scalar(out=sm[:, 9:10], in0=sm[:, 8:9], scalar1=sink[:, hh:hh + 1],
                                                         scalar2=-1.0, op0=ALU.max, op1=ALU.mult), ["sm8", "sink"], ["sm9"])
                      ACT(lambda e, NKB=NKB: e.activation(out=scr0[:, 0:NKB], in_=scr0[:, 0:NKB], func=AF.Exp,
                                                          bias=sm[:, 9:10], accum_out=sm[:, 10:11]),
                          ["scr0", "sm9"], ["scr0", "sm10"])
                      ACT(lambda e, hh=hh: e.activation(out=sm[:, 11:12], in_=sink[:, hh:hh + 1], func=AF.Exp,
                                                        bias=sm[:, 9:10]), ["sink", "sm9"], ["sm11"])
                      V(lambda e: e.tensor_tensor(out=sm[:, 12:13], in0=sm[:, 10:11], in1=sm[:, 11:12], op=ALU.add),
                        ["sm10", "sm11"], ["sm12"])
                      V(lambda e: e.reciprocal(out=sm[:, 13:14], in_=sm[:, 12:13]), ["sm12"], ["sm13"])
                      V(lambda e, NKB=NKB: e.tensor_scalar(out=w_bf[:, 0:NKB], in0=scr0[:, 0:NKB], scalar1=sm[:, 13:14],
                                                           scalar2=None, op0=ALU.mult), ["scr0", "sm13"], ["w_bf"])
                      attn_pv(kblocks, lambda jb, kv=kv: vb_all[:, jb, kv * 64:(kv + 1) * 64], "vb_all",
                              psC[:, 256 + hh * 64:256 + (hh + 1) * 64], "psC")
                  V(lambda e: e.tensor_copy(out=o_cat[:, 256:512], in_=psC[:, 256:512]), ["psC"], ["o_cat"])

                  if lat:
                      cblocks = list(range(10))
                  else:
                      cblocks = [0, 1]
                  NKC = len(cblocks) * 128
                  csc = 32.0 ** -0.5
                  pieces = [(p0, min(512, NKC - p0)) for p0 in range(0, NKC, 512)]
                  for hh in range(4):
                      for j in range(2):
                          ej = scr0 if j == 0 else scr1

                          def mmC(e, hh=hh, j=j, pieces=pieces):
                              ins = None
                              for (p0, pn) in pieces:
                                  ins = e.matmul(psA[:, p0:p0 + pn], lhsT=qTc[j * 32:(j + 1) * 32, hh, :],
                                                 rhs=kTc_all[j * 32:(j + 1) * 32, hh, p0:p0 + pn], start=True, stop=True)
                              return ins
                          PE(mmC, ["qTc", "kTc_all"], ["psA0", "psA1", "psA2"])
                          V(lambda e, j=j, NKC=NKC: e.reduce_max(out=sm[:, 14 + j:15 + j], in_=psA[:, 0:NKC], axis=AX.X),
                            ["psA0", "psA1", "psA2"], [f"smx{j}"])
                          V(lambda e, j=j, csc=csc: e.tensor_scalar(out=sm[:, 16 + j:17 + j], in0=sm[:, 14 + j:15 + j], scalar1=-csc,
                                                           scalar2=None, op0=ALU.mult), [f"smx{j}"], [f"smn{j}"])
                          ACT(lambda e, j=j, ej=ej, NKC=NKC, csc=csc: e.activation(out=ej[:, 0:NKC], in_=psA[:, 0:NKC], func=AF.Exp,
                                                                 bias=sm[:, 16 + j:17 + j], scale=csc,
                                                                 accum_out=sm[:, 18 + j:19 + j]),
                              ["psA0", "psA1", "psA2", f"smn{j}"], ["scr0" if j == 0 else "scr1o", f"sms{j}"])
                      V(lambda e: e.reciprocal(out=sm[:, 20:22], in_=sm[:, 18:20]), ["sms0", "sms1"], ["smr"])
                      V(lambda e: e.tensor_tensor(out=sm[:, 22:23], in0=sm[:, 21:22], in1=lamv[:, 5:6], op=ALU.mult),
                        ["smr", "nlam"], ["smc1"])
                      V(lambda e, NKC=NKC: e.tensor_scalar(out=scr0[:, 0:NKC], in0=scr0[:, 0:NKC], scalar1=sm[:, 20:21], scalar2=None,
                                                  op0=ALU.mult), ["scr0", "smr"], ["scr0"])
                      V(lambda e, NKC=NKC: e.scalar_tensor_tensor(out=w_bf[:, 0:NKC], in0=scr1[:, 0:NKC], scalar=sm[:, 22:23],
                                                         in1=scr0[:, 0:NKC], op0=ALU.mult, op1=ALU.add),
                        ["scr0", "scr1o", "smc1"], ["w_bf"])
                      attn_pv(cblocks, lambda jb, hh=hh: vc_all[:, jb, hh * 64:(hh + 1) * 64], "vc_all",
                              psC[:, hh * 64:(hh + 1) * 64], "psC")
                  oc = scr0[:, 0:256]
                  V(lambda e: e.tensor_copy(out=oc, in_=psC[:, 0:256]), ["psC"], ["scr0"])
                  V(lambda e: e.tensor_tensor(out=scr0[:, 256:512], in0=oc, in1=oc, op=ALU.mult), ["scr0"], ["scr0"])
                  V(lambda e: e.reduce_sum(out=sm[:, 24:28], in_=scr0[:, 256:512].rearrange("p (h d) -> p h d", d=64),
                                           axis=AX.X), ["scr0"], ["smq"])
                  V(lambda e: e.tensor_scalar(out=sm[:, 24:28], in0=sm[:, 24:28], scalar1=1.0 / 64.0, scalar2=LN_EPS,
                                              op0=ALU.mult, op1=ALU.add), ["smq"], ["smq"])
                  ACT(lambda e: e.activation(out=sm[:, 24:28], in_=sm[:, 24:28], func=AF.Sqrt), ["smq"], ["smq"])
                  V(lambda e: e.reciprocal(out=sm[:, 24:28], in_=sm[:, 24:28]), ["smq"], ["smq"])
                  V(lambda e: e.tensor_tensor(out=scr0[:, 0:256].rearrange("p (h d) -> p h d", d=64),
                                              in0=scr0[:, 0:256].rearrange("p (h d) -> p h d", d=64),
                                              in1=sm[:, 24:28].unsqueeze(2).to_broadcast([128, 4, 64]), op=ALU.mult),
                    ["scr0", "smq"], ["scr0"])
                  V(lambda e: e.tensor_tensor(out=o_cat[:, 512:768].rearrange("p (h d) -> p h d", d=64),
                                              in0=scr0[:, 0:256].rearrange("p (h d) -> p h d", d=64),
                                              in1=subg[:].unsqueeze(1).to_broadcast([128, 4, 64]), op=ALU.mult),
                    ["scr0", "subg"], ["o_cat"])
                  if l == 0 and t == 0:
                      dbg("ocat", o_cat, ["o_cat"])

                  transpose8(o_cat, "o_cat", hT, "hT")

                  def mmO(e):
                      ins = None
                      for n_ in range(2):
                          for k in range(8):
                              ins = e.matmul(psA[:, n_ * 512:(n_ + 1) * 512], lhsT=hT[:, k, :],
                                             rhs=w_out_sb[:, k, n_ * 512:(n_ + 1) * 512], start=(k == 0), stop=(k == 7))
                      return ins
                  PE(mmO, ["hT", "w_out"], ["psA0", "psA1"])
                  post_ln(t, psA[:, 0:512], "psA0", psA[:, 512:1024], "psA1", which, scr0, "scr0")
          if l == 0:
              dbg("x1", x[:, 0, :], ["x0"])

          ck("mix")
          barrier()
          for k in range(0, 8, 2):
              DMA("pool", wq_sb[:, k:k + 2, :], d_wq[l][k * 128:(k + 2) * 128, :].rearrange("(k p) n -> p k n", p=128),
                  w=["wq"])
          DMA("sp", lnp[:, 0, :], bcast_row(h_lng, (l * 2 + 1) * D, D), w=["lnp"])
          DMA("sp", lnp[:, 1, :], bcast_row(h_lnb, (l * 2 + 1) * D, D), w=["lnp"])
          for i in range(2):
              V(lambda e, i=i: e.memset(bigf[i], 0.0), [], [f"bg{i}_{q_}" for q_ in range(64)])
          mod_half(l, 1, p_wmb, "wmb", p_bmb, "bmb")

          def stage_A(i):
              t = i
              which = 0 if t < 4 else 1
              hb = h2[i % 2]
              hkey = f"h2_{i % 2}"
              slot = i % 2
              ln_mod(t, which, 0, 1024, hb, hkey, e_scr0, "scr0")
              yield
              transpose8(hb, hkey, h2T, "h2T")
              yield
              for half in range(2):
                  def mq(e, half=half):
                      ins = None
                      for c in range(4):
                          cc = half * 4 + c
                          for k in range(8):
                              ins = e.matmul(psA[:, c * 128:(c + 1) * 128], lhsT=wq_sb[:, k, cc * 128:(cc + 1) * 128],
                                             rhs=h2T[:, k, :], start=(k == 0), stop=(k == 7))
                      return ins
                  PE(mq, ["h2T", "wq"], ["psA0"])
                  ACT(lambda e, half=half: e.copy(out=qT_sb[:, half * 4:half * 4 + 4, :].rearrange("p c t -> p (c t)"),
                                                  in_=psA[:, 0:512]), ["psA0"], ["qT_sb"])
                  yield
              V(lambda e: e.memset(idxf, 0.0), [], ["idxf"])
              for hq in range(4):
                  def ms(e, hq=hq):
                      ins = None
                      for hh in range(2):
                          ins = e.matmul(psA[:, 512 + hh * 256:512 + (hh + 1) * 256], lhsT=qT_sb[:, 2 * hq + hh, :],
                                         rhs=kb_b[:], start=True, stop=True)
                      return ins
                  PE(ms, ["qT_sb", "kb_b"], ["psA1"])
                  ACT(lambda e: e.copy(out=sc_sb, in_=psA[:, 512:1024]), ["psA1"], ["sc_sb"])
                  yield
                  for g in range(4):
                      scg = sc_sb[:, g * 128:(g + 1) * 128]
                      V(lambda e, g=g, scg=scg: e.max(out=sv[:, g, 0:8], in_=scg), ["sc_sb"], ["sv"])
                      V(lambda e, g=g, scg=scg: e.max_index(out=si_u[:, g, 0:8], in_max=sv[:, g, 0:8], in_values=scg),
                        ["sc_sb", "sv"], ["si_u"])
                      V(lambda e, g=g, scg=scg: e.match_replace(out=scr_mr[:, 0:128], in_to_replace=sv[:, g, 0:8],
                                                                in_values=scg, imm_value=-1e30), ["sc_sb", "sv"], ["scr_mr"])
                      V(lambda e, g=g: e.max(out=sv[:, g, 8:16], in_=scr_mr[:, 0:128]), ["scr_mr"], ["sv"])
                      V(lambda e, g=g: e.max_index(out=si_u[:, g, 8:16], in_max=sv[:, g, 8:16], in_values=scr_mr[:, 0:128]),
                        ["scr_mr", "sv"], ["si_u"])
                      yield
                  V(lambda e: e.tensor_copy(out=sif, in_=si_u[:]), ["si_u"], ["sif"])
                  for hh in range(2):
                      c3 = cand[:, hh, :].rearrange("p (a b) -> p a b", b=16)
                      e3 = eid[:, hh, :].rearrange("p (a b) -> p a b", b=16)
                      V(lambda e, hh=hh, c3=c3: e.tensor_tensor(
                          out=c3, in0=sv[:, 2 * hh, :].unsqueeze(2).to_broadcast([128, 16, 16]),
                          in1=sv[:, 2 * hh + 1, :].unsqueeze(1).to_broadcast([128, 16, 16]), op=ALU.add),
                        ["sv"], ["cand"])
                      V(lambda e, hh=hh, e3=e3: e.scalar_tensor_tensor(
                          out=e3, in0=sif[:, 2 * hh, :].unsqueeze(2).to_broadcast([128, 16, 16]), scalar=128.0,
                          in1=sif[:, 2 * hh + 1, :].unsqueeze(1).to_broadcast([128, 16, 16]),
                          op0=ALU.mult, op1=ALU.add), ["sif"], ["eid"])
                      yield
                  for hh in range(2):
                      H = 2 * hq + hh
                      V(lambda e, hh=hh, H=H: e.max(out=cvv[:, H, 0:8], in_=cand[:, hh, :]), ["cand"], ["cvv"])
                      V(lambda e, hh=hh, H=H: e.match_replace(out=scr_mr[:, 0:256], in_to_replace=cvv[:, H, 0:8],
                                                              in_values=cand[:, hh, :], imm_value=-1e30),
                        ["cand", "cvv"], ["scr_mr"])
                      V(lambda e, H=H: e.max(out=cvv[:, H, 8:16], in_=scr_mr[:, 0:256]), ["scr_mr"], ["cvv"])
                      for k in range(16):
                          V(lambda e, hh=hh, H=H, k=k: e.scalar_tensor_tensor(
                              out=e_scr1[:, 0:256], in0=cand[:, hh, :], scalar=cvv[:, H, k:k + 1], in1=eid[:, hh, :],
                              op0=ALU.is_equal, op1=ALU.mult, accum_out=idxf[:, H * 16 + k:H * 16 + k + 1]),
                            ["cand", "eid", "cvv"], ["escr1", "idxf"])
                          if k % 4 == 3:
                              yield
              yield
              V(lambda e: e.tensor_tensor(out=gat, in0=cvv, in1=cvv[:, :, 0:1].to_broadcast([128, 8, 16]), op=ALU.subtract),
                ["cvv"], ["gat"])
              ACT(lambda e: e.activation(out=gat, in_=gat, func=AF.Exp), ["gat"], ["gat"])
              V(lambda e: e.reduce_sum(out=sm[:, 8:16], in_=gat, axis=AX.X), ["gat"], ["smg"])
              V(lambda e: e.reciprocal(out=sm[:, 8:16], in_=sm[:, 8:16]), ["smg"], ["smg"])
              V(lambda e: e.tensor_tensor(out=gat, in0=gat, in1=sm[:, 8:16].unsqueeze(2).to_broadcast([128, 8, 16]),
                                          op=ALU.mult), ["gat", "smg"], ["gat"])
              V(lambda e: e.tensor_scalar(out=idxf, in0=idxf, scalar1=float(NEXP - 1), scalar2=0.0, op0=ALU.min, op1=ALU.max),
                ["idxf"], ["idxf"])
              V(lambda e, l=l: e.tensor_scalar(out=idxf, in0=idxf, scalar1=float(l * NEXP + CFG.get("oob_add", 0)), scalar2=None, op0=ALU.add),
                ["idxf"], ["idxf"])

              def trI(e):
                  e.transpose(out=psA[:, 1536:1664], in_=idxf, identity=ident_f[:])
                  return e.transpose(out=psA[:, 1664:1792], in_=gat.rearrange("p h k -> p (h k)"), identity=ident_f[:])
              PE(trI, ["idxf", "gat", "ident_f"], ["psA3"])
              V(lambda e: e.tensor_copy(out=idx_i[:, slot, :], in_=psA[:, 1536:1664]), ["psA3"], [f"idx{slot}"])
              V(lambda e: e.tensor_copy(out=gT[slot], in_=psA[:, 1664:1792]), ["psA3"], [f"gT{slot}"])

          gcount = [0]
          ac = aT[0].rearrange("p a t -> p (a t)")[:, 0:32].rearrange("p (j c) -> p j c", c=4)
          bigdiag = [bigf[k_].rearrange("p (a b) -> p a b", b=65)[:, :, 0] for k_ in range(2)]

          tokst = {}

          def stage_S1(i, tt, tab=d_puv):
              slot = i % 2
              hb, hkey = h2[i % 2], f"h2_{i % 2}"
              n_ = gcount[0]
              gcount[0] += 1
              b = n_ % NGB
              j = n_ % 8
              tokst[(i, tt)] = (b, j)
              S.dma("pool", lambda e: e.indirect_dma_start(
                  out=gbuf[b], out_offset=None, in_=tab,
                  in_offset=bass.IndirectOffsetOnAxis(ap=idx_i[:, slot, tt:tt + 1], axis=0),
                  bounds_check=REGS["bc"], oob_is_err=False), [f"idx{slot}"], [f"gb{b}"], SEM_U[b])
              for hf in range(2):
                  PE(lambda e, hf=hf: e.matmul(psB[:, hf * 512:(hf + 1) * 512],
                                               lhsT=ident_b[:, tt:tt + 1].to_broadcast([128, 128]),
                                               rhs=hb[:, hf * 512:(hf + 1) * 512], start=True, stop=True),
                     [hkey, "ident_b"], [f"psB{hf}"])
                  V(lambda e, hf=hf: e.scalar_tensor_tensor(
                      out=junk[:, hf * 512:(hf + 1) * 512], in0=gbuf[b][:, hf * 512:(hf + 1) * 512], scalar=1.0,
                      in1=psB[:, hf * 512:(hf + 1) * 512], op0=ALU.mult, op1=ALU.mult,
                      accum_out=ac[:, j, hf:hf + 1]), [f"gb{b}", f"psB{hf}"], [f"ac{j}_{hf}"])
              ACT(lambda e: e.activation(out=ac[:, j, 3:4], in_=ac[:, j, 0:1], func=AF.Gelu_apprx_tanh,
                                         bias=ac[:, j, 1:2]), [f"ac{j}_0", f"ac{j}_1"], [f"ac{j}"])
              sbk, t6 = tt // 64, tt % 64
              ACT(lambda e: e.activation(out=bigdiag[sbk][:, t6:t6 + 1], in_=ac[:, j, 3:4], func=AF.Copy,
                                         scale=gT[slot][:, tt:tt + 1]), [f"ac{j}", f"gT{slot}"], [f"bg{sbk}_{t6}"])

          def stage_S23(i, tt):
              slot = i % 2
              b, j = tokst.pop((i, tt))
              sbk, t6 = tt // 64, tt % 64
              bkey = f"bg{sbk}_{t6}"
              lhs = bigf[sbk][:, 0:4096].rearrange("p (t m) -> p t m", m=64)[:, t6, :]

              def mv(e):
                  e.matmul(psC[sbk * 64:(sbk + 1) * 64, :], lhsT=lhs, rhs=gbuf[b][:, 1024:1536],
                           start=(t6 == 0), stop=(t6 == 63))
                  return e.matmul(psA[sbk * 64:(sbk + 1) * 64, 1024:1536], lhsT=lhs, rhs=gbuf[b][:, 1536:2048],
                                  start=(t6 == 0), stop=(t6 == 63))
              PE(mv, [bkey, f"gb{b}"], ["psC", "psA2"])

          def stage_E(i):
              which = 0 if i < 4 else 1
              post_ln(i, psC[:, 0:512], "psC", psA[:, 1024:1536], "psA2", which, e_scr0, "scr0")

          ck("mod2")
          for _ in stage_A(0):
              pass
          ck("A")
          toks = [(i, tt) for i in range(NT) for tt in range(128)]
          stage_S1(*toks[0])
          genA = iter(())
          for n_t, (i, tt) in enumerate(toks):
              if tt == 0:
                  genA = stage_A(i + 1) if i + 1 < NT else iter(())
              if tt == 119:
                  for _ in genA:
                      pass
              if n_t + 1 < len(toks):
                  stage_S1(*toks[n_t + 1])
              stage_S23(i, tt)
              next(genA, None)
              if tt == 127:
                  stage_E(i)
          if l == 0:
              dbg("x2", x[:, 0, :], ["x0"])
          barrier()

    except _Stop:
        pass

    for t in range(NT):
        DMA("sp", o_y[t], x[:, t, :], r=[f"x{t}"])

    import contextlib
    sems = {}
    es = contextlib.ExitStack()
    for e_ in S.ENG:
        sems[("e", e_)] = es.enter_context(nc.semaphore(f"s_{e_}"))
    for i_ in range(len(S.dma_n)):
        sems[("d", i_)] = es.enter_context(nc.semaphore(f"d_{i_}"))
    with nc.Block() as block:

        def sem_of(key):
            return sems[key]

        def replay(name, eng):
            for (waits, fn, me, inc) in S.q[name]:
                for (s, v) in waits:
                    eng.wait_ge(sem_of(s), v)
                ins = fn(eng)
                ins.then_inc(sem_of(me[0]), inc)
            for (s, v) in S.final_waits(name):
                eng.wait_ge(sem_of(s), v)

        @block.tensor
        def _(e):
            replay("pe", e)

        @block.scalar
        def _(e):
            replay("act", e)

        @block.vector
        def _(e):
            replay("dve", e)

        @block.gpsimd
        def _(e):
            REGS["bc"] = e.alloc_register("bc")
            e.reg_mov(REGS["bc"], DEPTH * NEXP - 1)
            replay("pool", e)

        @block.sync
        def _(e):
            replay("sp", e)
    es.close()
    return nc


def _prep_inputs(inp):
    f = lambda a: np.ascontiguousarray(np.asarray(a, dtype=np.float32))
    x_prompt, x_sample, c = f(inp["x_prompt"]), f(inp["x_sample"]), f(inp["c"])
    c_ctx = f(inp["c_ctx"])
    cosb, sinb = _rope_tables(64)
    cosc, sinc = _rope_tables(32)
    shared = {
        "w_mod": f(inp["w_mod"]), "b_mod": f(inp["b_mod"]), "w_in": f(inp["w_in"]), "w_out": f(inp["w_out"]),
        "chunk_wT": f(np.transpose(f(inp["chunk_w"]), (0, 3, 1, 2))),
        "chunk_bT": f(np.transpose(f(inp["chunk_b"]), (0, 2, 1))),
        "win_sink": f(inp["win_sink"]),
        "lam_q": f(f(inp["diff_lam_q"]).reshape(DEPTH, 64)), "lam_k": f(f(inp["diff_lam_k"]).reshape(DEPTH, 64)),
        "subln_g": f(inp["diff_subln_g"]),
        "pool_wr": f(np.transpose(f(inp["pool_w"]), (0, 2, 1, 3))),
        "pool_scale": f(inp["pool_scale"]), "ln_g": f(inp["ln_g"]), "ln_b": f(inp["ln_b"]),
        "peer_wq": f(inp["peer_wq"]),
        "peer_uv": np.concatenate([f(inp["peer_u"]).reshape(DEPTH * NEXP, D), f(inp["peer_v"]).reshape(DEPTH * NEXP, D)], axis=1),
        "c_ident": np.eye(128, dtype=np.float32), "c_rb": _rot_matrix(64, 2), "c_rc": _rot_matrix(32, 2),
        "c_cosb": f(np.tile(cosb.T, (2, 1))), "c_sinb": f(np.tile(sinb.T, (2, 1))),
        "c_cosc": f(np.tile(cosc.T, (2, 1))), "c_sinc": f(np.tile(sinc.T, (2, 1))),
        "c_band": _band_bias(), "c_poolat": f(np.transpose(_pool_mats(), (2, 0, 1, 3))),
    }
    keys = f(inp["peer_keys"])
    kb = np.zeros((DEPTH, 128, 256), np.float32)
    for j in range(2):
        kb[:, j * 64:(j + 1) * 64, j * 128:(j + 1) * 128] = np.transpose(keys[:, j], (0, 2, 1))
    shared["peer_kb"] = kb
    cwk, cwv = f(inp["cache_win_k"]), f(inp["cache_win_v"])
    cdk, cdv = f(inp["cache_diff_k"]), f(inp["cache_diff_v"])
    maps = []
    for core in range(8):
        b = core // 4
        xin = np.concatenate([x_prompt[2 * core].reshape(2, 128, D), x_prompt[2 * core + 1].reshape(2, 128, D),
                              x_sample[b].reshape(8, 128, D)], 0)
        cvec = np.stack([c_ctx.reshape(8, 128).T, c[b].reshape(8, 128).T], 1)
        m = dict(shared)
        m["xin"] = f(xin)
        m["cvec"] = f(cvec)
        m["cwkT"] = f(np.transpose(cwk[b].reshape(DEPTH, 256, 128), (0, 2, 1)))
        m["cwv"] = f(cwv[b].reshape(DEPTH, 256, 128))
        m["cdkT"] = f(np.transpose(cdk[b].reshape(DEPTH, 256, 4, 64), (0, 3, 2, 1)))
        m["cdv"] = f(cdv[b].reshape(DEPTH, 256, 256))
        maps.append(m)
    return maps


_NC_CACHE = {}


def kernel(**inputs):
    maps = _prep_inputs(inputs)
    if "nc" not in _NC_CACHE:
        _NC_CACHE["nc"] = build_program()
    nc = _NC_CACHE["nc"]
    res = run_bass_kernel_spmd(nc, maps, core_ids=list(range(8)))
    R = res.results
    y_p = np.zeros((16, 256, D), np.float32)
    y_s = np.zeros((2, 1024, D), np.float32)
    nwk = np.zeros((16, DEPTH, 256, 2, 64), np.float32)
    nwv = np.zeros((16, DEPTH, 256, 2, 64), np.float32)
    ndk = np.zeros((16, DEPTH, 256, 4, 2, 32), np.float32)
    ndv = np.zeros((16, DEPTH, 256, 4, 64), np.float32)
    for core in range(8):
        y = np.asarray(R[core]["y"])
        for s in range(2):
            y_p[2 * core + s] = y[2 * s:2 * s + 2].reshape(256, D)
            nwk[2 * core + s] = np.asarray(R[core]["o_wk"])[s].reshape(DEPTH, 256, 2, 64)
            nwv[2 * core + s] = np.asarray(R[core]["o_wv"])[s].reshape(DEPTH, 256, 2, 64)
            ndk[2 * core + s] = np.asarray(R[core]["o_dk"])[s].reshape(DEPTH, 256, 4, 2, 32)
            ndv[2 * core + s] = np.asarray(R[core]["o_dv"])[s].reshape(DEPTH, 256, 4, 64)
        if core % 4 == 0:
            y_s[core // 4] = y[4:12].reshape(1024, D)
    kernel.last_results = R
    return (y_p, y_s, nwk, nwv, ndk, ndv)
```

```python
import math
import numpy as np
import concourse.bass as bass
import concourse.mybir as mybir
from concourse.bass_utils import run_bass_kernel_spmd

F32 = mybir.dt.float32
BF16 = mybir.dt.bfloat16
I32 = mybir.dt.int32
U32 = mybir.dt.uint32
AF = mybir.ActivationFunctionType
ALU = mybir.AluOpType
AX = mybir.AxisListType

DEPTH = 4
D = 1024
NT = 12
NEXP = 16384
ALPHA = (2 * DEPTH) ** 0.25
LN_EPS = 1e-5
NEG = -30000.0
NGB = 8
DEBUG = {}
CFG = {"depth": DEPTH, "stop": None, "nexp": NEXP}


class _Stop(Exception):
    pass


C_AU, C_AV, C_BQ, C_BK, C_BV, C_CQ, C_CK, C_CV, C_DZ = 0, 256, 512, 768, 896, 1024, 1280, 1536, 1792


class Sched:
    ENG = ("pe", "act", "dve", "pool", "sp")

    def __init__(self, n_dma_sems):
        self.q = {e: [] for e in self.ENG}
        self.cnt = {e: 0 for e in self.ENG}
        self.seen = {e: {} for e in self.ENG}
        self.last_w = {}
        self.readers = {}
        self.dma_n = [0] * n_dma_sems
        self.rr = 0
        self.n_general = n_dma_sems

    def _collect(self, eng, reads, writes, extra=(), raw=None):
        deps = {}
        own = ("e", eng)
        raw = set(reads if raw is None else raw)

        def add(d, same_ok):
            if d is None:
                return
            s, v = d
            if s == own and not same_ok:
                return
            if deps.get(s, 0) < v:
                deps[s] = v
        same_raw = eng != "pe"
        for k in reads:
            add(self.last_w.get(k), same_raw)
        for k in writes:
            add(self.last_w.get(k), same_raw and k in raw)
            for d in self.readers.get(k, {}).items():
                add(d, False)
        for d in extra:
            add(d, True)
        waits = []
        sn = self.seen[eng]
        for s, v in deps.items():
            if sn.get(s, 0) < v:
                sn[s] = v
                waits.append((s, v))
        return waits

    def _commit(self, me, reads, writes):
        for k in writes:
            self.last_w[k] = me
            self.readers[k] = {}
        for k in reads:
            r = self.readers.setdefault(k, {})
            if r.get(me[0], 0) < me[1]:
                r[me[0]] = me[1]

    @staticmethod
    def _norm(reads, writes):
        ps = [k for k in reads if k.startswith("ps")]
        if ps:
            reads = [k for k in reads if not k.startswith("ps")]
            writes = list(writes) + [k for k in ps if k not in writes]
        return reads, writes

    def op(self, eng, fn, reads=(), writes=(), extra=()):
        raw = list(reads)
        reads, writes = self._norm(reads, writes)
        waits = self._collect(eng, reads, writes, extra, raw)
        self.cnt[eng] += 1
        me = (("e", eng), self.cnt[eng])
        self.q[eng].append((waits, fn, me, 1))
        self._commit(me, reads, writes)

    def snapshot(self):
        snap = [(("e", e), self.cnt[e]) for e in self.ENG if self.cnt[e]]
        snap += [(("d", i), 16 * n) for i, n in enumerate(self.dma_n) if n]
        return snap

    def dma(self, eng, fn, reads=(), writes=(), sem=None):
        raw = list(reads)
        reads, writes = self._norm(reads, writes)
        if sem is None:
            sem = self.rr
            self.rr = (self.rr + 1) % self.n_general
        prev = (("d", sem), 16 * self.dma_n[sem]) if self.dma_n[sem] else None
        waits = self._collect(eng, reads, writes, extra=(prev,), raw=raw)
        self.dma_n[sem] += 1
        me = (("d", sem), 16 * self.dma_n[sem])
        self.q[eng].append((waits, fn, me, 16))
        self._commit(me, reads, writes)

    def final_waits(self, eng):
        waits = []
        for i, n in enumerate(self.dma_n):
            if n:
                waits.append((("d", i), 16 * n))
        for e in self.ENG:
            if e != eng and self.cnt[e]:
                waits.append((("e", e), self.cnt[e]))
        return waits


def _rope_tables(dim):
    n_tok, grid = 1024, 64
    rows = n_tok // grid
    row = np.repeat(np.arange(rows, dtype=np.float32), grid)
    col = np.tile(np.arange(grid, dtype=np.float32), rows)
    nf = dim // 4
    inv = (np.float32(10000.0) ** (-np.arange(nf, dtype=np.float32) / np.float32(nf))).astype(np.float32)
    ar = row[:, None] * inv
    ac = col[:, None] * inv
    ang = np.concatenate([ar, ar, ac, ac], axis=-1).astype(np.float32)
    return np.cos(ang).astype(np.float32), np.sin(ang).astype(np.float32)


def _rot_matrix(dim, reps):
    q = dim // 4
    r = np.zeros((dim, dim), np.float32)
    for a in range(q):
        r[q + a, a] = -1.0
        r[a, q + a] = 1.0
        r[3 * q + a, 2 * q + a] = -1.0
        r[2 * q + a, 3 * q + a] = 1.0
    out = np.zeros((dim * reps, dim * reps), np.float32)
    for i in range(reps):
        out[i * dim:(i + 1) * dim, i * dim:(i + 1) * dim] = r
    return out


def _pool_mats():
    S = 384
    out = np.zeros((5, 4, 128, 128), np.float32)
    for g, w in enumerate((2, 4, 8, 16)):
        def mat(seq_lo, seq_hi):
            m = np.zeros((S, S), np.float32)
            for t in range(128, 256):
                lo = min(max(t - w // 2, seq_lo), seq_hi)
                hi = min(max(t + w // 2, seq_lo), seq_hi)
                m[t, lo:hi] = 1.0 / float(hi - lo)
                m[t, t] -= 1.0
            return m
        mid = mat(0, S)
        first = mat(128, S)
        last = mat(0, 256)
        out[0, g] = first[128:256, 128:256].T
        out[1, g] = mid[128:256, 128:256].T
        out[2, g] = last[128:256, 128:256].T
        out[3, g] = mid[128:256, 0:128].T
        out[4, g] = mid[128:256, 256:384].T
    return out


def _band_bias():
    qi = np.arange(128)[:, None]
    ki = np.arange(128)[None, :]
    prev = np.where(ki >= qi, 0.0, NEG).astype(np.float32)
    nxt = np.where(ki <= qi, 0.0, NEG).astype(np.float32)
    z = np.zeros((128, 128), np.float32)
    full = np.full((128, 128), NEG, np.float32)
    return np.stack([np.concatenate([full, z, nxt], 1), np.concatenate([prev, z, nxt], 1),
                     np.concatenate([prev, z, full], 1)], 0)


def build_program():
    nc = bass.Bass("TRN2", target_bir_lowering=False)

    def din(name, shape, dt=F32):
        return nc.dram_tensor(name, list(shape), dt, kind="ExternalInput")

    def dout(name, shape, dt=F32):
        return nc.dram_tensor(name, list(shape), dt, kind="ExternalOutput")

    d_x = din("xin", [NT, 128, D]).ap()
    d_cvec = din("cvec", [128, 2, 8]).ap()
    d_cwkT = din("cwkT", [DEPTH, 128, 256]).ap()
    d_cwv = din("cwv", [DEPTH, 256, 128]).ap()
    d_cdkT = din("cdkT", [DEPTH, 64, 4, 256]).ap()
    d_cdv = din("cdv", [DEPTH, 256, 256]).ap()
    d_wmod = din("w_mod", [CFG["depth"], D, 6 * D]).ap()
    h_bmod = din("b_mod", [DEPTH, 6 * D])
    d_win = din("w_in", [CFG["depth"], D, 2048]).ap()
    d_wout = din("w_out", [CFG["depth"], D, D]).ap()
    d_cwT = din("chunk_wT", [DEPTH, 128, 4, 128]).ap()
    d_cbT = din("chunk_bT", [DEPTH, 128, 4]).ap()
    h_sink = din("win_sink", [DEPTH, 4])
    h_lamq = din("lam_q", [DEPTH, 64])
    h_lamk = din("lam_k", [DEPTH, 64])
    h_subg = din("subln_g", [DEPTH, 64])
    d_poolw = din("pool_wr", [DEPTH, 64, 4, 64]).ap()
    h_pscale = din("pool_scale", [DEPTH, 256])
    h_lng = din("ln_g", [DEPTH, 2, D])
    h_lnb = din("ln_b", [DEPTH, 2, D])
    d_wq = din("peer_wq", [CFG["depth"], D, D]).ap()
    d_kb = din("peer_kb", [DEPTH, 128, 256]).ap()
    d_puv = din("peer_uv", [CFG["depth"] * CFG["nexp"], 2 * D]).ap()
    d_ident = din("c_ident", [128, 128]).ap()
    d_rb = din("c_rb", [128, 128]).ap()
    d_rc = din("c_rc", [64, 64]).ap()
    d_cosb = din("c_cosb", [128, 1024]).ap()
    d_sinb = din("c_sinb", [128, 1024]).ap()
    d_cosc = din("c_cosc", [64, 1024]).ap()
    d_sinc = din("c_sinc", [64, 1024]).ap()
    d_band = din("c_band", [3, 128, 384]).ap()
    d_poolat = din("c_poolat", [128, 5, 4, 128]).ap()

    o_y = dout("y", [NT, 128, D]).ap()
    o_wk = dout("o_wk", [2, DEPTH, 256, 128]).ap()
    o_wv = dout("o_wv", [2, DEPTH, 256, 128]).ap()
    o_dk = dout("o_dk", [2, DEPTH, 256, 256]).ap()
    o_dv = dout("o_dv", [2, DEPTH, 256, 256]).ap()
    dbg_out = {}
    for name, shp in DEBUG.items():
        dbg_out[name] = dout("dbg_" + name, shp).ap()

    def bcast_row(h, off, n, parts=128):
        return bass.AP(h, off, [[0, parts], [1, n]])

    def sb(name, shape, dt):
        return nc.alloc_sbuf_tensor(name, list(shape), dt)

    x = sb("x", [128, NT, D], F32)
    mod = sb("mod", [128, 2, 3072], F32)
    reg1 = sb("reg1", [128, 24576], BF16)
    A16N, A32N = 21760, 6784
    a16 = sb("a16", [128, A16N], BF16)
    a32 = sb("a32", [128, A32N], F32)
    idx_i = sb("idx_i", [128, 3, 128], I32)
    si_u = sb("si_u", [128, 4, 16], U32)
    ident_f = sb("ident_f", [128, 128], F32)
    ident_b = sb("ident_b", [128, 128], BF16)
    rb_f = sb("rb_f", [128, 128], F32)
    rc_f = sb("rc_f", [64, 64], F32)
    band = sb("band", [128, 3, 384], BF16)
    poolat = sb("poolat", [128, 5, 4, 128], BF16)
    silu_b = sb("silu_b", [128, 2, 8], BF16)
    cvec = sb("cvec_sb", [128, 2, 8], F32)
    cwT = sb("cwT", [128, 4, 128], BF16)
    cbT = sb("cbT", [128, 4], F32)
    sink = sb("sink", [128, 4], F32)
    lamt = sb("lamt", [128, 2, 64], F32)
    lamv = sb("lamv", [128, 8], F32)
    subg = sb("subg", [128, 64], F32)
    poolw = sb("poolw", [64, 4, 64], BF16)
    pscale = sb("pscale", [128, 256], F32)
    kb_b = sb("kb_b", [128, 256], BF16)
    st6 = sb("st6", [128, 2, 6], F32)
    sm = sb("sm", [128, 32], F32)
    scrB = sb("scrB", [128, 640], F32)

    psA = nc.alloc_psum_tensor("psA", [128, 2048], F32)
    psB = nc.alloc_psum_tensor("psB", [128, 1024], F32)
    psC = nc.alloc_psum_tensor("psC", [128, 512], F32)
    psT = nc.alloc_psum_tensor("psT", [128, 1024], BF16)

    class Carver:
        def __init__(self, t, n):
            self.t, self.n, self.o = t, n, 0

        def take(self, size, pat=None, parts=128, **kw):
            assert self.o + size <= self.n, (self.o, size, self.n)
            ap = self.t[0:parts, self.o:self.o + size]
            self.o += size
            if pat:
                ap = ap.rearrange(pat, **kw)
            return ap

    w_in_sb = reg1[:, 0:16384].rearrange("p (k n) -> p k n", n=2048)
    w_out_sb = reg1[:, 16384:24576].rearrange("p (k n) -> p k n", n=1024)
    wq_sb = reg1[:, 0:8192].rearrange("p (k n) -> p k n", n=1024)
    gbuf = [reg1[:, 8192 + i * 2048: 8192 + (i + 1) * 2048] for i in range(NGB)]

    c16 = Carver(a16, A16N)
    h_bf = c16.take(1024)
    hT = c16.take(1024, "p (k t) -> p k t", t=128)
    z_all = c16.take(2048, "p (b c) -> p b c", c=256)
    kTb_all = c16.take(1280)
    vb_all = c16.take(1280, "p (b c) -> p b c", c=128)
    kTc_all = c16.take(5120, "p (h s) -> p h s", parts=64, s=1280)
    vc_all = c16.take(2560, "p (b c) -> p b c", c=256)
    ua = c16.take(512)
    qTb = c16.take(256, "p (c t) -> p c t", t=128)
    qTc = c16.take(512, "p (c t) -> p c t", parts=64, t=128)
    w_bf = c16.take(1280)
    wT = c16.take(1280, "p (b q) -> p b q", q=128)
    o_cat = c16.take(1024)
    pooledT = c16.take(512, "p (g t) -> p g t", parts=64, t=128)
    wmb = [c16.take(1024, "p (k n) -> p k n", n=128) for _ in range(2)]
    c32 = Carver(a32, A32N)
    scr0 = c32.take(1280)
    scr1 = c32.take(1408)
    csB = c32.take(256, "p (a t) -> p a t", t=128)
    csC = c32.take(256, "p (a t) -> p a t", parts=64, t=128)
    bmb = [c32.take(128) for _ in range(2)]
    assert c32.o == 3456
    lnp = c32.take(2048, "p (a n) -> p a n", n=1024)
    p16 = Carver(a16, A16N)
    h2 = [p16.take(1024) for _ in range(2)]
    h2T = p16.take(1024, "p (k t) -> p k t", t=128)
    qT_sb = p16.take(1024, "p (c t) -> p c t", t=128)
    junk = p16.take(1024)
    w_pb = [p16.take(128) for _ in range(2)]
    BIGN = 64 * 65
    bigf = [p16.take(BIGN) for _ in range(2)]
    p_wmb = [p16.take(1024, "p (k n) -> p k n", n=128) for _ in range(2)]
    p32 = Carver(a32, A32N)
    e_scr0 = p32.take(1024)
    e_scr1 = p32.take(256)
    sc_sb = p32.take(512)
    scr_mr = p32.take(256)
    sv = p32.take(64, "p (g k) -> p g k", k=16)
    sif = p32.take(64, "p (g k) -> p g k", k=16)
    cand = p32.take(512, "p (h j) -> p h j", j=256)
    eid = p32.take(512, "p (h j) -> p h j", j=256)
    cvv = p32.take(128, "p (h k) -> p h k", k=16)
    gat = p32.take(128, "p (h k) -> p h k", k=16)
    assert p32.o == 3456
    p32.o = 5504
    idxf = p32.take(128)
    gT = [p32.take(128) for _ in range(3)]
    aT = [p32.take(256, "p (a t) -> p a t", t=128) for _ in range(2)]
    p_bmb = [p32.take(128) for _ in range(2)]

    n_general = 24
    REGS = {}
    S = Sched(n_general + 2 * NGB + 4)
    SEM_U = [n_general + i for i in range(NGB)]
    SEM_V = [n_general + NGB + i for i in range(NGB)]

    def V(fn, r=(), w=()):
        S.op("dve", fn, r, w)

    def ACT(fn, r=(), w=()):
        S.op("act", fn, r, w)

    def PE(fn, r=(), w=()):
        S.op("pe", fn, r, w)

    def POOL(fn, r=(), w=()):
        S.op("pool", fn, r, w)

    def DMA(eng, out, in_, r=(), w=(), sem=None):
        S.dma(eng, lambda e: e.dma_start(out=out, in_=in_), r, w, sem)

    def barrier():
        snap = S.snapshot()
        for e in S.ENG:
            S.op(e, lambda en: en.nop(), extra=snap)
        S.readers = {}
        S.last_w = {}

    DMA("sp", ident_f[:], d_ident, w=["ident_f"])
    DMA("sp", rb_f[:], d_rb, w=["rb_f"])
    DMA("sp", rc_f[:], d_rc, w=["rc_f"])
    DMA("pool", band[:], d_band.rearrange("v q k -> q v k"), w=["band"])
    DMA("sp", cvec[:], d_cvec, w=["cvec"])
    DMA("pool", ident_b[:], d_ident, w=["ident_b"])
    DMA("pool", poolat[:], d_poolat, w=["poolat"])
    for t in range(NT):
        DMA("sp", x[:, t, :], d_x[t], w=[f"x{t}"])
    ACT(lambda e: e.activation(out=silu_b[:], in_=cvec[:], func=AF.Silu), ["cvec"], ["silu_b"])
    for i in range(2):
        V(lambda e, i=i: e.memset(bigf[i], 0.0), [], [f"big{i}"])

    def ln_stats(src_ap, src_keys, tag):
        V(lambda e: e.bn_stats(out=st6[:, 0, :], in_=src_ap[:, 0:512]), src_keys, ["st6a"])
        V(lambda e: e.bn_stats(out=st6[:, 1, :], in_=src_ap[:, 512:1024]), src_keys, ["st6b"])
        V(lambda e: e.bn_aggr(out=sm[:, 0:2], in_=st6[:].rearrange("p a b -> p (a b)")), ["st6a", "st6b"], ["sm_mv"])
        V(lambda e: e.tensor_scalar(out=sm[:, 2:3], in0=sm[:, 1:2], scalar1=LN_EPS, scalar2=None, op0=ALU.add),
          ["sm_mv"], ["sm_rstd"])
        ACT(lambda e: e.activation(out=sm[:, 2:3], in_=sm[:, 2:3], func=AF.Sqrt), ["sm_rstd"], ["sm_rstd"])
        V(lambda e: e.reciprocal(out=sm[:, 2:3], in_=sm[:, 2:3]), ["sm_rstd"], ["sm_rstd"])
        V(lambda e: e.scalar_tensor_tensor(out=sm[:, 3:4], in0=sm[:, 0:1], scalar=-1.0, in1=sm[:, 2:3],
                                           op0=ALU.mult, op1=ALU.mult), ["sm_mv", "sm_rstd"], ["sm_nmr"])
        return sm[:, 2:3], sm[:, 3:4]

    def ln_mod(t, which, sh_off, sc_off, dst_bf, dst_key, tmp, tmp_key):
        rstd, nmr = ln_stats(x[:, t, :], [f"x{t}"], "a")
        ACT(lambda e: e.activation(out=tmp[:, 0:1024], in_=x[:, t, :], func=AF.Identity, bias=nmr, scale=rstd),
            [f"x{t}", "sm_rstd", "sm_nmr"], [tmp_key])
        V(lambda e: e.tensor_tensor(out=tmp[:, 0:1024], in0=tmp[:, 0:1024], in1=mod[:, which, sc_off:sc_off + 1024],
                                    op=ALU.mult), [tmp_key, "mod"], [tmp_key])
        V(lambda e: e.tensor_tensor(out=dst_bf, in0=tmp[:, 0:1024], in1=mod[:, which, sh_off:sh_off + 1024],
                                    op=ALU.add), [tmp_key, "mod"], [dst_key])

    def transpose8(src_bf, src_key, dst, dst_key):
        def tr(e):
            ins = None
            for k in range(8):
                ins = e.transpose(out=psT[:, k * 128:(k + 1) * 128], in_=src_bf[:, k * 128:(k + 1) * 128],
                                  identity=ident_b[:])
            return ins
        PE(tr, [src_key, "ident_b"], ["psT"])
        ACT(lambda e: e.copy(out=dst.rearrange("p k t -> p (k t)"), in_=psT[:, :]), ["psT"], [dst_key])

    def mod_half(l, half, wb, wb_key, bb, bb_key):
        for j in range(24):
            c0 = half * 3072 + j * 128
            b = j % 2
            DMA("pool", wb[b], d_wmod[l][:, c0:c0 + 128].rearrange("(k p) n -> p k n", p=128), w=[f"{wb_key}{b}"])
            DMA("sp", bb[b], bcast_row(h_bmod, l * 6144 + c0, 128), w=[f"{bb_key}{b}"])
            for which in range(2):
                def mm(e, which=which, b=b):
                    ins = None
                    for k in range(8):
                        ins = e.matmul(psA[:, which * 512: which * 512 + 128],
                                       lhsT=silu_b[:, which, k:k + 1].to_broadcast([128, 128]),
                                       rhs=wb[b][:, k, :], start=(k == 0), stop=(k == 7))
                    return ins
                PE(mm, ["silu_b", f"{wb_key}{b}"], [f"psA{which}"])
                addc = 1.0 if 8 <= j < 16 else 0.0
                V(lambda e, which=which, b=b, j=j, addc=addc: e.scalar_tensor_tensor(
                    out=mod[:, which, j * 128:(j + 1) * 128], in0=psA[:, which * 512: which * 512 + 128],
                    scalar=addc, in1=bb[b], op0=ALU.add, op1=ALU.add),
                  [f"psA{which}", f"{bb_key}{b}"], ["mod"])

    def dbg(name, ap, keys):
        if name in dbg_out:
            DMA("pool", dbg_out[name], ap, r=keys)

    def attn_pv(blocks, vfn, vkey, out_ps, out_key, first_group_start=True):
        n = len(blocks)
        for r0 in range(0, n, 8):
            grp = list(range(r0, min(n, r0 + 8)))

            def tr(e, grp=grp, r0=r0):
                ins = None
                for n_ in grp:
                    ins = e.transpose(out=psT[:, (n_ - r0) * 128:(n_ - r0 + 1) * 128],
                                      in_=w_bf[:, n_ * 128:(n_ + 1) * 128], identity=ident_b[:])
                return ins
            PE(tr, ["w_bf", "ident_b"], ["psT"])
            ACT(lambda e, grp=grp, r0=r0: e.copy(out=wT[:, r0:r0 + len(grp), :].rearrange("p b q -> p (b q)"),
                                                 in_=psT[:, 0:len(grp) * 128]), ["psT"], ["wT"])

        def pv(e):
            ins = None
            for n_, jb in enumerate(blocks):
                ins = e.matmul(out_ps, lhsT=wT[:, n_, :], rhs=vfn(jb), start=(n_ == 0), stop=(n_ == n - 1))
            return ins
        PE(pv, ["wT", vkey], [out_key])

    def post_ln(t, src0, src0_key, src1, src1_key, which, tmp, tmp_key):
        V(lambda e: e.tensor_tensor(out=tmp[:, 0:512], in0=src0, in1=mod[:, which, 2048:2560], op=ALU.mult),
          [src0_key, "mod"], [tmp_key])
        V(lambda e: e.tensor_tensor(out=tmp[:, 512:1024], in0=src1, in1=mod[:, which, 2560:3072], op=ALU.mult),
          [src1_key, "mod"], [tmp_key])
        V(lambda e: e.scalar_tensor_tensor(out=tmp[:, 0:1024], in0=x[:, t, :], scalar=ALPHA, in1=tmp[:, 0:1024],
                                           op0=ALU.mult, op1=ALU.add), [f"x{t}", tmp_key], [tmp_key])
        rstd, nmr = ln_stats(tmp, [tmp_key], "p")
        ACT(lambda e: e.activation(out=tmp[:, 0:1024], in_=tmp[:, 0:1024], func=AF.Identity, bias=nmr, scale=rstd),
            [tmp_key, "sm_rstd", "sm_nmr"], [tmp_key])
        V(lambda e: e.tensor_tensor(out=tmp[:, 0:1024], in0=tmp[:, 0:1024], in1=lnp[:, 0, :], op=ALU.mult),
          [tmp_key, "lnp"], [tmp_key])
        V(lambda e: e.tensor_tensor(out=x[:, t, :], in0=tmp[:, 0:1024], in1=lnp[:, 1, :], op=ALU.add),
          [tmp_key, "lnp"], [f"x{t}"])

    def ck(name):
        if CFG["stop"] == name:
            raise _Stop()

    try:
      ck("init")
      for l in range(CFG["depth"]):
          lam_init = 0.8 - 0.6 * math.exp(-0.3 * l)
          for k in range(8):
              DMA("pool", w_in_sb[:, k, :], d_win[l][k * 128:(k + 1) * 128, :], w=["w_in"])
          for k in range(0, 8, 2):
              DMA("pool", w_out_sb[:, k:k + 2, :],
                  d_wout[l][k * 128:(k + 2) * 128, :].rearrange("(k p) n -> p k n", p=128), w=["w_out"])
          DMA("pool", cwT[:], d_cwT[l], w=["cwT"])
          DMA("pool", poolw[:], d_poolw[l], w=["poolw"])
          DMA("pool", kb_b[:], d_kb[l], w=["kb_b"])
          DMA("sp", cbT[:], d_cbT[l], w=["cbT"])
          DMA("sp", sink[:], bcast_row(h_sink, l * 4, 4), w=["sink"])
          DMA("sp", lamt[:, 0, :], bcast_row(h_lamq, l * 64, 64), w=["lamt"])
          DMA("sp", lamt[:, 1, :], bcast_row(h_lamk, l * 64, 64), w=["lamt"])
          DMA("sp", subg[:], bcast_row(h_subg, l * 64, 64), w=["subg"])
          DMA("sp", pscale[:], bcast_row(h_pscale, l * 256, 256), w=["pscale"])
          DMA("sp", lnp[:, 0, :], bcast_row(h_lng, (l * 2 + 0) * D, D), w=["lnp"])
          DMA("sp", lnp[:, 1, :], bcast_row(h_lnb, (l * 2 + 0) * D, D), w=["lnp"])
          for j in range(2):
              V(lambda e, j=j: e.tensor_tensor(out=lamt[:, 0, j * 32:(j + 1) * 32], in0=lamt[:, 0, j * 32:(j + 1) * 32],
                                               in1=lamt[:, 1, j * 32:(j + 1) * 32], op=ALU.mult), ["lamt"], ["lamt"])
              V(lambda e, j=j: e.reduce_sum(out=lamv[:, j:j + 1], in_=lamt[:, 0, j * 32:(j + 1) * 32], axis=AX.X),
                ["lamt"], ["lamv"])
          ACT(lambda e: e.activation(out=lamv[:, 2:4], in_=lamv[:, 0:2], func=AF.Exp), ["lamv"], ["lamve"])
          V(lambda e, lam_init=lam_init: e.scalar_tensor_tensor(out=lamv[:, 5:6], in0=lamv[:, 3:4], scalar=-lam_init, in1=lamv[:, 2:3],
                                             op0=ALU.add, op1=ALU.subtract), ["lamve"], ["nlam"])
          V(lambda e, lam_init=lam_init: e.tensor_scalar(out=subg[:], in0=subg[:], scalar1=1.0 - lam_init, scalar2=None, op0=ALU.mult),
            ["subg"], ["subg"])
          mod_half(l, 0, wmb, "wmb", bmb, "bmb")
          if l == 0:
              dbg("mod0", mod[:, 0, :], ["mod"])
          ck("mod")

          seqs = [(0, 2, False), (2, 2, False), (4, 8, True)]
          for (t0, nb, lat) in seqs:
              which = 1 if lat else 0
              nkb = nb + (2 if lat else 0)
              if lat:
                  DMA("pool", kTb_all[:, 1024:1280], d_cwkT[l], w=["kTb_all"])
                  DMA("pool", vb_all[:, 8:10, :], d_cwv[l].rearrange("(b p) c -> p b c", p=128), w=["vb_all"])
                  DMA("pool", kTc_all[:, :, 1024:1280], d_cdkT[l], w=["kTc_all"])
                  DMA("pool", vc_all[:, 8:10, :], d_cdv[l].rearrange("(b p) c -> p b c", p=128), w=["vc_all"])

              def rope(src_ps, src_key, n_ch, parts, tmp32, tmp_key, cs, cskey, rmat, rkey, dst, dst_key, lat=lat):
                  if not lat:
                      ACT(lambda e: e.copy(out=dst, in_=src_ps), [src_key], [dst_key])
                      return
                  V(lambda e: e.tensor_copy(out=tmp32, in_=src_ps), [src_key], [tmp_key])

                  def mm(e):
                      ins = None
                      for c in range(n_ch):
                          ins = e.matmul(src_ps[:, c, :], lhsT=rmat, rhs=tmp32[:, c, :], start=True, stop=True)
                      return ins
                  PE(mm, [tmp_key, rkey], [src_key])
                  V(lambda e: e.tensor_tensor(out=tmp32, in0=tmp32, in1=cs[:, 0:1, :].to_broadcast([parts, n_ch, 128]),
                                              op=ALU.mult), [tmp_key, cskey], [tmp_key])
                  V(lambda e: e.tensor_tensor(out=src_ps, in0=src_ps, in1=cs[:, 1:2, :].to_broadcast([parts, n_ch, 128]),
                                              op=ALU.mult), [src_key, cskey], [src_key])
                  V(lambda e: e.tensor_tensor(out=dst, in0=tmp32, in1=src_ps, op=ALU.add), [tmp_key, src_key], [dst_key])

              def load_cs(i):
                  tk = i * 128
                  DMA("sp", csB[:, 0, :], d_cosb[:, tk:tk + 128], w=["csB"])
                  DMA("sp", csB[:, 1, :], d_sinb[:, tk:tk + 128], w=["csB"])
                  DMA("sp", csC[:, 0, :], d_cosc[:, tk:tk + 128], w=["csC"])
                  DMA("sp", csC[:, 1, :], d_sinc[:, tk:tk + 128], w=["csC"])

              for i in range(nb):
                  t = t0 + i
                  ln_mod(t, which, 0, 1024, h_bf, "h_bf", scr0, "scr0")
                  ck("ln")
                  transpose8(h_bf, "h_bf", hT, "hT")
                  ck("tr")
                  if lat:
                      load_cs(i)

                  def tm(e):
                      ins = None
                      for (c_lo, n, po) in ((C_BK, 256, 0), (C_CK, 256, 512), (C_CV, 512, 1024)):
                          for k in range(8):
                              ins = e.matmul(psA[:, po:po + n], lhsT=hT[:, k, :], rhs=w_in_sb[:, k, c_lo:c_lo + n],
                                             start=(k == 0), stop=(k == 7))
                      return ins
                  PE(tm, ["hT", "w_in"], ["psA0", "psA1", "psA2"])

                  def fm(e):
                      ins = None
                      for k in range(8):
                          ins = e.matmul(psC[:, 0:128], lhsT=w_in_sb[:, k, C_BK:C_BK + 128], rhs=hT[:, k, :],
                                         start=(k == 0), stop=(k == 7))
                      for c in range(4):
                          for k in range(8):
                              ins = e.matmul(psB[0:64, c * 128:(c + 1) * 128],
                                             lhsT=w_in_sb[:, k, C_CK + c * 64:C_CK + (c + 1) * 64], rhs=hT[:, k, :],
                                             start=(k == 0), stop=(k == 7))
                      return ins
                  PE(fm, ["hT", "w_in"], ["psC", "psB0"])
                  ck("mm")
                  ACT(lambda e, i=i: e.copy(out=vb_all[:, i, :], in_=psA[:, 128:256]), ["psA0"], ["vb_all"])
                  ACT(lambda e, i=i: e.copy(out=vc_all[:, i, :], in_=psA[:, 1024:1280]), ["psA2"], ["vc_all"])
                  ACT(lambda e, i=i: e.copy(out=z_all[:, i, :], in_=psA[:, 1280:1536]), ["psA2"], ["z_all"])
                  if not lat:
                      sq = t0 // 2
                      V(lambda e: e.tensor_copy(out=scr1[:, 0:256], in_=psA[:, 0:256]), ["psA0"], ["scr1o"])
                      V(lambda e: e.tensor_copy(out=scr1[:, 256:512], in_=psA[:, 512:768]), ["psA1"], ["scr1o"])
                      V(lambda e: e.tensor_copy(out=scr1[:, 512:768], in_=psA[:, 1024:1280]), ["psA2"], ["scr1o"])
                      r0 = i * 128
                      DMA("sp", o_wk[sq, l, r0:r0 + 128, :], scr1[:, 0:128], r=["scr1o"])
                      DMA("sp", o_wv[sq, l, r0:r0 + 128, :], scr1[:, 128:256], r=["scr1o"])
                      DMA("sp", o_dk[sq, l, r0:r0 + 128, :], scr1[:, 256:512], r=["scr1o"])
                      DMA("sp", o_dv[sq, l, r0:r0 + 128, :], scr1[:, 512:768], r=["scr1o"])
                  ck("ev")
                  tk = i * 128
                  fmB32 = scr1[:, 768:896].rearrange("p (c t) -> p c t", t=128)
                  fmC32 = scr1[0:64, 896:1408].rearrange("p (c t) -> p c t", t=128)
                  rope(psC[:, 0:128].rearrange("p (c t) -> p c t", t=128), "psC", 1, 128, fmB32, "scr1o", csB, "csB",
                       rb_f[:], "rb_f", kTb_all[:, tk:tk + 128].rearrange("p (c t) -> p c t", t=128), "kTb_all")
                  rope(psB[0:64, 0:512].rearrange("p (c t) -> p c t", t=128), "psB0", 4, 64, fmC32, "scr1o", csC, "csC",
                       rc_f[:], "rc_f", kTc_all[:, :, tk:tk + 128], "kTc_all")
              if l == 0 and t0 == 0:
                  dbg("kTb", kTb_all[:, 0:256], ["kTb_all"])
                  dbg("z", z_all[:, 0:2, :], ["z_all"])
              ck("p1")

              for i in range(nb):
                  t = t0 + i
                  ln_mod(t, which, 0, 1024, h_bf, "h_bf", scr0, "scr0")
                  transpose8(h_bf, "h_bf", hT, "hT")
                  if lat:
                      load_cs(i)

                  def tmA(e):
                      ins = None
                      for k in range(8):
                          ins = e.matmul(psA[:, 0:512], lhsT=hT[:, k, :], rhs=w_in_sb[:, k, 0:512],
                                         start=(k == 0), stop=(k == 7))
                      return ins
                  PE(tmA, ["hT", "w_in"], ["psA0"])

                  def fmq(e):
                      ins = None
                      for g in range(2):
                          for kv in range(2):
                              c0 = C_BQ + (kv * 2 + g) * 64
                              for k in range(8):
                                  ins = e.matmul(psC[kv * 64:(kv + 1) * 64, g * 128:(g + 1) * 128],
                                                 lhsT=w_in_sb[:, k, c0:c0 + 64], rhs=hT[:, k, :],
                                                 start=(k == 0), stop=(k == 7))
                      for c in range(4):
                          for k in range(8):
                              ins = e.matmul(psB[0:64, c * 128:(c + 1) * 128],
                                             lhsT=w_in_sb[:, k, C_CQ + c * 64:C_CQ + (c + 1) * 64], rhs=hT[:, k, :],
                                             start=(k == 0), stop=(k == 7))
                      return ins
                  PE(fmq, ["hT", "w_in"], ["psC", "psB0"])
                  ACT(lambda e: e.activation(out=ua, in_=psA[:, 0:512], func=AF.Gelu_apprx_tanh), ["psA0"], ["ua"])
                  fmB32 = scr1[:, 768:1024].rearrange("p (c t) -> p c t", t=128)
                  fmC32 = scr1[0:64, 0:512].rearrange("p (c t) -> p c t", t=128)
                  rope(psC[:, 0:256].rearrange("p (c t) -> p c t", t=128), "psC", 2, 128, fmB32, "scr1o", csB, "csB",
                       rb_f[:], "rb_f", qTb, "qTb")
                  rope(psB[0:64, 0:512].rearrange("p (c t) -> p c t", t=128), "psB0", 4, 64, fmC32, "scr1o", csC, "csC",
                       rc_f[:], "rc_f", qTc, "qTc")

                  def mmA(e):
                      ins = None
                      for hh in range(4):
                          ins = e.matmul(psA[:, 1536 + hh * 64:1536 + (hh + 1) * 64], lhsT=cwT[:, hh, :],
                                         rhs=ua[:, 256 + hh * 64:256 + (hh + 1) * 64], start=True, stop=True)
                      return ins
                  PE(mmA, ["ua", "cwT"], ["psA3"])
                  for hh in range(4):
                      V(lambda e, hh=hh: e.scalar_tensor_tensor(
                          out=o_cat[:, hh * 64:(hh + 1) * 64], in0=psA[:, 1536 + hh * 64:1536 + (hh + 1) * 64],
                          scalar=cbT[:, hh:hh + 1], in1=ua[:, hh * 64:(hh + 1) * 64], op0=ALU.add, op1=ALU.mult),
                        ["psA3", "cbT", "ua"], ["o_cat"])

                  rels = []
                  if i > 0:
                      rels.append((i - 1, 3))
                  rels.append((i, 0 if i == 0 else (2 if i == nb - 1 else 1)))
                  if i < nb - 1:
                      rels.append((i + 1, 4))

                  def mmD(e, rels=rels):
                      ins = None
                      for g in range(4):
                          for n_, (j, v) in enumerate(rels):
                              ins = e.matmul(psB[0:64, 512 + g * 128:512 + (g + 1) * 128],
                                             lhsT=z_all[:, j, g * 64:(g + 1) * 64], rhs=poolat[:, v, g, :],
                                             start=(n_ == 0), stop=(n_ == len(rels) - 1))
                      return ins
                  PE(mmD, ["z_all", "poolat"], ["psB1"])
                  ACT(lambda e: e.copy(out=pooledT.rearrange("p g t -> p (g t)"), in_=psB[0:64, 512:1024]),
                      ["psB1"], ["pooledT"])

                  def mmD2(e):
                      ins = None
                      for g in range(4):
                          ins = e.matmul(psA[:, 1792 + g * 64:1792 + (g + 1) * 64], lhsT=pooledT[:, g, :],
                                         rhs=poolw[:, g, :], start=True, stop=True)
                      return ins
                  PE(mmD2, ["pooledT", "poolw"], ["psA3"])
                  V(lambda e: e.tensor_tensor(out=o_cat[:, 768:1024], in0=psA[:, 1792:2048], in1=pscale[:], op=ALU.mult),
                    ["psA3", "pscale"], ["o_cat"])

                  if lat:
                      kblocks = [min(max(j, 0), nb - 1) for j in (i - 1, i, i + 1)] + [8, 9]
                      bvar = 0 if i == 0 else (2 if i == nb - 1 else 1)
                      cblocks = list(range(10))
                  else:
                      kblocks = [0, 1]
                      bvar = 0
                      cblocks = [0, 1]
                  NKB = len(kblocks) * 128
                  NKC = len(cblocks) * 128
                  csc = 32.0 ** -0.5
                  pieces = [(p0, min(512, NKC - p0)) for p0 in range(0, NKC, 512)]
                  w_bfB = hT.rearrange("p k t -> p (k t)")
                  wTB = h_bf[:, 0:640].rearrange("p (b q) -> p b q", q=128)

                  def pv_gen(blocks, wsrc, wkey, wTd, wTkey, vfn, vkey, out_ps, out_key):
                      n = len(blocks)
                      for r0 in range(0, n, 8):
                          grp = list(range(r0, min(n, r0 + 8)))

                          def tr(e, grp=grp, r0=r0):
                              ins = None
                              for n_ in grp:
                                  ins = e.transpose(out=psT[:, (n_ - r0) * 128:(n_ - r0 + 1) * 128],
                                                    in_=wsrc[:, n_ * 128:(n_ + 1) * 128], identity=ident_b[:])
                              return ins
                          PE(tr, [wkey, "ident_b"], ["psT"])
                          ACT(lambda e, grp=grp, r0=r0: e.copy(
                              out=wTd[:, r0:r0 + len(grp), :].rearrange("p b q -> p (b q)"),
                              in_=psT[:, 0:len(grp) * 128]), ["psT"], [wTkey])
                          yield

                      def pv(e):
                          ins = None
                          for n_, jb in enumerate(blocks):
                              ins = e.matmul(out_ps, lhsT=wTd[:, n_, :], rhs=vfn(jb), start=(n_ == 0), stop=(n_ == n - 1))
                          return ins
                      PE(pv, [wTkey, vkey], [out_key])
                      yield

                  def chainB(kblocks=kblocks, bvar=bvar, NKB=NKB, lat=lat):
                      for hh in range(4):
                          kv, g = hh // 2, hh % 2

                          def mmS(e, kv=kv, g=g):
                              ins = None
                              for n_, jb in enumerate(kblocks):
                                  dst = psA[:, 1536 + n_ * 128:1536 + (n_ + 1) * 128] if n_ < 4 else psB[:, 512:640]
                                  ins = e.matmul(dst, lhsT=qTb[kv * 64:(kv + 1) * 64, g, :],
                                                 rhs=kTb_all[kv * 64:(kv + 1) * 64, jb * 128:(jb + 1) * 128],
                                                 start=True, stop=True)
                              return ins
                          PE(mmS, ["qTb", "kTb_all"], ["psA3", "psB1"] if lat else ["psA3"])
                          yield
                          if lat:
                              V(lambda e: e.scalar_tensor_tensor(out=scrB[:, 0:384], in0=psA[:, 1536:1920], scalar=0.125,
                                                                 in1=band[:, bvar, :], op0=ALU.mult, op1=ALU.add),
                                ["psA3", "band"], ["scrB"])
                              V(lambda e: e.tensor_scalar(out=scrB[:, 384:512], in0=psA[:, 1920:2048], scalar1=0.125,
                                                          scalar2=None, op0=ALU.mult), ["psA3"], ["scrB"])
                              V(lambda e: e.tensor_scalar(out=scrB[:, 512:640], in0=psB[:, 512:640], scalar1=0.125,
                                                          scalar2=None, op0=ALU.mult), ["psB1"], ["scrB"])
                          else:
                              V(lambda e: e.tensor_scalar(out=scrB[:, 0:256], in0=psA[:, 1536:1792], scalar1=0.125,
                                                          scalar2=None, op0=ALU.mult), ["psA3"], ["scrB"])
                          yield
                          V(lambda e: e.reduce_max(out=sm[:, 8:9], in_=scrB[:, 0:NKB], axis=AX.X), ["scrB"], ["sm8"])
                          V(lambda e, hh=hh: e.tensor_scalar(out=sm[:, 9:10], in0=sm[:, 8:9], scalar1=sink[:, hh:hh + 1],
                                                             scalar2=-1.0, op0=ALU.max, op1=ALU.mult), ["sm8", "sink"], ["sm9"])
                          yield
                          ACT(lambda e: e.activation(out=scrB[:, 0:NKB], in_=scrB[:, 0:NKB], func=AF.Exp,
                                                     bias=sm[:, 9:10], accum_out=sm[:, 10:11]),
                              ["scrB", "sm9"], ["scrB", "sm10"])
                          ACT(lambda e, hh=hh: e.activation(out=sm[:, 11:12], in_=sink[:, hh:hh + 1], func=AF.Exp,
                                                            bias=sm[:, 9:10]), ["sink", "sm9"], ["sm11"])
                          yield
                          V(lambda e: e.tensor_tensor(out=sm[:, 12:13], in0=sm[:, 10:11], in1=sm[:, 11:12], op=ALU.add),
                            ["sm10", "sm11"], ["sm12"])
                          V(lambda e: e.reciprocal(out=sm[:, 13:14], in_=sm[:, 12:13]), ["sm12"], ["sm13"])
                          V(lambda e: e.tensor_scalar(out=w_bfB[:, 0:NKB], in0=scrB[:, 0:NKB], scalar1=sm[:, 13:14],
                                                      scalar2=None, op0=ALU.mult), ["scrB", "sm13"], ["hT"])
                          yield
                          yield from pv_gen(kblocks, w_bfB, "hT", wTB, "h_bf",
                                            lambda jb, kv=kv: vb_all[:, jb, kv * 64:(kv + 1) * 64], "vb_all",
                                            psC[:, 256 + hh * 64:256 + (hh + 1) * 64], "psC")
                      V(lambda e: e.tensor_copy(out=o_cat[:, 256:512], in_=psC[:, 256:512]), ["psC"], ["o_cat"])
                      yield

                  def chainC(cblocks=cblocks, NKC=NKC, pieces=pieces):
                      for hh in range(4):
                          for j in range(2):
                              ej = scr0 if j == 0 else scr1

                              def mmC(e, hh=hh, j=j):
                                  ins = None
                                  for (p0, pn) in pieces:
                                      ins = e.matmul(psA[:, p0:p0 + pn], lhsT=qTc[j * 32:(j + 1) * 32, hh, :],
                                                     rhs=kTc_all[j * 32:(j + 1) * 32, hh, p0:p0 + pn], start=True, stop=True)
                                  return ins
                              pk = ["psA0", "psA1", "psA2"][:len(pieces)]
                              PE(mmC, ["qTc", "kTc_all"], pk)
                              yield
                              V(lambda e, j=j: e.reduce_max(out=sm[:, 14 + j:15 + j], in_=psA[:, 0:NKC], axis=AX.X),
                                pk, [f"smx{j}"])
                              V(lambda e, j=j: e.tensor_scalar(out=sm[:, 16 + j:17 + j], in0=sm[:, 14 + j:15 + j],
                                                               scalar1=-csc, scalar2=None, op0=ALU.mult),
                                [f"smx{j}"], [f"smn{j}"])
                              yield
                              ACT(lambda e, j=j, ej=ej: e.activation(out=ej[:, 0:NKC], in_=psA[:, 0:NKC], func=AF.Exp,
                                                                     bias=sm[:, 16 + j:17 + j], scale=csc,
                                                                     accum_out=sm[:, 18 + j:19 + j]),
                                  pk + [f"smn{j}"], ["scr0" if j == 0 else "scr1o", f"sms{j}"])
                              yield
                          V(lambda e: e.reciprocal(out=sm[:, 20:22], in_=sm[:, 18:20]), ["sms0", "sms1"], ["smr"])
                          V(lambda e: e.tensor_tensor(out=sm[:, 22:23], in0=sm[:, 21:22], in1=lamv[:, 5:6], op=ALU.mult),
                            ["smr", "nlam"], ["smc1"])
                          yield
                          V(lambda e: e.tensor_scalar(out=scr0[:, 0:NKC], in0=scr0[:, 0:NKC], scalar1=sm[:, 20:21],
                                                      scalar2=None, op0=ALU.mult), ["scr0", "smr"], ["scr0"])
                          yield
                          V(lambda e: e.scalar_tensor_tensor(out=w_bf[:, 0:NKC], in0=scr1[:, 0:NKC], scalar=sm[:, 22:23],
                                                             in1=scr0[:, 0:NKC], op0=ALU.mult, op1=ALU.add),
                            ["scr0", "scr1o", "smc1"], ["w_bf"])
                          yield
                          yield from pv_gen(cblocks, w_bf, "w_bf", wT, "wT",
                                            lambda jb, hh=hh: vc_all[:, jb, hh * 64:(hh + 1) * 64], "vc_all",
                                            psC[:, hh * 64:(hh + 1) * 64], "psC")
                      oc = scr0[:, 0:256]
                      V(lambda e: e.tensor_copy(out=oc, in_=psC[:, 0:256]), ["psC"], ["scr0"])
                      V(lambda e: e.tensor_tensor(out=scr0[:, 256:512], in0=oc, in1=oc, op=ALU.mult), ["scr0"], ["scr0"])
                      V(lambda e: e.reduce_sum(out=sm[:, 24:28], in_=scr0[:, 256:512].rearrange("p (h d) -> p h d", d=64),
                                               axis=AX.X), ["scr0"], ["smq"])
                      V(lambda e: e.tensor_scalar(out=sm[:, 24:28], in0=sm[:, 24:28], scalar1=1.0 / 64.0, scalar2=LN_EPS,
                                                  op0=ALU.mult, op1=ALU.add), ["smq"], ["smq"])
                      yield
                      ACT(lambda e: e.activation(out=sm[:, 24:28], in_=sm[:, 24:28], func=AF.Sqrt), ["smq"], ["smq"])
                      V(lambda e: e.reciprocal(out=sm[:, 24:28], in_=sm[:, 24:28]), ["smq"], ["smq"])
                      V(lambda e: e.tensor_tensor(out=scr0[:, 0:256].rearrange("p (h d) -> p h d", d=64),
                                                  in0=scr0[:, 0:256].rearrange("p (h d) -> p h d", d=64),
                                                  in1=sm[:, 24:28].unsqueeze(2).to_broadcast([128, 4, 64]), op=ALU.mult),
                        ["scr0", "smq"], ["scr0"])
                      V(lambda e: e.tensor_tensor(out=o_cat[:, 512:768].rearrange("p (h d) -> p h d", d=64),
                                                  in0=scr0[:, 0:256].rearrange("p (h d) -> p h d", d=64),
                                                  in1=subg[:].unsqueeze(1).to_broadcast([128, 4, 64]), op=ALU.mult),
                        ["scr0", "subg"], ["o_cat"])
                      yield

                  chains = [chainC(), chainB()] if CFG.get('ilv', 1) else []
                  if not CFG.get('ilv', 1):
                      for _ in chainC():
                          pass
                      for _ in chainB():
                          pass
                  while chains:
                      for g_ in list(chains):
                          try:
                              next(g_)
                          except StopIteration:
                              chains.remove(g_)
                  if l == 0 and t == 0:
                      dbg("ocat", o_cat, ["o_cat"])

                  transpose8(o_cat, "o_cat", hT, "hT")

                  def mmO(e):
                      ins = None
                      for n_ in range(2):
                          for k in range(8):
                              ins = e.matmul(psA[:, n_ * 512:(n_ + 1) * 512], lhsT=hT[:, k, :],
                                             rhs=w_out_sb[:, k, n_ * 512:(n_ + 1) * 512], start=(k == 0), stop=(k == 7))
                      return ins
                  PE(mmO, ["hT", "w_out"], ["psA0", "psA1"])
                  post_ln(t, psA[:, 0:512], "psA0", psA[:, 512:1024], "psA1", which, scr0, "scr0")
          if l == 0:
              dbg("x1", x[:, 0, :], ["x0"])

          ck("mix")
          barrier()
          for k in range(0, 8, 2):
              DMA("pool", wq_sb[:, k:k + 2, :], d_wq[l][k * 128:(k + 2) * 128, :].rearrange("(k p) n -> p k n", p=128),
                  w=["wq"])
          DMA("sp", lnp[:, 0, :], bcast_row(h_lng, (l * 2 + 1) * D, D), w=["lnp"])
          DMA("sp", lnp[:, 1, :], bcast_row(h_lnb, (l * 2 + 1) * D, D), w=["lnp"])
          for i in range(2):
              V(lambda e, i=i: e.memset(bigf[i], 0.0), [], [f"bg{i}_{q_}" for q_ in range(64)])
          mod_half(l, 1, p_wmb, "wmb", p_bmb, "bmb")

          def stage_A(i):
              t = i
              which = 0 if t < 4 else 1
              hb = h2[i % 2]
              hkey = f"h2_{i % 2}"
              slot = i % 2
              ln_mod(t, which, 0, 1024, hb, hkey, e_scr0, "scr0")
              yield
              transpose8(hb, hkey, h2T, "h2T")
              yield
              for half in range(2):
                  def mq(e, half=half):
                      ins = None
                      for c in range(4):
                          cc = half * 4 + c
                          for k in range(8):
                              ins = e.matmul(psA[:, c * 128:(c + 1) * 128], lhsT=wq_sb[:, k, cc * 128:(cc + 1) * 128],
                                             rhs=h2T[:, k, :], start=(k == 0), stop=(k == 7))
                      return ins
                  PE(mq, ["h2T", "wq"], ["psA0"])
                  ACT(lambda e, half=half: e.copy(out=qT_sb[:, half * 4:half * 4 + 4, :].rearrange("p c t -> p (c t)"),
                                                  in_=psA[:, 0:512]), ["psA0"], ["qT_sb"])
                  yield
              V(lambda e: e.memset(idxf, 0.0), [], ["idxf"])
              for hq in range(4):
                  def ms(e, hq=hq):
                      ins = None
                      for hh in range(2):
                          ins = e.matmul(psA[:, 512 + hh * 256:512 + (hh + 1) * 256], lhsT=qT_sb[:, 2 * hq + hh, :],
                                         rhs=kb_b[:], start=True, stop=True)
                      return ins
                  PE(ms, ["qT_sb", "kb_b"], ["psA1"])
                  ACT(lambda e: e.copy(out=sc_sb, in_=psA[:, 512:1024]), ["psA1"], ["sc_sb"])
                  yield
                  for g in range(4):
                      scg = sc_sb[:, g * 128:(g + 1) * 128]
                      V(lambda e, g=g, scg=scg: e.max(out=sv[:, g, 0:8], in_=scg), ["sc_sb"], ["sv"])
                      V(lambda e, g=g, scg=scg: e.max_index(out=si_u[:, g, 0:8], in_max=sv[:, g, 0:8], in_values=scg),
                        ["sc_sb", "sv"], ["si_u"])
                      V(lambda e, g=g, scg=scg: e.match_replace(out=scr_mr[:, 0:128], in_to_replace=sv[:, g, 0:8],
                                                                in_values=scg, imm_value=-1e30), ["sc_sb", "sv"], ["scr_mr"])
                      V(lambda e, g=g: e.max(out=sv[:, g, 8:16], in_=scr_mr[:, 0:128]), ["scr_mr"], ["sv"])
                      V(lambda e, g=g: e.max_index(out=si_u[:, g, 8:16], in_max=sv[:, g, 8:16], in_values=scr_mr[:, 0:128]),
                        ["scr_mr", "sv"], ["si_u"])
                      yield
                  V(lambda e: e.tensor_copy(out=sif, in_=si_u[:]), ["si_u"], ["sif"])
                  for hh in range(2):
                      c3 = cand[:, hh, :].rearrange("p (a b) -> p a b", b=16)
                      e3 = eid[:, hh, :].rearrange("p (a b) -> p a b", b=16)
                      V(lambda e, hh=hh, c3=c3: e.tensor_tensor(
                          out=c3, in0=sv[:, 2 * hh, :].unsqueeze(2).to_broadcast([128, 16, 16]),
                          in1=sv[:, 2 * hh + 1, :].unsqueeze(1).to_broadcast([128, 16, 16]), op=ALU.add),
                        ["sv"], ["cand"])
                      V(lambda e, hh=hh, e3=e3: e.scalar_tensor_tensor(
                          out=e3, in0=sif[:, 2 * hh, :].unsqueeze(2).to_broadcast([128, 16, 16]), scalar=128.0,
                          in1=sif[:, 2 * hh + 1, :].unsqueeze(1).to_broadcast([128, 16, 16]),
                          op0=ALU.mult, op1=ALU.add), ["sif"], ["eid"])
                      yield
                  for hh in range(2):
                      H = 2 * hq + hh
                      V(lambda e, hh=hh, H=H: e.max(out=cvv[:, H, 0:8], in_=cand[:, hh, :]), ["cand"], ["cvv"])
                      V(lambda e, hh=hh, H=H: e.match_replace(out=scr_mr[:, 0:256], in_to_replace=cvv[:, H, 0:8],
                                                              in_values=cand[:, hh, :], imm_value=-1e30),
                        ["cand", "cvv"], ["scr_mr"])
                      V(lambda e, H=H: e.max(out=cvv[:, H, 8:16], in_=scr_mr[:, 0:256]), ["scr_mr"], ["cvv"])
                      for k in range(16):
                          V(lambda e, hh=hh, H=H, k=k: e.scalar_tensor_tensor(
                              out=e_scr1[:, 0:256], in0=cand[:, hh, :], scalar=cvv[:, H, k:k + 1], in1=eid[:, hh, :],
                              op0=ALU.is_equal, op1=ALU.mult, accum_out=idxf[:, H * 16 + k:H * 16 + k + 1]),
                            ["cand", "eid", "cvv"], ["escr1", "idxf"])
                          if k % 4 == 3:
                              yield
              yield
              V(lambda e: e.tensor_tensor(out=gat, in0=cvv, in1=cvv[:, :, 0:1].to_broadcast([128, 8, 16]), op=ALU.subtract),
                ["cvv"], ["gat"])
              ACT(lambda e: e.activation(out=gat, in_=gat, func=AF.Exp), ["gat"], ["gat"])
              V(lambda e: e.reduce_sum(out=sm[:, 8:16], in_=gat, axis=AX.X), ["gat"], ["smg"])
              V(lambda e: e.reciprocal(out=sm[:, 8:16], in_=sm[:, 8:16]), ["smg"], ["smg"])
              V(lambda e: e.tensor_tensor(out=gat, in0=gat, in1=sm[:, 8:16].unsqueeze(2).to_broadcast([128, 8, 16]),
                                          op=ALU.mult), ["gat", "smg"], ["gat"])
              V(lambda e: e.tensor_scalar(out=idxf, in0=idxf, scalar1=float(NEXP - 1), scalar2=0.0, op0=ALU.min, op1=ALU.max),
                ["idxf"], ["idxf"])
              V(lambda e, l=l: e.tensor_scalar(out=idxf, in0=idxf, scalar1=float(l * NEXP + CFG.get("oob_add", 0)), scalar2=None, op0=ALU.add),
                ["idxf"], ["idxf"])

              def trI(e):
                  e.transpose(out=psA[:, 1536:1664], in_=idxf, identity=ident_f[:])
                  return e.transpose(out=psA[:, 1664:1792], in_=gat.rearrange("p h k -> p (h k)"), identity=ident_f[:])
              PE(trI, ["idxf", "gat", "ident_f"], ["psA3"])
              V(lambda e: e.tensor_copy(out=idx_i[:, slot, :], in_=psA[:, 1536:1664]), ["psA3"], [f"idx{slot}"])
              V(lambda e: e.tensor_copy(out=gT[slot], in_=psA[:, 1664:1792]), ["psA3"], [f"gT{slot}"])

          gcount = [0]
          ac = aT[0].rearrange("p a t -> p (a t)")[:, 0:32].rearrange("p (j c) -> p j c", c=4)
          bigdiag = [bigf[k_].rearrange("p (a b) -> p a b", b=65)[:, :, 0] for k_ in range(2)]

          tokst = {}

          def stage_S1(i, tt, tab=d_puv):
              slot = i % 2
              hb, hkey = h2[i % 2], f"h2_{i % 2}"
              n_ = gcount[0]
              gcount[0] += 1
              b = n_ % NGB
              j = n_ % 8
              tokst[(i, tt)] = (b, j)
              S.dma("pool", lambda e: e.indirect_dma_start(
                  out=gbuf[b], out_offset=None, in_=tab,
                  in_offset=bass.IndirectOffsetOnAxis(ap=idx_i[:, slot, tt:tt + 1], axis=0),
                  bounds_check=REGS["bc"], oob_is_err=False), [f"idx{slot}"], [f"gb{b}"], SEM_U[b])
              for hf in range(2):
                  PE(lambda e, hf=hf: e.matmul(psB[:, hf * 512:(hf + 1) * 512],
                                               lhsT=ident_b[:, tt:tt + 1].to_broadcast([128, 128]),
                                               rhs=hb[:, hf * 512:(hf + 1) * 512], start=True, stop=True),
                     [hkey, "ident_b"], [f"psB{hf}"])
                  V(lambda e, hf=hf: e.scalar_tensor_tensor(
                      out=junk[:, hf * 512:(hf + 1) * 512], in0=gbuf[b][:, hf * 512:(hf + 1) * 512], scalar=1.0,
                      in1=psB[:, hf * 512:(hf + 1) * 512], op0=ALU.mult, op1=ALU.mult,
                      accum_out=ac[:, j, hf:hf + 1]), [f"gb{b}", f"psB{hf}"], [f"ac{j}_{hf}"])
              ACT(lambda e: e.activation(out=ac[:, j, 3:4], in_=ac[:, j, 0:1], func=AF.Gelu_apprx_tanh,
                                         bias=ac[:, j, 1:2]), [f"ac{j}_0", f"ac{j}_1"], [f"ac{j}"])
              sbk, t6 = tt // 64, tt % 64
              ACT(lambda e: e.activation(out=bigdiag[sbk][:, t6:t6 + 1], in_=ac[:, j, 3:4], func=AF.Copy,
                                         scale=gT[slot][:, tt:tt + 1]), [f"ac{j}", f"gT{slot}"], [f"bg{sbk}_{t6}"])

          def stage_S23(i, tt):
              slot = i % 2
              b, j = tokst.pop((i, tt))
              sbk, t6 = tt // 64, tt % 64
              bkey = f"bg{sbk}_{t6}"
              lhs = bigf[sbk][:, 0:4096].rearrange("p (t m) -> p t m", m=64)[:, t6, :]

              def mv(e):
                  e.matmul(psC[sbk * 64:(sbk + 1) * 64, :], lhsT=lhs, rhs=gbuf[b][:, 1024:1536],
                           start=(t6 == 0), stop=(t6 == 63))
                  return e.matmul(psA[sbk * 64:(sbk + 1) * 64, 1024:1536], lhsT=lhs, rhs=gbuf[b][:, 1536:2048],
                                  start=(t6 == 0), stop=(t6 == 63))
              PE(mv, [bkey, f"gb{b}"], ["psC", "psA2"])

          def stage_E(i):
              which = 0 if i < 4 else 1
              post_ln(i, psC[:, 0:512], "psC", psA[:, 1024:1536], "psA2", which, e_scr0, "scr0")

          ck("mod2")
          for _ in stage_A(0):
              pass
          ck("A")
          toks = [(i, tt) for i in range(NT) for tt in range(128)]
          stage_S1(*toks[0])
          genA = iter(())
          for n_t, (i, tt) in enumerate(toks):
              if tt == 0:
                  genA = stage_A(i + 1) if i + 1 < NT else iter(())
              if tt == 119:
                  for _ in genA:
                      pass
              if n_t + 1 < len(toks):
                  stage_S1(*toks[n_t + 1])
              stage_S23(i, tt)
              next(genA, None)
              if tt == 127:
                  stage_E(i)
          if l == 0:
              dbg("x2", x[:, 0, :], ["x0"])
          barrier()

    except _Stop:
        pass

    for t in range(NT):
        DMA("sp", o_y[t], x[:, t, :], r=[f"x{t}"])

    import contextlib
    sems = {}
    es = contextlib.ExitStack()
    for e_ in S.ENG:
        sems[("e", e_)] = es.enter_context(nc.semaphore(f"s_{e_}"))
    for i_ in range(len(S.dma_n)):
        sems[("d", i_)] = es.enter_context(nc.semaphore(f"d_{i_}"))
    with nc.Block() as block:

        def sem_of(key):
            return sems[key]

        def replay(name, eng):
            for (waits, fn, me, inc) in S.q[name]:
                for (s, v) in waits:
                    eng.wait_ge(sem_of(s), v)
                ins = fn(eng)
                ins.then_inc(sem_of(me[0]), inc)
            for (s, v) in S.final_waits(name):
                eng.wait_ge(sem_of(s), v)

        @block.tensor
        def _(e):
            replay("pe", e)

        @block.scalar
        def _(e):
            replay("act", e)

        @block.vector
        def _(e):
            replay("dve", e)

        @block.gpsimd
        def _(e):
            REGS["bc"] = e.alloc_register("bc")
            e.reg_mov(REGS["bc"], DEPTH * NEXP - 1)
            replay("pool", e)

        @block.sync
        def _(e):
            replay("sp", e)
    es.close()
    return nc


def _prep_inputs(inp):
    f = lambda a: np.ascontiguousarray(np.asarray(a, dtype=np.float32))
    x_prompt, x_sample, c = f(inp["x_prompt"]), f(inp["x_sample"]), f(inp["c"])
    c_ctx = f(inp["c_ctx"])
    cosb, sinb = _rope_tables(64)
    cosc, sinc = _rope_tables(32)
    shared = {
        "w_mod": f(inp["w_mod"]), "b_mod": f(inp["b_mod"]), "w_in": f(inp["w_in"]), "w_out": f(inp["w_out"]),
        "chunk_wT": f(np.transpose(f(inp["chunk_w"]), (0, 3, 1, 2))),
        "chunk_bT": f(np.transpose(f(inp["chunk_b"]), (0, 2, 1))),
        "win_sink": f(inp["win_sink"]),
        "lam_q": f(f(inp["diff_lam_q"]).reshape(DEPTH, 64)), "lam_k": f(f(inp["diff_lam_k"]).reshape(DEPTH, 64)),
        "subln_g": f(inp["diff_subln_g"]),
        "pool_wr": f(np.transpose(f(inp["pool_w"]), (0, 2, 1, 3))),
        "pool_scale": f(inp["pool_scale"]), "ln_g": f(inp["ln_g"]), "ln_b": f(inp["ln_b"]),
        "peer_wq": f(inp["peer_wq"]),
        "peer_uv": np.concatenate([f(inp["peer_u"]).reshape(DEPTH * NEXP, D), f(inp["peer_v"]).reshape(DEPTH * NEXP, D)], axis=1),
        "c_ident": np.eye(128, dtype=np.float32), "c_rb": _rot_matrix(64, 2), "c_rc": _rot_matrix(32, 2),
        "c_cosb": f(np.tile(cosb.T, (2, 1))), "c_sinb": f(np.tile(sinb.T, (2, 1))),
        "c_cosc": f(np.tile(cosc.T, (2, 1))), "c_sinc": f(np.tile(sinc.T, (2, 1))),
        "c_band": _band_bias(), "c_poolat": f(np.transpose(_pool_mats(), (2, 0, 1, 3))),
    }
    keys = f(inp["peer_keys"])
    kb = np.zeros((DEPTH, 128, 256), np.float32)
    for j in range(2):
        kb[:, j * 64:(j + 1) * 64, j * 128:(j + 1) * 128] = np.transpose(keys[:, j], (0, 2, 1))
    shared["peer_kb"] = kb
    cwk, cwv = f(inp["cache_win_k"]), f(inp["cache_win_v"])
    cdk, cdv = f(inp["cache_diff_k"]), f(inp["cache_diff_v"])
    maps = []
    for core in range(8):
        b = core // 4
        xin = np.concatenate([x_prompt[2 * core].reshape(2, 128, D), x_prompt[2 * core + 1].reshape(2, 128, D),
                              x_sample[b].reshape(8, 128, D)], 0)
        cvec = np.stack([c_ctx.reshape(8, 128).T, c[b].reshape(8, 128).T], 1)
        m = dict(shared)
        m["xin"] = f(xin)
        m["cvec"] = f(cvec)
        m["cwkT"] = f(np.transpose(cwk[b].reshape(DEPTH, 256, 128), (0, 2, 1)))
        m["cwv"] = f(cwv[b].reshape(DEPTH, 256, 128))
        m["cdkT"] = f(np.transpose(cdk[b].reshape(DEPTH, 256, 4, 64), (0, 3, 2, 1)))
        m["cdv"] = f(cdv[b].reshape(DEPTH, 256, 256))
        maps.append(m)
    return maps


_NC_CACHE = {}


def kernel(**inputs):
    maps = _prep_inputs(inputs)
    if "nc" not in _NC_CACHE:
        _NC_CACHE["nc"] = build_program()
    nc = _NC_CACHE["nc"]
    res = run_bass_kernel_spmd(nc, maps, core_ids=list(range(8)))
    R = res.results
    y_p = np.zeros((16, 256, D), np.float32)
    y_s = np.zeros((2, 1024, D), np.float32)
    nwk = np.zeros((16, DEPTH, 256, 2, 64), np.float32)
    nwv = np.zeros((16, DEPTH, 256, 2, 64), np.float32)
    ndk = np.zeros((16, DEPTH, 256, 4, 2, 32), np.float32)
    ndv = np.zeros((16, DEPTH, 256, 4, 64), np.float32)
    for core in range(8):
        y = np.asarray(R[core]["y"])
        for s in range(2):
            y_p[2 * core + s] = y[2 * s:2 * s + 2].reshape(256, D)
            nwk[2 * core + s] = np.asarray(R[core]["o_wk"])[s].reshape(DEPTH, 256, 2, 64)
            nwv[2 * core + s] = np.asarray(R[core]["o_wv"])[s].reshape(DEPTH, 256, 2, 64)
            ndk[2 * core + s] = np.asarray(R[core]["o_dk"])[s].reshape(DEPTH, 256, 4, 2, 32)
            ndv[2 * core + s] = np.asarray(R[core]["o_dv"])[s].reshape(DEPTH, 256, 4, 64)
        if core % 4 == 0:
            y_s[core // 4] = y[4:12].reshape(1024, D)
    kernel.last_results = R
    return (y_p, y_s, nwk, nwv, ndk, ndv)
```

```python
import math
import numpy as np
import concourse.bass as bass
import concourse.mybir as mybir
from concourse.bass_utils import run_bass_kernel_spmd

F32 = mybir.dt.float32
BF16 = mybir.dt.bfloat16
I32 = mybir.dt.int32
U32 = mybir.dt.uint32
AF = mybir.ActivationFunctionType
ALU = mybir.AluOpType
AX = mybir.AxisListType

DEPTH = 4
D = 1024
NT = 12
NEXP = 16384
ALPHA = (2 * DEPTH) ** 0.25
LN_EPS = 1e-5
NEG = -30000.0
NGB = 8
DEBUG = {}
CFG = {"depth": DEPTH, "stop": None, "nexp": NEXP}


class _Stop(Exception):
    pass


C_AU, C_AV, C_BQ, C_BK, C_BV, C_CQ, C_CK, C_CV, C_DZ = 0, 256, 512, 768, 896, 1024, 1280, 1536, 1792


class Sched:
    ENG = ("pe", "act", "dve", "pool", "sp")

    def __init__(self, n_dma_sems):
        self.q = {e: [] for e in self.ENG}
        self.cnt = {e: 0 for e in self.ENG}
        self.seen = {e: {} for e in self.ENG}
        self.last_w = {}
        self.readers = {}
        self.dma_n = [0] * n_dma_sems
        self.rr = 0
        self.n_general = n_dma_sems

    def _collect(self, eng, reads, writes, extra=(), raw=None):
        deps = {}
        own = ("e", eng)
        raw = set(reads if raw is None else raw)

        def add(d, same_ok):
            if d is None:
                return
            s, v = d
            if s == own and not same_ok:
                return
            if deps.get(s, 0) < v:
                deps[s] = v
        same_raw = eng != "pe"
        for k in reads:
            add(self.last_w.get(k), same_raw)
        for k in writes:
            add(self.last_w.get(k), same_raw and k in raw)
            for d in self.readers.get(k, {}).items():
                add(d, False)
        for d in extra:
            add(d, True)
        waits = []
        sn = self.seen[eng]
        for s, v in deps.items():
            if sn.get(s, 0) < v:
                sn[s] = v
                waits.append((s, v))
        return waits

    def _commit(self, me, reads, writes):
        for k in writes:
            self.last_w[k] = me
            self.readers[k] = {}
        for k in reads:
            r = self.readers.setdefault(k, {})
            if r.get(me[0], 0) < me[1]:
                r[me[0]] = me[1]

    @staticmethod
    def _norm(reads, writes):
        ps = [k for k in reads if k.startswith("ps")]
        if ps:
            reads = [k for k in reads if not k.startswith("ps")]
            writes = list(writes) + [k for k in ps if k not in writes]
        return reads, writes

    def op(self, eng, fn, reads=(), writes=(), extra=()):
        raw = list(reads)
        reads, writes = self._norm(reads, writes)
        waits = self._collect(eng, reads, writes, extra, raw)
        self.cnt[eng] += 1
        me = (("e", eng), self.cnt[eng])
        self.q[eng].append((waits, fn, me, 1))
        self._commit(me, reads, writes)

    def snapshot(self):
        snap = [(("e", e), self.cnt[e]) for e in self.ENG if self.cnt[e]]
        snap += [(("d", i), 16 * n) for i, n in enumerate(self.dma_n) if n]
        return snap

    def dma(self, eng, fn, reads=(), writes=(), sem=None):
        raw = list(reads)
        reads, writes = self._norm(reads, writes)
        if sem is None:
            sem = self.rr
            self.rr = (self.rr + 1) % self.n_general
        prev = (("d", sem), 16 * self.dma_n[sem]) if self.dma_n[sem] else None
        waits = self._collect(eng, reads, writes, extra=(prev,), raw=raw)
        self.dma_n[sem] += 1
        me = (("d", sem), 16 * self.dma_n[sem])
        self.q[eng].append((waits, fn, me, 16))
        self._commit(me, reads, writes)

    def final_waits(self, eng):
        waits = []
        for i, n in enumerate(self.dma_n):
            if n:
                waits.append((("d", i), 16 * n))
        for e in self.ENG:
            if e != eng and self.cnt[e]:
                waits.append((("e", e), self.cnt[e]))
        return waits


def _rope_tables(dim):
    n_tok, grid = 1024, 64
    rows = n_tok // grid
    row = np.repeat(np.arange(rows, dtype=np.float32), grid)
    col = np.tile(np.arange(grid, dtype=np.float32), rows)
    nf = dim // 4
    inv = (np.float32(10000.0) ** (-np.arange(nf, dtype=np.float32) / np.float32(nf))).astype(np.float32)
    ar = row[:, None] * inv
    ac = col[:, None] * inv
    ang = np.concatenate([ar, ar, ac, ac], axis=-1).astype(np.float32)
    return np.cos(ang).astype(np.float32), np.sin(ang).astype(np.float32)


def _rot_matrix(dim, reps):
    q = dim // 4
    r = np.zeros((dim, dim), np.float32)
    for a in range(q):
        r[q + a, a] = -1.0
        r[a, q + a] = 1.0
        r[3 * q + a, 2 * q + a] = -1.0
        r[2 * q + a, 3 * q + a] = 1.0
    out = np.zeros((dim * reps, dim * reps), np.float32)
    for i in range(reps):
        out[i * dim:(i + 1) * dim, i * dim:(i + 1) * dim] = r
    return out


def _pool_mats():
    S = 384
    out = np.zeros((5, 4, 128, 128), np.float32)
    for g, w in enumerate((2, 4, 8, 16)):
        def mat(seq_lo, seq_hi):
            m = np.zeros((S, S), np.float32)
            for t in range(128, 256):
                lo = min(max(t - w // 2, seq_lo), seq_hi)
                hi = min(max(t + w // 2, seq_lo), seq_hi)
                m[t, lo:hi] = 1.0 / float(hi - lo)
                m[t, t] -= 1.0
            return m
        mid = mat(0, S)
        first = mat(128, S)
        last = mat(0, 256)
        out[0, g] = first[128:256, 128:256].T
        out[1, g] = mid[128:256, 128:256].T
        out[2, g] = last[128:256, 128:256].T
        out[3, g] = mid[128:256, 0:128].T
        out[4, g] = mid[128:256, 256:384].T
    return out


def _band_bias():
    qi = np.arange(128)[:, None]
    ki = np.arange(128)[None, :]
    prev = np.where(ki >= qi, 0.0, NEG).astype(np.float32)
    nxt = np.where(ki <= qi, 0.0, NEG).astype(np.float32)
    z = np.zeros((128, 128), np.float32)
    full = np.full((128, 128), NEG, np.float32)
    return np.stack([np.concatenate([full, z, nxt], 1), np.concatenate([prev, z, nxt], 1),
                     np.concatenate([prev, z, full], 1)], 0)


def build_program():
    nc = bass.Bass("TRN2", target_bir_lowering=False)

    def din(name, shape, dt=F32):
        return nc.dram_tensor(name, list(shape), dt, kind="ExternalInput")

    def dout(name, shape, dt=F32):
        return nc.dram_tensor(name, list(shape), dt, kind="ExternalOutput")

    d_x = din("xin", [NT, 128, D]).ap()
    d_cvec = din("cvec", [128, 2, 8]).ap()
    d_cwkT = din("cwkT", [DEPTH, 128, 256]).ap()
    d_cwv = din("cwv", [DEPTH, 256, 128]).ap()
    d_cdkT = din("cdkT", [DEPTH, 64, 4, 256]).ap()
    d_cdv = din("cdv", [DEPTH, 256, 256]).ap()
    d_wmod = din("w_mod", [CFG["depth"], D, 6 * D]).ap()
    h_bmod = din("b_mod", [DEPTH, 6 * D])
    d_win = din("w_in", [CFG["depth"], D, 2048]).ap()
    d_wout = din("w_out", [CFG["depth"], D, D]).ap()
    d_cwT = din("chunk_wT", [DEPTH, 128, 4, 128]).ap()
    d_cbT = din("chunk_bT", [DEPTH, 128, 4]).ap()
    h_sink = din("win_sink", [DEPTH, 4])
    h_lamq = din("lam_q", [DEPTH, 64])
    h_lamk = din("lam_k", [DEPTH, 64])
    h_subg = din("subln_g", [DEPTH, 64])
    d_poolw = din("pool_wr", [DEPTH, 64, 4, 64]).ap()
    h_pscale = din("pool_scale", [DEPTH, 256])
    h_lng = din("ln_g", [DEPTH, 2, D])
    h_lnb = din("ln_b", [DEPTH, 2, D])
    d_wq = din("peer_wq", [CFG["depth"], D, D]).ap()
    d_kb = din("peer_kb", [DEPTH, 128, 256]).ap()
    d_puv = din("peer_uv", [CFG["depth"] * CFG["nexp"], 2 * D]).ap()
    d_ident = din("c_ident", [128, 128]).ap()
    d_rb = din("c_rb", [128, 128]).ap()
    d_rc = din("c_rc", [64, 64]).ap()
    d_cosb = din("c_cosb", [128, 1024]).ap()
    d_sinb = din("c_sinb", [128, 1024]).ap()
    d_cosc = din("c_cosc", [64, 1024]).ap()
    d_sinc = din("c_sinc", [64, 1024]).ap()
    d_band = din("c_band", [3, 128, 384]).ap()
    d_poolat = din("c_poolat", [128, 5, 4, 128]).ap()

    d_tab16 = nc.dram_tensor("tab16", [NEXP, 2 * D], BF16, kind="Internal").ap()
    o_y = dout("y", [NT, 128, D]).ap()
    o_wk = dout("o_wk", [2, DEPTH, 256, 128]).ap()
    o_wv = dout("o_wv", [2, DEPTH, 256, 128]).ap()
    o_dk = dout("o_dk", [2, DEPTH, 256, 256]).ap()
    o_dv = dout("o_dv", [2, DEPTH, 256, 256]).ap()
    dbg_out = {}
    for name, shp in DEBUG.items():
        dbg_out[name] = dout("dbg_" + name, shp).ap()

    def bcast_row(h, off, n, parts=128):
        return bass.AP(h, off, [[0, parts], [1, n]])

    def sb(name, shape, dt):
        return nc.alloc_sbuf_tensor(name, list(shape), dt)

    x = sb("x", [128, NT, D], F32)
    mod = sb("mod", [128, 2, 3072], F32)
    reg1 = sb("reg1", [128, 24576], BF16)
    A16N, A32N = 21760, 6784
    a16 = sb("a16", [128, A16N], BF16)
    a32 = sb("a32", [128, A32N], F32)
    idx_i = sb("idx_i", [128, 3, 128], I32)
    si_u = sb("si_u", [128, 4, 16], U32)
    ident_f = sb("ident_f", [128, 128], F32)
    ident_b = sb("ident_b", [128, 128], BF16)
    rb_f = sb("rb_f", [128, 128], F32)
    rc_f = sb("rc_f", [64, 64], F32)
    band = sb("band", [128, 3, 384], BF16)
    poolat = sb("poolat", [128, 5, 4, 128], BF16)
    silu_b = sb("silu_b", [128, 2, 8], BF16)
    cvec = sb("cvec_sb", [128, 2, 8], F32)
    cwT = sb("cwT", [128, 4, 128], BF16)
    cbT = sb("cbT", [128, 4], F32)
    sink = sb("sink", [128, 4], F32)
    lamt = sb("lamt", [128, 2, 64], F32)
    lamv = sb("lamv", [128, 8], F32)
    subg = sb("subg", [128, 64], F32)
    poolw = sb("poolw", [64, 4, 64], BF16)
    pscale = sb("pscale", [128, 256], F32)
    kb_b = sb("kb_b", [128, 256], BF16)
    st6 = sb("st6", [128, 2, 6], F32)
    sm = sb("sm", [128, 32], F32)
    scrB = sb("scrB", [128, 640], F32)

    psA = nc.alloc_psum_tensor("psA", [128, 2048], F32)
    psB = nc.alloc_psum_tensor("psB", [128, 1024], F32)
    psC = nc.alloc_psum_tensor("psC", [128, 512], F32)
    psT = nc.alloc_psum_tensor("psT", [128, 1024], BF16)

    class Carver:
        def __init__(self, t, n):
            self.t, self.n, self.o = t, n, 0

        def take(self, size, pat=None, parts=128, **kw):
            assert self.o + size <= self.n, (self.o, size, self.n)
            ap = self.t[0:parts, self.o:self.o + size]
            self.o += size
            if pat:
                ap = ap.rearrange(pat, **kw)
            return ap

    w_in_sb = reg1[:, 0:16384].rearrange("p (k n) -> p k n", n=2048)
    w_out_sb = reg1[:, 16384:24576].rearrange("p (k n) -> p k n", n=1024)
    wq_sb = reg1[:, 0:8192].rearrange("p (k n) -> p k n", n=1024)
    gbuf = [reg1[:, 8192 + i * 2048: 8192 + (i + 1) * 2048] for i in range(NGB)]

    c16 = Carver(a16, A16N)
    h_bf = c16.take(1024)
    hT = c16.take(1024, "p (k t) -> p k t", t=128)
    z_all = c16.take(2048, "p (b c) -> p b c", c=256)
    kTb_all = c16.take(1280)
    vb_all = c16.take(1280, "p (b c) -> p b c", c=128)
    kTc_all = c16.take(5120, "p (h s) -> p h s", parts=64, s=1280)
    vc_all = c16.take(2560, "p (b c) -> p b c", c=256)
    ua = c16.take(512)
    qTb = c16.take(256, "p (c t) -> p c t", t=128)
    qTc = c16.take(512, "p (c t) -> p c t", parts=64, t=128)
    w_bf = c16.take(1280)
    wT = c16.take(1280, "p (b q) -> p b q", q=128)
    o_cat = c16.take(1024)
    pooledT = c16.take(512, "p (g t) -> p g t", parts=64, t=128)
    wmb = [c16.take(1024, "p (k n) -> p k n", n=128) for _ in range(2)]
    c32 = Carver(a32, A32N)
    scr0 = c32.take(1280)
    scr1 = c32.take(1408)
    csB = c32.take(256, "p (a t) -> p a t", t=128)
    csC = c32.take(256, "p (a t) -> p a t", parts=64, t=128)
    bmb = [c32.take(128) for _ in range(2)]
    assert c32.o == 3456
    lnp = c32.take(2048, "p (a n) -> p a n", n=1024)
    p16 = Carver(a16, A16N)
    h2 = [p16.take(1024) for _ in range(2)]
    h2T = p16.take(1024, "p (k t) -> p k t", t=128)
    qT_sb = p16.take(1024, "p (c t) -> p c t", t=128)
    junk = p16.take(1024)
    w_pb = [p16.take(128) for _ in range(2)]
    BIGN = 64 * 65
    bigf = [p16.take(BIGN) for _ in range(2)]
    p_wmb = [p16.take(1024, "p (k n) -> p k n", n=128) for _ in range(2)]
    p32 = Carver(a32, A32N)
    e_scr0 = p32.take(1024)
    e_scr1 = p32.take(256)
    sc_sb = p32.take(512)
    scr_mr = p32.take(256)
    sv = p32.take(64, "p (g k) -> p g k", k=16)
    sif = p32.take(64, "p (g k) -> p g k", k=16)
    cand = p32.take(512, "p (h j) -> p h j", j=256)
    eid = p32.take(512, "p (h j) -> p h j", j=256)
    cvv = p32.take(128, "p (h k) -> p h k", k=16)
    gat = p32.take(128, "p (h k) -> p h k", k=16)
    assert p32.o == 3456
    p32.o = 5504
    idxf = p32.take(128)
    gT = [p32.take(128) for _ in range(3)]
    aT = [p32.take(256, "p (a t) -> p a t", t=128) for _ in range(2)]
    p_bmb = [p32.take(128) for _ in range(2)]

    n_general = 24
    REGS = {}
    S = Sched(n_general + 2 * NGB + 4)
    SEM_U = [n_general + i for i in range(NGB)]
    SEM_V = [n_general + NGB + i for i in range(NGB)]

    def V(fn, r=(), w=()):
        S.op("dve", fn, r, w)

    def ACT(fn, r=(), w=()):
        S.op("act", fn, r, w)

    def PE(fn, r=(), w=()):
        S.op("pe", fn, r, w)

    def POOL(fn, r=(), w=()):
        S.op("pool", fn, r, w)

    def DMA(eng, out, in_, r=(), w=(), sem=None):
        S.dma(eng, lambda e: e.dma_start(out=out, in_=in_), r, w, sem)

    def barrier():
        snap = S.snapshot()
        for e in S.ENG:
            S.op(e, lambda en: en.nop(), extra=snap)
        S.readers = {}
        S.last_w = {}

    DMA("sp", ident_f[:], d_ident, w=["ident_f"])
    DMA("sp", rb_f[:], d_rb, w=["rb_f"])
    DMA("sp", rc_f[:], d_rc, w=["rc_f"])
    DMA("pool", band[:], d_band.rearrange("v q k -> q v k"), w=["band"])
    DMA("sp", cvec[:], d_cvec, w=["cvec"])
    DMA("pool", ident_b[:], d_ident, w=["ident_b"])
    DMA("pool", poolat[:], d_poolat, w=["poolat"])
    for t in range(NT):
        DMA("sp", x[:, t, :], d_x[t], w=[f"x{t}"])
    ACT(lambda e: e.activation(out=silu_b[:], in_=cvec[:], func=AF.Silu), ["cvec"], ["silu_b"])
    for i in range(2):
        V(lambda e, i=i: e.memset(bigf[i], 0.0), [], [f"big{i}"])

    def ln_stats(src_ap, src_keys, tag):
        V(lambda e: e.bn_stats(out=st6[:, 0, :], in_=src_ap[:, 0:512]), src_keys, ["st6a"])
        V(lambda e: e.bn_stats(out=st6[:, 1, :], in_=src_ap[:, 512:1024]), src_keys, ["st6b"])
        V(lambda e: e.bn_aggr(out=sm[:, 0:2], in_=st6[:].rearrange("p a b -> p (a b)")), ["st6a", "st6b"], ["sm_mv"])
        V(lambda e: e.tensor_scalar(out=sm[:, 2:3], in0=sm[:, 1:2], scalar1=LN_EPS, scalar2=None, op0=ALU.add),
          ["sm_mv"], ["sm_rstd"])
        ACT(lambda e: e.activation(out=sm[:, 2:3], in_=sm[:, 2:3], func=AF.Sqrt), ["sm_rstd"], ["sm_rstd"])
        V(lambda e: e.reciprocal(out=sm[:, 2:3], in_=sm[:, 2:3]), ["sm_rstd"], ["sm_rstd"])
        V(lambda e: e.scalar_tensor_tensor(out=sm[:, 3:4], in0=sm[:, 0:1], scalar=-1.0, in1=sm[:, 2:3],
                                           op0=ALU.mult, op1=ALU.mult), ["sm_mv", "sm_rstd"], ["sm_nmr"])
        return sm[:, 2:3], sm[:, 3:4]

    def ln_mod(t, which, sh_off, sc_off, dst_bf, dst_key, tmp, tmp_key):
        rstd, nmr = ln_stats(x[:, t, :], [f"x{t}"], "a")
        ACT(lambda e: e.activation(out=tmp[:, 0:1024], in_=x[:, t, :], func=AF.Identity, bias=nmr, scale=rstd),
            [f"x{t}", "sm_rstd", "sm_nmr"], [tmp_key])
        V(lambda e: e.tensor_tensor(out=tmp[:, 0:1024], in0=tmp[:, 0:1024], in1=mod[:, which, sc_off:sc_off + 1024],
                                    op=ALU.mult), [tmp_key, "mod"], [tmp_key])
        V(lambda e: e.tensor_tensor(out=dst_bf, in0=tmp[:, 0:1024], in1=mod[:, which, sh_off:sh_off + 1024],
                                    op=ALU.add), [tmp_key, "mod"], [dst_key])

    def transpose8(src_bf, src_key, dst, dst_key):
        def tr(e):
            ins = None
            for k in range(8):
                ins = e.transpose(out=psT[:, k * 128:(k + 1) * 128], in_=src_bf[:, k * 128:(k + 1) * 128],
                                  identity=ident_b[:])
            return ins
        PE(tr, [src_key, "ident_b"], ["psT"])
        ACT(lambda e: e.copy(out=dst.rearrange("p k t -> p (k t)"), in_=psT[:, :]), ["psT"], [dst_key])

    def mod_half(l, half, wb, wb_key, bb, bb_key):
        for j in range(24):
            c0 = half * 3072 + j * 128
            b = j % 2
            DMA("pool", wb[b], d_wmod[l][:, c0:c0 + 128].rearrange("(k p) n -> p k n", p=128), w=[f"{wb_key}{b}"])
            DMA("sp", bb[b], bcast_row(h_bmod, l * 6144 + c0, 128), w=[f"{bb_key}{b}"])
            for which in range(2):
                def mm(e, which=which, b=b):
                    ins = None
                    for k in range(8):
                        ins = e.matmul(psA[:, which * 512: which * 512 + 128],
                                       lhsT=silu_b[:, which, k:k + 1].to_broadcast([128, 128]),
                                       rhs=wb[b][:, k, :], start=(k == 0), stop=(k == 7))
                    return ins
                PE(mm, ["silu_b", f"{wb_key}{b}"], [f"psA{which}"])
                addc = 1.0 if 8 <= j < 16 else 0.0
                V(lambda e, which=which, b=b, j=j, addc=addc: e.scalar_tensor_tensor(
                    out=mod[:, which, j * 128:(j + 1) * 128], in0=psA[:, which * 512: which * 512 + 128],
                    scalar=addc, in1=bb[b], op0=ALU.add, op1=ALU.add),
                  [f"psA{which}", f"{bb_key}{b}"], ["mod"])

    def dbg(name, ap, keys):
        if name in dbg_out:
            DMA("pool", dbg_out[name], ap, r=keys)

    def attn_pv(blocks, vfn, vkey, out_ps, out_key, first_group_start=True):
        n = len(blocks)
        for r0 in range(0, n, 8):
            grp = list(range(r0, min(n, r0 + 8)))

            def tr(e, grp=grp, r0=r0):
                ins = None
                for n_ in grp:
                    ins = e.transpose(out=psT[:, (n_ - r0) * 128:(n_ - r0 + 1) * 128],
                                      in_=w_bf[:, n_ * 128:(n_ + 1) * 128], identity=ident_b[:])
                return ins
            PE(tr, ["w_bf", "ident_b"], ["psT"])
            ACT(lambda e, grp=grp, r0=r0: e.copy(out=wT[:, r0:r0 + len(grp), :].rearrange("p b q -> p (b q)"),
                                                 in_=psT[:, 0:len(grp) * 128]), ["psT"], ["wT"])

        def pv(e):
            ins = None
            for n_, jb in enumerate(blocks):
                ins = e.matmul(out_ps, lhsT=wT[:, n_, :], rhs=vfn(jb), start=(n_ == 0), stop=(n_ == n - 1))
            return ins
        PE(pv, ["wT", vkey], [out_key])

    def post_ln(t, src0, src0_key, src1, src1_key, which, tmp, tmp_key):
        V(lambda e: e.tensor_tensor(out=tmp[:, 0:512], in0=src0, in1=mod[:, which, 2048:2560], op=ALU.mult),
          [src0_key, "mod"], [tmp_key])
        V(lambda e: e.tensor_tensor(out=tmp[:, 512:1024], in0=src1, in1=mod[:, which, 2560:3072], op=ALU.mult),
          [src1_key, "mod"], [tmp_key])
        V(lambda e: e.scalar_tensor_tensor(out=tmp[:, 0:1024], in0=x[:, t, :], scalar=ALPHA, in1=tmp[:, 0:1024],
                                           op0=ALU.mult, op1=ALU.add), [f"x{t}", tmp_key], [tmp_key])
        rstd, nmr = ln_stats(tmp, [tmp_key], "p")
        ACT(lambda e: e.activation(out=tmp[:, 0:1024], in_=tmp[:, 0:1024], func=AF.Identity, bias=nmr, scale=rstd),
            [tmp_key, "sm_rstd", "sm_nmr"], [tmp_key])
        V(lambda e: e.tensor_tensor(out=tmp[:, 0:1024], in0=tmp[:, 0:1024], in1=lnp[:, 0, :], op=ALU.mult),
          [tmp_key, "lnp"], [tmp_key])
        V(lambda e: e.tensor_tensor(out=x[:, t, :], in0=tmp[:, 0:1024], in1=lnp[:, 1, :], op=ALU.add),
          [tmp_key, "lnp"], [f"x{t}"])

    def ck(name):
        if CFG["stop"] == name:
            raise _Stop()

    try:
      ck("init")
      for l in range(CFG["depth"]):
          lam_init = 0.8 - 0.6 * math.exp(-0.3 * l)
          for k in range(8):
              DMA("pool", w_in_sb[:, k, :], d_win[l][k * 128:(k + 1) * 128, :], w=["w_in"])
          for k in range(0, 8, 2):
              DMA("pool", w_out_sb[:, k:k + 2, :],
                  d_wout[l][k * 128:(k + 2) * 128, :].rearrange("(k p) n -> p k n", p=128), w=["w_out"])
          DMA("pool", cwT[:], d_cwT[l], w=["cwT"])
          DMA("pool", poolw[:], d_poolw[l], w=["poolw"])
          DMA("pool", kb_b[:], d_kb[l], w=["kb_b"])
          DMA("sp", cbT[:], d_cbT[l], w=["cbT"])
          DMA("sp", sink[:], bcast_row(h_sink, l * 4, 4), w=["sink"])
          DMA("sp", lamt[:, 0, :], bcast_row(h_lamq, l * 64, 64), w=["lamt"])
          DMA("sp", lamt[:, 1, :], bcast_row(h_lamk, l * 64, 64), w=["lamt"])
          DMA("sp", subg[:], bcast_row(h_subg, l * 64, 64), w=["subg"])
          DMA("sp", pscale[:], bcast_row(h_pscale, l * 256, 256), w=["pscale"])
          DMA("sp", lnp[:, 0, :], bcast_row(h_lng, (l * 2 + 0) * D, D), w=["lnp"])
          DMA("sp", lnp[:, 1, :], bcast_row(h_lnb, (l * 2 + 0) * D, D), w=["lnp"])
          for j in range(2):
              V(lambda e, j=j: e.tensor_tensor(out=lamt[:, 0, j * 32:(j + 1) * 32], in0=lamt[:, 0, j * 32:(j + 1) * 32],
                                               in1=lamt[:, 1, j * 32:(j + 1) * 32], op=ALU.mult), ["lamt"], ["lamt"])
              V(lambda e, j=j: e.reduce_sum(out=lamv[:, j:j + 1], in_=lamt[:, 0, j * 32:(j + 1) * 32], axis=AX.X),
                ["lamt"], ["lamv"])
          ACT(lambda e: e.activation(out=lamv[:, 2:4], in_=lamv[:, 0:2], func=AF.Exp), ["lamv"], ["lamve"])
          V(lambda e, lam_init=lam_init: e.scalar_tensor_tensor(out=lamv[:, 5:6], in0=lamv[:, 3:4], scalar=-lam_init, in1=lamv[:, 2:3],
                                             op0=ALU.add, op1=ALU.subtract), ["lamve"], ["nlam"])
          V(lambda e, lam_init=lam_init: e.tensor_scalar(out=subg[:], in0=subg[:], scalar1=1.0 - lam_init, scalar2=None, op0=ALU.mult),
            ["subg"], ["subg"])
          mod_half(l, 0, wmb, "wmb", bmb, "bmb")
          NE_ = CFG["nexp"]
          for q_ in range(0, NE_, 1024):
              rr = min(1024, NE_ - q_)
              DMA("pool", d_tab16[q_:q_ + rr, :], d_puv[l * NE_ + q_:l * NE_ + q_ + rr, :], w=[f"tab16_{q_ // 1024}"])
          if l == 0:
              dbg("mod0", mod[:, 0, :], ["mod"])
          ck("mod")

          seqs = [(0, 2, False), (2, 2, False), (4, 8, True)]
          for (t0, nb, lat) in seqs:
              which = 1 if lat else 0
              nkb = nb + (2 if lat else 0)
              if lat:
                  DMA("pool", kTb_all[:, 1024:1280], d_cwkT[l], w=["kTb_all"])
                  DMA("pool", vb_all[:, 8:10, :], d_cwv[l].rearrange("(b p) c -> p b c", p=128), w=["vb_all"])
                  DMA("pool", kTc_all[:, :, 1024:1280], d_cdkT[l], w=["kTc_all"])
                  DMA("pool", vc_all[:, 8:10, :], d_cdv[l].rearrange("(b p) c -> p b c", p=128), w=["vc_all"])

              def rope(src_ps, src_key, n_ch, parts, tmp32, tmp_key, cs, cskey, rmat, rkey, dst, dst_key, lat=lat):
                  if not lat:
                      ACT(lambda e: e.copy(out=dst, in_=src_ps), [src_key], [dst_key])
                      return
                  V(lambda e: e.tensor_copy(out=tmp32, in_=src_ps), [src_key], [tmp_key])

                  def mm(e):
                      ins = None
                      for c in range(n_ch):
                          ins = e.matmul(src_ps[:, c, :], lhsT=rmat, rhs=tmp32[:, c, :], start=True, stop=True)
                      return ins
                  PE(mm, [tmp_key, rkey], [src_key])
                  V(lambda e: e.tensor_tensor(out=tmp32, in0=tmp32, in1=cs[:, 0:1, :].to_broadcast([parts, n_ch, 128]),
                                              op=ALU.mult), [tmp_key, cskey], [tmp_key])
                  V(lambda e: e.tensor_tensor(out=src_ps, in0=src_ps, in1=cs[:, 1:2, :].to_broadcast([parts, n_ch, 128]),
                                              op=ALU.mult), [src_key, cskey], [src_key])
                  V(lambda e: e.tensor_tensor(out=dst, in0=tmp32, in1=src_ps, op=ALU.add), [tmp_key, src_key], [dst_key])

              def load_cs(i):
                  tk = i * 128
                  DMA("sp", csB[:, 0, :], d_cosb[:, tk:tk + 128], w=["csB"])
                  DMA("sp", csB[:, 1, :], d_sinb[:, tk:tk + 128], w=["csB"])
                  DMA("sp", csC[:, 0, :], d_cosc[:, tk:tk + 128], w=["csC"])
                  DMA("sp", csC[:, 1, :], d_sinc[:, tk:tk + 128], w=["csC"])

              for i in range(nb):
                  t = t0 + i
                  ln_mod(t, which, 0, 1024, h_bf, "h_bf", scr0, "scr0")
                  ck("ln")
                  transpose8(h_bf, "h_bf", hT, "hT")
                  ck("tr")
                  if lat:
                      load_cs(i)

                  def tm(e):
                      ins = None
                      for (c_lo, n, po) in ((C_BK, 256, 0), (C_CK, 256, 512), (C_CV, 512, 1024)):
                          for k in range(8):
                              ins = e.matmul(psA[:, po:po + n], lhsT=hT[:, k, :], rhs=w_in_sb[:, k, c_lo:c_lo + n],
                                             start=(k == 0), stop=(k == 7))
                      return ins
                  PE(tm, ["hT", "w_in"], ["psA0", "psA1", "psA2"])

                  def fm(e):
                      ins = None
                      for k in range(8):
                          ins = e.matmul(psC[:, 0:128], lhsT=w_in_sb[:, k, C_BK:C_BK + 128], rhs=hT[:, k, :],
                                         start=(k == 0), stop=(k == 7))
                      for c in range(4):
                          for k in range(8):
                              ins = e.matmul(psB[0:64, c * 128:(c + 1) * 128],
                                             lhsT=w_in_sb[:, k, C_CK + c * 64:C_CK + (c + 1) * 64], rhs=hT[:, k, :],
                                             start=(k == 0), stop=(k == 7))
                      return ins
                  PE(fm, ["hT", "w_in"], ["psC", "psB0"])
                  ck("mm")
                  ACT(lambda e, i=i: e.copy(out=vb_all[:, i, :], in_=psA[:, 128:256]), ["psA0"], ["vb_all"])
                  ACT(lambda e, i=i: e.copy(out=vc_all[:, i, :], in_=psA[:, 1024:1280]), ["psA2"], ["vc_all"])
                  ACT(lambda e, i=i: e.copy(out=z_all[:, i, :], in_=psA[:, 1280:1536]), ["psA2"], ["z_all"])
                  if not lat:
                      sq = t0 // 2
                      V(lambda e: e.tensor_copy(out=scr1[:, 0:256], in_=psA[:, 0:256]), ["psA0"], ["scr1o"])
                      V(lambda e: e.tensor_copy(out=scr1[:, 256:512], in_=psA[:, 512:768]), ["psA1"], ["scr1o"])
                      V(lambda e: e.tensor_copy(out=scr1[:, 512:768], in_=psA[:, 1024:1280]), ["psA2"], ["scr1o"])
                      r0 = i * 128
                      DMA("sp", o_wk[sq, l, r0:r0 + 128, :], scr1[:, 0:128], r=["scr1o"])
                      DMA("sp", o_wv[sq, l, r0:r0 + 128, :], scr1[:, 128:256], r=["scr1o"])
                      DMA("sp", o_dk[sq, l, r0:r0 + 128, :], scr1[:, 256:512], r=["scr1o"])
                      DMA("sp", o_dv[sq, l, r0:r0 + 128, :], scr1[:, 512:768], r=["scr1o"])
                  ck("ev")
                  tk = i * 128
                  fmB32 = scr1[:, 768:896].rearrange("p (c t) -> p c t", t=128)
                  fmC32 = scr1[0:64, 896:1408].rearrange("p (c t) -> p c t", t=128)
                  rope(psC[:, 0:128].rearrange("p (c t) -> p c t", t=128), "psC", 1, 128, fmB32, "scr1o", csB, "csB",
                       rb_f[:], "rb_f", kTb_all[:, tk:tk + 128].rearrange("p (c t) -> p c t", t=128), "kTb_all")
                  rope(psB[0:64, 0:512].rearrange("p (c t) -> p c t", t=128), "psB0", 4, 64, fmC32, "scr1o", csC, "csC",
                       rc_f[:], "rc_f", kTc_all[:, :, tk:tk + 128], "kTc_all")
              if l == 0 and t0 == 0:
                  dbg("kTb", kTb_all[:, 0:256], ["kTb_all"])
                  dbg("z", z_all[:, 0:2, :], ["z_all"])
              ck("p1")

              for i in range(nb):
                  t = t0 + i
                  ln_mod(t, which, 0, 1024, h_bf, "h_bf", scr0, "scr0")
                  transpose8(h_bf, "h_bf", hT, "hT")
                  if lat:
                      load_cs(i)

                  def tmA(e):
                      ins = None
                      for k in range(8):
                          ins = e.matmul(psA[:, 0:512], lhsT=hT[:, k, :], rhs=w_in_sb[:, k, 0:512],
                                         start=(k == 0), stop=(k == 7))
                      return ins
                  PE(tmA, ["hT", "w_in"], ["psA0"])

                  def fmq(e):
                      ins = None
                      for g in range(2):
                          for kv in range(2):
                              c0 = C_BQ + (kv * 2 + g) * 64
                              for k in range(8):
                                  ins = e.matmul(psC[kv * 64:(kv + 1) * 64, g * 128:(g + 1) * 128],
                                                 lhsT=w_in_sb[:, k, c0:c0 + 64], rhs=hT[:, k, :],
                                                 start=(k == 0), stop=(k == 7))
                      for c in range(4):
                          for k in range(8):
                              ins = e.matmul(psB[0:64, c * 128:(c + 1) * 128],
                                             lhsT=w_in_sb[:, k, C_CQ + c * 64:C_CQ + (c + 1) * 64], rhs=hT[:, k, :],
                                             start=(k == 0), stop=(k == 7))
                      return ins
                  PE(fmq, ["hT", "w_in"], ["psC", "psB0"])
                  ACT(lambda e: e.activation(out=ua, in_=psA[:, 0:512], func=AF.Gelu_apprx_tanh), ["psA0"], ["ua"])
                  fmB32 = scr1[:, 768:1024].rearrange("p (c t) -> p c t", t=128)
                  fmC32 = scr1[0:64, 0:512].rearrange("p (c t) -> p c t", t=128)
                  rope(psC[:, 0:256].rearrange("p (c t) -> p c t", t=128), "psC", 2, 128, fmB32, "scr1o", csB, "csB",
                       rb_f[:], "rb_f", qTb, "qTb")
                  rope(psB[0:64, 0:512].rearrange("p (c t) -> p c t", t=128), "psB0", 4, 64, fmC32, "scr1o", csC, "csC",
                       rc_f[:], "rc_f", qTc, "qTc")

                  def mmA(e):
                      ins = None
                      for hh in range(4):
                          ins = e.matmul(psA[:, 1536 + hh * 64:1536 + (hh + 1) * 64], lhsT=cwT[:, hh, :],
                                         rhs=ua[:, 256 + hh * 64:256 + (hh + 1) * 64], start=True, stop=True)
                      return ins
                  PE(mmA, ["ua", "cwT"], ["psA3"])
                  for hh in range(4):
                      V(lambda e, hh=hh: e.scalar_tensor_tensor(
                          out=o_cat[:, hh * 64:(hh + 1) * 64], in0=psA[:, 1536 + hh * 64:1536 + (hh + 1) * 64],
                          scalar=cbT[:, hh:hh + 1], in1=ua[:, hh * 64:(hh + 1) * 64], op0=ALU.add, op1=ALU.mult),
                        ["psA3", "cbT", "ua"], ["o_cat"])

                  rels = []
                  if i > 0:
                      rels.append((i - 1, 3))
                  rels.append((i, 0 if i == 0 else (2 if i == nb - 1 else 1)))
                  if i < nb - 1:
                      rels.append((i + 1, 4))

                  def mmD(e, rels=rels):
                      ins = None
                      for g in range(4):
                          for n_, (j, v) in enumerate(rels):
                              ins = e.matmul(psB[0:64, 512 + g * 128:512 + (g + 1) * 128],
                                             lhsT=z_all[:, j, g * 64:(g + 1) * 64], rhs=poolat[:, v, g, :],
                                             start=(n_ == 0), stop=(n_ == len(rels) - 1))
                      return ins
                  PE(mmD, ["z_all", "poolat"], ["psB1"])
                  ACT(lambda e: e.copy(out=pooledT.rearrange("p g t -> p (g t)"), in_=psB[0:64, 512:1024]),
                      ["psB1"], ["pooledT"])

                  def mmD2(e):
                      ins = None
                      for g in range(4):
                          ins = e.matmul(psA[:, 1792 + g * 64:1792 + (g + 1) * 64], lhsT=pooledT[:, g, :],
                                         rhs=poolw[:, g, :], start=True, stop=True)
                      return ins
                  PE(mmD2, ["pooledT", "poolw"], ["psA3"])
                  V(lambda e: e.tensor_tensor(out=o_cat[:, 768:1024], in0=psA[:, 1792:2048], in1=pscale[:], op=ALU.mult),
                    ["psA3", "pscale"], ["o_cat"])

                  if lat:
                      kblocks = [min(max(j, 0), nb - 1) for j in (i - 1, i, i + 1)] + [8, 9]
                      bvar = 0 if i == 0 else (2 if i == nb - 1 else 1)
                      cblocks = list(range(10))
                  else:
                      kblocks = [0, 1]
                      bvar = 0
                      cblocks = [0, 1]
                  NKB = len(kblocks) * 128
                  NKC = len(cblocks) * 128
                  csc = 32.0 ** -0.5
                  pieces = [(p0, min(512, NKC - p0)) for p0 in range(0, NKC, 512)]
                  w_bfB = hT.rearrange("p k t -> p (k t)")
                  wTB = h_bf[:, 0:640].rearrange("p (b q) -> p b q", q=128)

                  def pv_gen(blocks, wsrc, wkey, wTd, wTkey, vfn, vkey, out_ps, out_key):
                      n = len(blocks)
                      for r0 in range(0, n, 8):
                          grp = list(range(r0, min(n, r0 + 8)))

                          def tr(e, grp=grp, r0=r0):
                              ins = None
                              for n_ in grp:
                                  ins = e.transpose(out=psT[:, (n_ - r0) * 128:(n_ - r0 + 1) * 128],
                                                    in_=wsrc[:, n_ * 128:(n_ + 1) * 128], identity=ident_b[:])
                              return ins
                          PE(tr, [wkey, "ident_b"], ["psT"])
                          ACT(lambda e, grp=grp, r0=r0: e.copy(
                              out=wTd[:, r0:r0 + len(grp), :].rearrange("p b q -> p (b q)"),
                              in_=psT[:, 0:len(grp) * 128]), ["psT"], [wTkey])
                          yield

                      def pv(e):
                          ins = None
                          for n_, jb in enumerate(blocks):
                              ins = e.matmul(out_ps, lhsT=wTd[:, n_, :], rhs=vfn(jb), start=(n_ == 0), stop=(n_ == n - 1))
                          return ins
                      PE(pv, [wTkey, vkey], [out_key])
                      yield

                  def chainB(kblocks=kblocks, bvar=bvar, NKB=NKB, lat=lat):
                      for hh in range(4):
                          kv, g = hh // 2, hh % 2

                          def mmS(e, kv=kv, g=g):
                              ins = None
                              for n_, jb in enumerate(kblocks):
                                  dst = psA[:, 1536 + n_ * 128:1536 + (n_ + 1) * 128] if n_ < 4 else psB[:, 512:640]
                                  ins = e.matmul(dst, lhsT=qTb[kv * 64:(kv + 1) * 64, g, :],
                                                 rhs=kTb_all[kv * 64:(kv + 1) * 64, jb * 128:(jb + 1) * 128],
                                                 start=True, stop=True)
                              return ins
                          PE(mmS, ["qTb", "kTb_all"], ["psA3", "psB1"] if lat else ["psA3"])
                          yield
                          if lat:
                              V(lambda e: e.scalar_tensor_tensor(out=scrB[:, 0:384], in0=psA[:, 1536:1920], scalar=0.125,
                                                                 in1=band[:, bvar, :], op0=ALU.mult, op1=ALU.add),
                                ["psA3", "band"], ["scrB"])
                              V(lambda e: e.tensor_scalar(out=scrB[:, 384:512], in0=psA[:, 1920:2048], scalar1=0.125,
                                                          scalar2=None, op0=ALU.mult), ["psA3"], ["scrB"])
                              V(lambda e: e.tensor_scalar(out=scrB[:, 512:640], in0=psB[:, 512:640], scalar1=0.125,
                                                          scalar2=None, op0=ALU.mult), ["psB1"], ["scrB"])
                          else:
                              V(lambda e: e.tensor_scalar(out=scrB[:, 0:256], in0=psA[:, 1536:1792], scalar1=0.125,
                                                          scalar2=None, op0=ALU.mult), ["psA3"], ["scrB"])
                          yield
                          V(lambda e: e.reduce_max(out=sm[:, 8:9], in_=scrB[:, 0:NKB], axis=AX.X), ["scrB"], ["sm8"])
                          V(lambda e, hh=hh: e.tensor_scalar(out=sm[:, 9:10], in0=sm[:, 8:9], scalar1=sink[:, hh:hh + 1],
                                                             scalar2=-1.0, op0=ALU.max, op1=ALU.mult), ["sm8", "sink"], ["sm9"])
                          yield
                          ACT(lambda e: e.activation(out=scrB[:, 0:NKB], in_=scrB[:, 0:NKB], func=AF.Exp,
                                                     bias=sm[:, 9:10], accum_out=sm[:, 10:11]),
                              ["scrB", "sm9"], ["scrB", "sm10"])
                          ACT(lambda e, hh=hh: e.activation(out=sm[:, 11:12], in_=sink[:, hh:hh + 1], func=AF.Exp,
                                                            bias=sm[:, 9:10]), ["sink", "sm9"], ["sm11"])
                          yield
                          V(lambda e: e.tensor_tensor(out=sm[:, 12:13], in0=sm[:, 10:11], in1=sm[:, 11:12], op=ALU.add),
                            ["sm10", "sm11"], ["sm12"])
                          V(lambda e: e.reciprocal(out=sm[:, 13:14], in_=sm[:, 12:13]), ["sm12"], ["sm13"])
                          V(lambda e: e.tensor_scalar(out=w_bfB[:, 0:NKB], in0=scrB[:, 0:NKB], scalar1=sm[:, 13:14],
                                                      scalar2=None, op0=ALU.mult), ["scrB", "sm13"], ["hT"])
                          yield
                          yield from pv_gen(kblocks, w_bfB, "hT", wTB, "h_bf",
                                            lambda jb, kv=kv: vb_all[:, jb, kv * 64:(kv + 1) * 64], "vb_all",
                                            psC[:, 256 + hh * 64:256 + (hh + 1) * 64], "psC")
                      V(lambda e: e.tensor_copy(out=o_cat[:, 256:512], in_=psC[:, 256:512]), ["psC"], ["o_cat"])
                      yield

                  def chainC(cblocks=cblocks, NKC=NKC, pieces=pieces):
                      for hh in range(4):
                          for j in range(2):
                              ej = scr0 if j == 0 else scr1

                              def mmC(e, hh=hh, j=j):
                                  ins = None
                                  for (p0, pn) in pieces:
                                      ins = e.matmul(psA[:, p0:p0 + pn], lhsT=qTc[j * 32:(j + 1) * 32, hh, :],
                                                     rhs=kTc_all[j * 32:(j + 1) * 32, hh, p0:p0 + pn], start=True, stop=True)
                                  return ins
                              pk = ["psA0", "psA1", "psA2"][:len(pieces)]
                              PE(mmC, ["qTc", "kTc_all"], pk)
                              yield
                              V(lambda e, j=j: e.reduce_max(out=sm[:, 14 + j:15 + j], in_=psA[:, 0:NKC], axis=AX.X),
                                pk, [f"smx{j}"])
                              V(lambda e, j=j: e.tensor_scalar(out=sm[:, 16 + j:17 + j], in0=sm[:, 14 + j:15 + j],
                                                               scalar1=-csc, scalar2=None, op0=ALU.mult),
                                [f"smx{j}"], [f"smn{j}"])
                              yield
                              ACT(lambda e, j=j, ej=ej: e.activation(out=ej[:, 0:NKC], in_=psA[:, 0:NKC], func=AF.Exp,
                                                                     bias=sm[:, 16 + j:17 + j], scale=csc,
                                                                     accum_out=sm[:, 18 + j:19 + j]),
                                  pk + [f"smn{j}"], ["scr0" if j == 0 else "scr1o", f"sms{j}"])
                              yield
                          V(lambda e: e.reciprocal(out=sm[:, 20:22], in_=sm[:, 18:20]), ["sms0", "sms1"], ["smr"])
                          V(lambda e: e.tensor_tensor(out=sm[:, 22:23], in0=sm[:, 21:22], in1=lamv[:, 5:6], op=ALU.mult),
                            ["smr", "nlam"], ["smc1"])
                          yield
                          V(lambda e: e.tensor_scalar(out=scr0[:, 0:NKC], in0=scr0[:, 0:NKC], scalar1=sm[:, 20:21],
                                                      scalar2=None, op0=ALU.mult), ["scr0", "smr"], ["scr0"])
                          yield
                          V(lambda e: e.scalar_tensor_tensor(out=w_bf[:, 0:NKC], in0=scr1[:, 0:NKC], scalar=sm[:, 22:23],
                                                             in1=scr0[:, 0:NKC], op0=ALU.mult, op1=ALU.add),
                            ["scr0", "scr1o", "smc1"], ["w_bf"])
                          yield
                          yield from pv_gen(cblocks, w_bf, "w_bf", wT, "wT",
                                            lambda jb, hh=hh: vc_all[:, jb, hh * 64:(hh + 1) * 64], "vc_all",
                                            psC[:, hh * 64:(hh + 1) * 64], "psC")
                      oc = scr0[:, 0:256]
                      V(lambda e: e.tensor_copy(out=oc, in_=psC[:, 0:256]), ["psC"], ["scr0"])
                      V(lambda e: e.tensor_tensor(out=scr0[:, 256:512], in0=oc, in1=oc, op=ALU.mult), ["scr0"], ["scr0"])
                      V(lambda e: e.reduce_sum(out=sm[:, 24:28], in_=scr0[:, 256:512].rearrange("p (h d) -> p h d", d=64),
                                               axis=AX.X), ["scr0"], ["smq"])
                      V(lambda e: e.tensor_scalar(out=sm[:, 24:28], in0=sm[:, 24:28], scalar1=1.0 / 64.0, scalar2=LN_EPS,
                                                  op0=ALU.mult, op1=ALU.add), ["smq"], ["smq"])
                      yield
                      ACT(lambda e: e.activation(out=sm[:, 24:28], in_=sm[:, 24:28], func=AF.Sqrt), ["smq"], ["smq"])
                      V(lambda e: e.reciprocal(out=sm[:, 24:28], in_=sm[:, 24:28]), ["smq"], ["smq"])
                      V(lambda e: e.tensor_tensor(out=scr0[:, 0:256].rearrange("p (h d) -> p h d", d=64),
                                                  in0=scr0[:, 0:256].rearrange("p (h d) -> p h d", d=64),
                                                  in1=sm[:, 24:28].unsqueeze(2).to_broadcast([128, 4, 64]), op=ALU.mult),
                        ["scr0", "smq"], ["scr0"])
                      V(lambda e: e.tensor_tensor(out=o_cat[:, 512:768].rearrange("p (h d) -> p h d", d=64),
                                                  in0=scr0[:, 0:256].rearrange("p (h d) -> p h d", d=64),
                                                  in1=subg[:].unsqueeze(1).to_broadcast([128, 4, 64]), op=ALU.mult),
                        ["scr0", "subg"], ["o_cat"])
                      yield

                  chains = [chainC(), chainB()] if CFG.get('ilv', 1) else []
                  if not CFG.get('ilv', 1):
                      for _ in chainC():
                          pass
                      for _ in chainB():
                          pass
                  while chains:
                      for g_ in list(chains):
                          try:
                              next(g_)
                          except StopIteration:
                              chains.remove(g_)
                  if l == 0 and t == 0:
                      dbg("ocat", o_cat, ["o_cat"])

                  transpose8(o_cat, "o_cat", hT, "hT")

                  def mmO(e):
                      ins = None
                      for n_ in range(2):
                          for k in range(8):
                              ins = e.matmul(psA[:, n_ * 512:(n_ + 1) * 512], lhsT=hT[:, k, :],
                                             rhs=w_out_sb[:, k, n_ * 512:(n_ + 1) * 512], start=(k == 0), stop=(k == 7))
                      return ins
                  PE(mmO, ["hT", "w_out"], ["psA0", "psA1"])
                  post_ln(t, psA[:, 0:512], "psA0", psA[:, 512:1024], "psA1", which, scr0, "scr0")
          if l == 0:
              dbg("x1", x[:, 0, :], ["x0"])

          ck("mix")
          barrier()
          for k in range(0, 8, 2):
              DMA("pool", wq_sb[:, k:k + 2, :], d_wq[l][k * 128:(k + 2) * 128, :].rearrange("(k p) n -> p k n", p=128),
                  w=["wq"])
          DMA("sp", lnp[:, 0, :], bcast_row(h_lng, (l * 2 + 1) * D, D), w=["lnp"])
          DMA("sp", lnp[:, 1, :], bcast_row(h_lnb, (l * 2 + 1) * D, D), w=["lnp"])
          for i in range(2):
              V(lambda e, i=i: e.memset(bigf[i], 0.0), [], [f"bg{i}_{q_}" for q_ in range(64)])
          mod_half(l, 1, p_wmb, "wmb", p_bmb, "bmb")

          def stage_A(i):
              t = i
              which = 0 if t < 4 else 1
              hb = h2[i % 2]
              hkey = f"h2_{i % 2}"
              slot = i % 2
              ln_mod(t, which, 0, 1024, hb, hkey, e_scr0, "scr0")
              yield
              transpose8(hb, hkey, h2T, "h2T")
              yield
              for half in range(2):
                  def mq(e, half=half):
                      ins = None
                      for c in range(4):
                          cc = half * 4 + c
                          for k in range(8):
                              ins = e.matmul(psA[:, c * 128:(c + 1) * 128], lhsT=wq_sb[:, k, cc * 128:(cc + 1) * 128],
                                             rhs=h2T[:, k, :], start=(k == 0), stop=(k == 7))
                      return ins
                  PE(mq, ["h2T", "wq"], ["psA0"])
                  ACT(lambda e, half=half: e.copy(out=qT_sb[:, half * 4:half * 4 + 4, :].rearrange("p c t -> p (c t)"),
                                                  in_=psA[:, 0:512]), ["psA0"], ["qT_sb"])
                  yield
              V(lambda e: e.memset(idxf, 0.0), [], ["idxf"])
              for hq in range(4):
                  def ms(e, hq=hq):
                      ins = None
                      for hh in range(2):
                          ins = e.matmul(psA[:, hh * 256:(hh + 1) * 256], lhsT=qT_sb[:, 2 * hq + hh, :],
                                         rhs=kb_b[:], start=True, stop=True)
                      return ins
                  PE(ms, ["qT_sb", "kb_b"], ["psA0"])
                  ACT(lambda e: e.copy(out=sc_sb, in_=psA[:, 0:512]), ["psA0"], ["sc_sb"])
                  yield
                  for g in range(4):
                      scg = sc_sb[:, g * 128:(g + 1) * 128]
                      V(lambda e, g=g, scg=scg: e.max(out=sv[:, g, 0:8], in_=scg), ["sc_sb"], ["sv"])
                      V(lambda e, g=g, scg=scg: e.max_index(out=si_u[:, g, 0:8], in_max=sv[:, g, 0:8], in_values=scg),
                        ["sc_sb", "sv"], ["si_u"])
                      V(lambda e, g=g, scg=scg: e.match_replace(out=scr_mr[:, 0:128], in_to_replace=sv[:, g, 0:8],
                                                                in_values=scg, imm_value=-1e30), ["sc_sb", "sv"], ["scr_mr"])
                      V(lambda e, g=g: e.max(out=sv[:, g, 8:16], in_=scr_mr[:, 0:128]), ["scr_mr"], ["sv"])
                      V(lambda e, g=g: e.max_index(out=si_u[:, g, 8:16], in_max=sv[:, g, 8:16], in_values=scr_mr[:, 0:128]),
                        ["scr_mr", "sv"], ["si_u"])
                      yield
                  V(lambda e: e.tensor_copy(out=sif, in_=si_u[:]), ["si_u"], ["sif"])
                  for hh in range(2):
                      c3 = cand[:, hh, :].rearrange("p (a b) -> p a b", b=16)
                      e3 = eid[:, hh, :].rearrange("p (a b) -> p a b", b=16)
                      V(lambda e, hh=hh, c3=c3: e.tensor_tensor(
                          out=c3, in0=sv[:, 2 * hh, :].unsqueeze(2).to_broadcast([128, 16, 16]),
                          in1=sv[:, 2 * hh + 1, :].unsqueeze(1).to_broadcast([128, 16, 16]), op=ALU.add),
                        ["sv"], ["cand"])
                      V(lambda e, hh=hh, e3=e3: e.scalar_tensor_tensor(
                          out=e3, in0=sif[:, 2 * hh, :].unsqueeze(2).to_broadcast([128, 16, 16]), scalar=128.0,
                          in1=sif[:, 2 * hh + 1, :].unsqueeze(1).to_broadcast([128, 16, 16]),
                          op0=ALU.mult, op1=ALU.add), ["sif"], ["eid"])
                      yield
                  for hh in range(2):
                      H = 2 * hq + hh
                      V(lambda e, hh=hh, H=H: e.max(out=cvv[:, H, 0:8], in_=cand[:, hh, :]), ["cand"], ["cvv"])
                      V(lambda e, hh=hh, H=H: e.match_replace(out=scr_mr[:, 0:256], in_to_replace=cvv[:, H, 0:8],
                                                              in_values=cand[:, hh, :], imm_value=-1e30),
                        ["cand", "cvv"], ["scr_mr"])
                      V(lambda e, H=H: e.max(out=cvv[:, H, 8:16], in_=scr_mr[:, 0:256]), ["scr_mr"], ["cvv"])
                      for k in range(16):
                          V(lambda e, hh=hh, H=H, k=k: e.scalar_tensor_tensor(
                              out=e_scr1[:, 0:256], in0=cand[:, hh, :], scalar=cvv[:, H, k:k + 1], in1=eid[:, hh, :],
                              op0=ALU.is_equal, op1=ALU.mult, accum_out=idxf[:, H * 16 + k:H * 16 + k + 1]),
                            ["cand", "eid", "cvv"], ["idxf"])
                          if k % 4 == 3:
                              yield
              yield
              V(lambda e: e.tensor_tensor(out=gat, in0=cvv, in1=cvv[:, :, 0:1].to_broadcast([128, 8, 16]), op=ALU.subtract),
                ["cvv"], ["gat"])
              ACT(lambda e: e.activation(out=gat, in_=gat, func=AF.Exp), ["gat"], ["gat"])
              V(lambda e: e.reduce_sum(out=sm[:, 8:16], in_=gat, axis=AX.X), ["gat"], ["smg"])
              V(lambda e: e.reciprocal(out=sm[:, 8:16], in_=sm[:, 8:16]), ["smg"], ["smg"])
              V(lambda e: e.tensor_tensor(out=gat, in0=gat, in1=sm[:, 8:16].unsqueeze(2).to_broadcast([128, 8, 16]),
                                          op=ALU.mult), ["gat", "smg"], ["gat"])
              V(lambda e: e.tensor_scalar(out=idxf, in0=idxf, scalar1=float(NEXP - 1), scalar2=0.0, op0=ALU.min, op1=ALU.max),
                ["idxf"], ["idxf"])
              V(lambda e, l=l: e.tensor_scalar(out=idxf, in0=idxf, scalar1=float(CFG.get("oob_add", 0)), scalar2=None, op0=ALU.add),
                ["idxf"], ["idxf"])

              def trI(e):
                  e.transpose(out=psA[:, 0:128], in_=idxf, identity=ident_f[:])
                  return e.transpose(out=psA[:, 128:256], in_=gat.rearrange("p h k -> p (h k)"), identity=ident_f[:])
              PE(trI, ["idxf", "gat", "ident_f"], ["psA0"])
              V(lambda e: e.tensor_copy(out=idx_i[:, slot, :], in_=psA[:, 0:128]), ["psA0"], [f"idx{slot}"])
              V(lambda e: e.tensor_copy(out=gT[slot], in_=psA[:, 128:256]), ["psA0"], [f"gT{slot}"])

          gcount = [0]
          ac = aT[0].rearrange("p a t -> p (a t)")[:, 0:32].rearrange("p (j c) -> p j c", c=4)
          bigdiag = [bigf[k_].rearrange("p (a b) -> p a b", b=65)[:, :, 0] for k_ in range(2)]

          tokst = {}
          XB = [(psB[:, 0:512], "psB0"), (psB[:, 512:1024], "psB1"), (psA[:, 512:1024], "psA1"), (psA[:, 1536:2048], "psA3")]

          def stage_S1(i, tt, tab=d_tab16):
              slot = i % 2
              hb, hkey = h2[i % 2], f"h2_{i % 2}"
              n_ = gcount[0]
              gcount[0] += 1
              b = n_ % NGB
              j = n_ % 8
              tokst[(i, tt)] = (b, j)
              S.dma("pool", lambda e: e.indirect_dma_start(
                  out=gbuf[b], out_offset=None, in_=tab,
                  in_offset=bass.IndirectOffsetOnAxis(ap=idx_i[:, slot, tt:tt + 1], axis=0),
                  bounds_check=REGS["bc"], oob_is_err=False), [f"idx{slot}"] + [f"tab16_{q_}" for q_ in range((CFG["nexp"] + 1023) // 1024)], [f"gb{b}"], SEM_U[b])
              for hf in range(2):
                  xb, xk = XB[2 * (n_ % 2) + hf]
                  PE(lambda e, hf=hf, xb=xb: e.matmul(xb, lhsT=ident_b[:, tt:tt + 1].to_broadcast([128, 128]),
                                                      rhs=hb[:, hf * 512:(hf + 1) * 512], start=True, stop=True),
                     [hkey, "ident_b"], [xk])
                  V(lambda e, hf=hf, xb=xb: e.scalar_tensor_tensor(
                      out=junk[:, hf * 512:(hf + 1) * 512], in0=gbuf[b][:, hf * 512:(hf + 1) * 512], scalar=1.0,
                      in1=xb, op0=ALU.mult, op1=ALU.mult,
                      accum_out=ac[:, j, hf:hf + 1]), [f"gb{b}", xk], [f"ac{j}_{hf}"])
              ACT(lambda e: e.activation(out=ac[:, j, 3:4], in_=ac[:, j, 0:1], func=AF.Gelu_apprx_tanh,
                                         bias=ac[:, j, 1:2]), [f"ac{j}_0", f"ac{j}_1"], [f"ac{j}"])
              sbk, t6 = tt // 64, tt % 64
              ACT(lambda e: e.activation(out=bigdiag[sbk][:, t6:t6 + 1], in_=ac[:, j, 3:4], func=AF.Copy,
                                         scale=gT[slot][:, tt:tt + 1]), [f"ac{j}", f"gT{slot}"], [f"bg{sbk}_{t6}"])

          def stage_S23(i, tt):
              slot = i % 2
              b, j = tokst.pop((i, tt))
              sbk, t6 = tt // 64, tt % 64
              bkey = f"bg{sbk}_{t6}"
              lhs = bigf[sbk][:, 0:4096].rearrange("p (t m) -> p t m", m=64)[:, t6, :]

              def mv(e):
                  e.matmul(psC[sbk * 64:(sbk + 1) * 64, :], lhsT=lhs, rhs=gbuf[b][:, 1024:1536],
                           start=(t6 == 0), stop=(t6 == 63))
                  return e.matmul(psA[sbk * 64:(sbk + 1) * 64, 1024:1536], lhsT=lhs, rhs=gbuf[b][:, 1536:2048],
                                  start=(t6 == 0), stop=(t6 == 63))
              PE(mv, [bkey, f"gb{b}"], ["psC", "psA2"])

          def stage_E(i):
              which = 0 if i < 4 else 1
              post_ln(i, psC[:, 0:512], "psC", psA[:, 1024:1536], "psA2", which, e_scr0, "scr0")

          ck("mod2")
          for _ in stage_A(0):
              pass
          ck("A")
          toks = [(i, tt) for i in range(NT) for tt in range(128)]
          stage_S1(*toks[0])
          stage_S1(*toks[1])
          genA = iter(())
          for n_t, (i, tt) in enumerate(toks):
              if tt == 0:
                  genA = stage_A(i + 1) if i + 1 < NT else iter(())
              if tt == 110:
                  for _ in genA:
                      pass
              if n_t + 2 < len(toks):
                  stage_S1(*toks[n_t + 2])
              stage_S23(i, tt)
              next(genA, None)
              if tt == 127:
                  stage_E(i)
          if l == 0:
              dbg("x2", x[:, 0, :], ["x0"])
          barrier()

    except _Stop:
        pass

    for t in range(NT):
        DMA("sp", o_y[t], x[:, t, :], r=[f"x{t}"])

    import contextlib
    sems = {}
    es = contextlib.ExitStack()
    for e_ in S.ENG:
        sems[("e", e_)] = es.enter_context(nc.semaphore(f"s_{e_}"))
    for i_ in range(len(S.dma_n)):
        sems[("d", i_)] = es.enter_context(nc.semaphore(f"d_{i_}"))
    with nc.Block() as block:

        def sem_of(key):
            return sems[key]

        def replay(name, eng):
            for (waits, fn, me, inc) in S.q[name]:
                for (s, v) in waits:
                    eng.wait_ge(sem_of(s), v)
                ins = fn(eng)
                ins.then_inc(sem_of(me[0]), inc)
            for (s, v) in S.final_waits(name):
                eng.wait_ge(sem_of(s), v)

        @block.tensor
        def _(e):
            replay("pe", e)

        @block.scalar
        def _(e):
            replay("act", e)

        @block.vector
        def _(e):
            replay("dve", e)

        @block.gpsimd
        def _(e):
            REGS["bc"] = e.alloc_register("bc")
            e.reg_mov(REGS["bc"], NEXP - 1)
            replay("pool", e)

        @block.sync
        def _(e):
            replay("sp", e)
    es.close()
    return nc


def _prep_inputs(inp):
    f = lambda a: np.ascontiguousarray(np.asarray(a, dtype=np.float32))
    x_prompt, x_sample, c = f(inp["x_prompt"]), f(inp["x_sample"]), f(inp["c"])
    c_ctx = f(inp["c_ctx"])
    cosb, sinb = _rope_tables(64)
    cosc, sinc = _rope_tables(32)
    shared = {
        "w_mod": f(inp["w_mod"]), "b_mod": f(inp["b_mod"]), "w_in": f(inp["w_in"]), "w_out": f(inp["w_out"]),
        "chunk_wT": f(np.transpose(f(inp["chunk_w"]), (0, 3, 1, 2))),
        "chunk_bT": f(np.transpose(f(inp["chunk_b"]), (0, 2, 1))),
        "win_sink": f(inp["win_sink"]),
        "lam_q": f(f(inp["diff_lam_q"]).reshape(DEPTH, 64)), "lam_k": f(f(inp["diff_lam_k"]).reshape(DEPTH, 64)),
        "subln_g": f(inp["diff_subln_g"]),
        "pool_wr": f(np.transpose(f(inp["pool_w"]), (0, 2, 1, 3))),
        "pool_scale": f(inp["pool_scale"]), "ln_g": f(inp["ln_g"]), "ln_b": f(inp["ln_b"]),
        "peer_wq": f(inp["peer_wq"]),
        "peer_uv": np.concatenate([f(inp["peer_u"]).reshape(DEPTH * NEXP, D), f(inp["peer_v"]).reshape(DEPTH * NEXP, D)], axis=1),
        "c_ident": np.eye(128, dtype=np.float32), "c_rb": _rot_matrix(64, 2), "c_rc": _rot_matrix(32, 2),
        "c_cosb": f(np.tile(cosb.T, (2, 1))), "c_sinb": f(np.tile(sinb.T, (2, 1))),
        "c_cosc": f(np.tile(cosc.T, (2, 1))), "c_sinc": f(np.tile(sinc.T, (2, 1))),
        "c_band": _band_bias(), "c_poolat": f(np.transpose(_pool_mats(), (2, 0, 1, 3))),
    }
    keys = f(inp["peer_keys"])
    kb = np.zeros((DEPTH, 128, 256), np.float32)
    for j in range(2):
        kb[:, j * 64:(j + 1) * 64, j * 128:(j + 1) * 128] = np.transpose(keys[:, j], (0, 2, 1))
    shared["peer_kb"] = kb
    cwk, cwv = f(inp["cache_win_k"]), f(inp["cache_win_v"])
    cdk, cdv = f(inp["cache_diff_k"]), f(inp["cache_diff_v"])
    maps = []
    for core in range(8):
        b = core // 4
        xin = np.concatenate([x_prompt[2 * core].reshape(2, 128, D), x_prompt[2 * core + 1].reshape(2, 128, D),
                              x_sample[b].reshape(8, 128, D)], 0)
        cvec = np.stack([c_ctx.reshape(8, 128).T, c[b].reshape(8, 128).T], 1)
        m = dict(shared)
        m["xin"] = f(xin)
        m["cvec"] = f(cvec)
        m["cwkT"] = f(np.transpose(cwk[b].reshape(DEPTH, 256, 128), (0, 2, 1)))
        m["cwv"] = f(cwv[b].reshape(DEPTH, 256, 128))
        m["cdkT"] = f(np.transpose(cdk[b].reshape(DEPTH, 256, 4, 64), (0, 3, 2, 1)))
        m["cdv"] = f(cdv[b].reshape(DEPTH, 256, 256))
        maps.append(m)
    return maps


_NC_CACHE = {}


def kernel(**inputs):
    maps = _prep_inputs(inputs)
    if "nc" not in _NC_CACHE:
        _NC_CACHE["nc"] = build_program()
    nc = _NC_CACHE["nc"]
    res = run_bass_kernel_spmd(nc, maps, core_ids=list(range(8)))
    R = res.results
    y_p = np.zeros((16, 256, D), np.float32)
    y_s = np.zeros((2, 1024, D), np.float32)
    nwk = np.zeros((16, DEPTH, 256, 2, 64), np.float32)
    nwv = np.zeros((16, DEPTH, 256, 2, 64), np.float32)
    ndk = np.zeros((16, DEPTH, 256, 4, 2, 32), np.float32)
    ndv = np.zeros((16, DEPTH, 256, 4, 64), np.float32)
    for core in range(8):
        y = np.asarray(R[core]["y"])
        for s in range(2):
            y_p[2 * core + s] = y[2 * s:2 * s + 2].reshape(256, D)
            nwk[2 * core + s] = np.asarray(R[core]["o_wk"])[s].reshape(DEPTH, 256, 2, 64)
            nwv[2 * core + s] = np.asarray(R[core]["o_wv"])[s].reshape(DEPTH, 256, 2, 64)
            ndk[2 * core + s] = np.asarray(R[core]["o_dk"])[s].reshape(DEPTH, 256, 4, 2, 32)
            ndv[2 * core + s] = np.asarray(R[core]["o_dv"])[s].reshape(DEPTH, 256, 4, 64)
        if core % 4 == 0:
            y_s[core // 4] = y[4:12].reshape(1024, D)
    kernel.last_results = R
    return (y_p, y_s, nwk, nwv, ndk, ndv)
```

```python
import math
import numpy as np
import concourse.bass as bass
import concourse.mybir as mybir
from concourse.bass_utils import run_bass_kernel_spmd

F32 = mybir.dt.float32
BF16 = mybir.dt.bfloat16
I32 = mybir.dt.int32
U32 = mybir.dt.uint32
AF = mybir.ActivationFunctionType
ALU = mybir.AluOpType
AX = mybir.AxisListType

DEPTH = 4
D = 1024
NT = 12
NEXP = 16384
ALPHA = (2 * DEPTH) ** 0.25
LN_EPS = 1e-5
NEG = -30000.0
NGB = 8
DEBUG = {}
CFG = {"depth": DEPTH, "stop": None, "nexp": NEXP}


class _Stop(Exception):
    pass


C_AU, C_AV, C_BQ, C_BK, C_BV, C_CQ, C_CK, C_CV, C_DZ = 0, 256, 512, 768, 896, 1024, 1280, 1536, 1792


class Sched:
    ENG = ("pe", "act", "dve", "pool", "sp")

    def __init__(self, n_dma_sems):
        self.q = {e: [] for e in self.ENG}
        self.cnt = {e: 0 for e in self.ENG}
        self.seen = {e: {} for e in self.ENG}
        self.last_w = {}
        self.readers = {}
        self.dma_n = [0] * n_dma_sems
        self.rr = 0
        self.n_general = n_dma_sems

    def _collect(self, eng, reads, writes, extra=(), raw=None):
        deps = {}
        own = ("e", eng)
        raw = set(reads if raw is None else raw)

        def add(d, same_ok):
            if d is None:
                return
            s, v = d
            if s == own and not same_ok:
                return
            if deps.get(s, 0) < v:
                deps[s] = v
        same_raw = eng != "pe"
        for k in reads:
            add(self.last_w.get(k), same_raw)
        for k in writes:
            add(self.last_w.get(k), same_raw and k in raw)
            for d in self.readers.get(k, {}).items():
                add(d, False)
        for d in extra:
            add(d, True)
        waits = []
        sn = self.seen[eng]
        for s, v in deps.items():
            if sn.get(s, 0) < v:
                sn[s] = v
                waits.append((s, v))
        return waits

    def _commit(self, me, reads, writes):
        for k in writes:
            self.last_w[k] = me
            self.readers[k] = {}
        for k in reads:
            r = self.readers.setdefault(k, {})
            if r.get(me[0], 0) < me[1]:
                r[me[0]] = me[1]

    @staticmethod
    def _norm(reads, writes):
        ps = [k for k in reads if k.startswith("ps")]
        if ps:
            reads = [k for k in reads if not k.startswith("ps")]
            writes = list(writes) + [k for k in ps if k not in writes]
        return reads, writes

    def op(self, eng, fn, reads=(), writes=(), extra=()):
        raw = list(reads)
        reads, writes = self._norm(reads, writes)
        waits = self._collect(eng, reads, writes, extra, raw)
        self.cnt[eng] += 1
        me = (("e", eng), self.cnt[eng])
        self.q[eng].append((waits, fn, me, 1))
        self._commit(me, reads, writes)

    def snapshot(self):
        snap = [(("e", e), self.cnt[e]) for e in self.ENG if self.cnt[e]]
        snap += [(("d", i), 16 * n) for i, n in enumerate(self.dma_n) if n]
        return snap

    def dma(self, eng, fn, reads=(), writes=(), sem=None):
        raw = list(reads)
        reads, writes = self._norm(reads, writes)
        if sem is None:
            sem = self.rr
            self.rr = (self.rr + 1) % self.n_general
        prev = (("d", sem), 16 * self.dma_n[sem]) if self.dma_n[sem] else None
        waits = self._collect(eng, reads, writes, extra=(prev,), raw=raw)
        self.dma_n[sem] += 1
        me = (("d", sem), 16 * self.dma_n[sem])
        self.q[eng].append((waits, fn, me, 16))
        self._commit(me, reads, writes)

    def final_waits(self, eng):
        waits = []
        for i, n in enumerate(self.dma_n):
            if n:
                waits.append((("d", i), 16 * n))
        for e in self.ENG:
            if e != eng and self.cnt[e]:
                waits.append((("e", e), self.cnt[e]))
        return waits


def _rope_tables(dim):
    n_tok, grid = 1024, 64
    rows = n_tok // grid
    row = np.repeat(np.arange(rows, dtype=np.float32), grid)
    col = np.tile(np.arange(grid, dtype=np.float32), rows)
    nf = dim // 4
    inv = (np.float32(10000.0) ** (-np.arange(nf, dtype=np.float32) / np.float32(nf))).astype(np.float32)
    ar = row[:, None] * inv
    ac = col[:, None] * inv
    ang = np.concatenate([ar, ar, ac, ac], axis=-1).astype(np.float32)
    return np.cos(ang).astype(np.float32), np.sin(ang).astype(np.float32)


def _rot_matrix(dim, reps):
    q = dim // 4
    r = np.zeros((dim, dim), np.float32)
    for a in range(q):
        r[q + a, a] = -1.0
        r[a, q + a] = 1.0
        r[3 * q + a, 2 * q + a] = -1.0
        r[2 * q + a, 3 * q + a] = 1.0
    out = np.zeros((dim * reps, dim * reps), np.float32)
    for i in range(reps):
        out[i * dim:(i + 1) * dim, i * dim:(i + 1) * dim] = r
    return out


def _pool_mats():
    S = 384
    out = np.zeros((5, 4, 128, 128), np.float32)
    for g, w in enumerate((2, 4, 8, 16)):
        def mat(seq_lo, seq_hi):
            m = np.zeros((S, S), np.float32)
            for t in range(128, 256):
                lo = min(max(t - w // 2, seq_lo), seq_hi)
                hi = min(max(t + w // 2, seq_lo), seq_hi)
                m[t, lo:hi] = 1.0 / float(hi - lo)
                m[t, t] -= 1.0
            return m
        mid = mat(0, S)
        first = mat(128, S)
        last = mat(0, 256)
        out[0, g] = first[128:256, 128:256].T
        out[1, g] = mid[128:256, 128:256].T
        out[2, g] = last[128:256, 128:256].T
        out[3, g] = mid[128:256, 0:128].T
        out[4, g] = mid[128:256, 256:384].T
    return out


def _band_bias():
    qi = np.arange(128)[:, None]
    ki = np.arange(128)[None, :]
    prev = np.where(ki >= qi, 0.0, NEG).astype(np.float32)
    nxt = np.where(ki <= qi, 0.0, NEG).astype(np.float32)
    z = np.zeros((128, 128), np.float32)
    full = np.full((128, 128), NEG, np.float32)
    return np.stack([np.concatenate([full, z, nxt], 1), np.concatenate([prev, z, nxt], 1),
                     np.concatenate([prev, z, full], 1)], 0)


def build_program():
    nc = bass.Bass("TRN2", target_bir_lowering=False)

    def din(name, shape, dt=F32):
        return nc.dram_tensor(name, list(shape), dt, kind="ExternalInput")

    def dout(name, shape, dt=F32):
        return nc.dram_tensor(name, list(shape), dt, kind="ExternalOutput")

    d_x = din("xin", [NT, 128, D]).ap()
    d_cvec = din("cvec", [128, 2, 8]).ap()
    d_cwkT = din("cwkT", [DEPTH, 128, 256]).ap()
    d_cwv = din("cwv", [DEPTH, 256, 128]).ap()
    d_cdkT = din("cdkT", [DEPTH, 64, 4, 256]).ap()
    d_cdv = din("cdv", [DEPTH, 256, 256]).ap()
    d_wmod = din("w_mod", [CFG["depth"], D, 6 * D]).ap()
    h_bmod = din("b_mod", [DEPTH, 6 * D])
    d_win = din("w_in", [CFG["depth"], D, 2048]).ap()
    d_wout = din("w_out", [CFG["depth"], D, D]).ap()
    d_cwT = din("chunk_wT", [DEPTH, 128, 4, 128]).ap()
    d_cbT = din("chunk_bT", [DEPTH, 128, 4]).ap()
    h_sink = din("win_sink", [DEPTH, 4])
    h_lamq = din("lam_q", [DEPTH, 64])
    h_lamk = din("lam_k", [DEPTH, 64])
    h_subg = din("subln_g", [DEPTH, 64])
    d_poolw = din("pool_wr", [DEPTH, 64, 4, 64]).ap()
    h_pscale = din("pool_scale", [DEPTH, 256])
    h_lng = din("ln_g", [DEPTH, 2, D])
    h_lnb = din("ln_b", [DEPTH, 2, D])
    d_wq = din("peer_wq", [CFG["depth"], D, D]).ap()
    d_kb = din("peer_kb", [DEPTH, 128, 256]).ap()
    d_puv = din("peer_uv", [CFG["depth"] * CFG["nexp"], 2 * D]).ap()
    d_ident = din("c_ident", [128, 128]).ap()
    d_rb = din("c_rb", [128, 128]).ap()
    d_rc = din("c_rc", [64, 64]).ap()
    d_cosb = din("c_cosb", [128, 1024]).ap()
    d_sinb = din("c_sinb", [128, 1024]).ap()
    d_cosc = din("c_cosc", [64, 1024]).ap()
    d_sinc = din("c_sinc", [64, 1024]).ap()
    d_band = din("c_band", [3, 128, 384]).ap()
    d_poolat = din("c_poolat", [128, 5, 4, 128]).ap()

    d_tab16 = nc.dram_tensor("tab16", [NEXP, 2 * D], BF16, kind="Internal").ap()
    o_y = dout("y", [NT, 128, D]).ap()
    o_wk = dout("o_wk", [2, DEPTH, 256, 128]).ap()
    o_wv = dout("o_wv", [2, DEPTH, 256, 128]).ap()
    o_dk = dout("o_dk", [2, DEPTH, 256, 256]).ap()
    o_dv = dout("o_dv", [2, DEPTH, 256, 256]).ap()
    dbg_out = {}
    for name, shp in DEBUG.items():
        dbg_out[name] = dout("dbg_" + name, shp).ap()

    def bcast_row(h, off, n, parts=128):
        return bass.AP(h, off, [[0, parts], [1, n]])

    def sb(name, shape, dt):
        return nc.alloc_sbuf_tensor(name, list(shape), dt)

    x = sb("x", [128, NT, D], F32)
    mod = sb("mod", [128, 2, 3072], F32)
    reg1 = sb("reg1", [128, 24576], BF16)
    A16N, A32N = 21760, 6784
    a16 = sb("a16", [128, A16N], BF16)
    a32 = sb("a32", [128, A32N], F32)
    idx_i = sb("idx_i", [128, 3, 128], I32)
    si_u = sb("si_u", [128, 4, 16], U32)
    ident_f = sb("ident_f", [128, 128], F32)
    ident_b = sb("ident_b", [128, 128], BF16)
    rb_f = sb("rb_f", [128, 128], F32)
    rc_f = sb("rc_f", [64, 64], F32)
    band = sb("band", [128, 3, 384], BF16)
    poolat = sb("poolat", [128, 5, 4, 128], BF16)
    silu_b = sb("silu_b", [128, 2, 8], BF16)
    cvec = sb("cvec_sb", [128, 2, 8], F32)
    cwT = sb("cwT", [128, 4, 128], BF16)
    cbT = sb("cbT", [128, 4], F32)
    sink = sb("sink", [128, 4], F32)
    lamt = sb("lamt", [128, 2, 64], F32)
    lamv = sb("lamv", [128, 8], F32)
    subg = sb("subg", [128, 64], F32)
    poolw = sb("poolw", [64, 4, 64], BF16)
    pscale = sb("pscale", [128, 256], F32)
    kb_b = sb("kb_b", [128, 256], BF16)
    st6 = sb("st6", [128, 2, 6], F32)
    sm = sb("sm", [128, 32], F32)
    scrB = sb("scrB", [128, 640], F32)

    psA = nc.alloc_psum_tensor("psA", [128, 2048], F32)
    psB = nc.alloc_psum_tensor("psB", [128, 1024], F32)
    psC = nc.alloc_psum_tensor("psC", [128, 512], F32)
    psT = nc.alloc_psum_tensor("psT", [128, 1024], BF16)

    class Carver:
        def __init__(self, t, n):
            self.t, self.n, self.o = t, n, 0

        def take(self, size, pat=None, parts=128, **kw):
            assert self.o + size <= self.n, (self.o, size, self.n)
            ap = self.t[0:parts, self.o:self.o + size]
            self.o += size
            if pat:
                ap = ap.rearrange(pat, **kw)
            return ap

    w_in_sb = reg1[:, 0:16384].rearrange("p (k n) -> p k n", n=2048)
    w_out_sb = reg1[:, 16384:24576].rearrange("p (k n) -> p k n", n=1024)
    wq_sb = reg1[:, 0:8192].rearrange("p (k n) -> p k n", n=1024)
    gbuf = [reg1[:, 8192 + i * 2048: 8192 + (i + 1) * 2048] for i in range(NGB)]

    c16 = Carver(a16, A16N)
    h_bf = c16.take(1024)
    hT = c16.take(1024, "p (k t) -> p k t", t=128)
    z_all = c16.take(2048, "p (b c) -> p b c", c=256)
    kTb_all = c16.take(1280)
    vb_all = c16.take(1280, "p (b c) -> p b c", c=128)
    kTc_all = c16.take(5120, "p (h s) -> p h s", parts=64, s=1280)
    vc_all = c16.take(2560, "p (b c) -> p b c", c=256)
    ua = c16.take(512)
    qTb = c16.take(256, "p (c t) -> p c t", t=128)
    qTc = c16.take(512, "p (c t) -> p c t", parts=64, t=128)
    w_bf = c16.take(1280)
    wT = c16.take(1280, "p (b q) -> p b q", q=128)
    o_cat = c16.take(1024)
    pooledT = c16.take(512, "p (g t) -> p g t", parts=64, t=128)
    wmb = [c16.take(1024, "p (k n) -> p k n", n=128) for _ in range(2)]
    c32 = Carver(a32, A32N)
    scr0 = c32.take(1280)
    scr1 = c32.take(1408)
    csB = c32.take(256, "p (a t) -> p a t", t=128)
    csC = c32.take(256, "p (a t) -> p a t", parts=64, t=128)
    bmb = [c32.take(128) for _ in range(2)]
    assert c32.o == 3456
    lnp = c32.take(2048, "p (a n) -> p a n", n=1024)
    p16 = Carver(a16, A16N)
    h2 = [p16.take(1024) for _ in range(2)]
    h2T = p16.take(1024, "p (k t) -> p k t", t=128)
    qT_sb = p16.take(1024, "p (c t) -> p c t", t=128)
    junk = p16.take(1024)
    w_pb = [p16.take(128) for _ in range(2)]
    BIGN = 64 * 65
    bigf = [p16.take(BIGN) for _ in range(2)]
    p_wmb = [p16.take(1024, "p (k n) -> p k n", n=128) for _ in range(2)]
    p32 = Carver(a32, A32N)
    e_scr0 = p32.take(1024)
    e_scr1 = p32.take(256)
    sc_sb = p32.take(512)
    scr_mr = p32.take(256)
    sv = p32.take(64, "p (g k) -> p g k", k=16)
    sif = p32.take(64, "p (g k) -> p g k", k=16)
    cand = p32.take(512, "p (h j) -> p h j", j=256)
    eid = p32.take(512, "p (h j) -> p h j", j=256)
    cvv = p32.take(128, "p (h k) -> p h k", k=16)
    gat = p32.take(128, "p (h k) -> p h k", k=16)
    assert p32.o == 3456
    p32.o = 5504
    idxf = p32.take(128)
    gT = [p32.take(128) for _ in range(3)]
    aT = [p32.take(256, "p (a t) -> p a t", t=128) for _ in range(2)]
    p_bmb = [p32.take(128) for _ in range(2)]

    n_general = 24
    REGS = {}
    S = Sched(n_general + 2 * NGB + 4)
    SEM_U = [n_general + i for i in range(NGB)]
    SEM_V = [n_general + NGB + i for i in range(NGB)]

    def V(fn, r=(), w=()):
        S.op("dve", fn, r, w)

    def ACT(fn, r=(), w=()):
        S.op("act", fn, r, w)

    def PE(fn, r=(), w=()):
        S.op("pe", fn, r, w)

    def POOL(fn, r=(), w=()):
        S.op("pool", fn, r, w)

    def DMA(eng, out, in_, r=(), w=(), sem=None):
        S.dma(eng, lambda e: e.dma_start(out=out, in_=in_), r, w, sem)

    def barrier():
        snap = S.snapshot()
        for e in S.ENG:
            S.op(e, lambda en: en.nop(), extra=snap)
        S.readers = {}
        S.last_w = {}

    DMA("sp", ident_f[:], d_ident, w=["ident_f"])
    DMA("sp", rb_f[:], d_rb, w=["rb_f"])
    DMA("sp", rc_f[:], d_rc, w=["rc_f"])
    DMA("pool", band[:], d_band.rearrange("v q k -> q v k"), w=["band"])
    DMA("sp", cvec[:], d_cvec, w=["cvec"])
    DMA("pool", ident_b[:], d_ident, w=["ident_b"])
    DMA("pool", poolat[:], d_poolat, w=["poolat"])
    for t in range(NT):
        DMA("sp", x[:, t, :], d_x[t], w=[f"x{t}"])
    ACT(lambda e: e.activation(out=silu_b[:], in_=cvec[:], func=AF.Silu), ["cvec"], ["silu_b"])
    for i in range(2):
        V(lambda e, i=i: e.memset(bigf[i], 0.0), [], [f"big{i}"])

    def ln_stats(src_ap, src_keys, tag):
        V(lambda e: e.bn_stats(out=st6[:, 0, :], in_=src_ap[:, 0:512]), src_keys, ["st6a"])
        V(lambda e: e.bn_stats(out=st6[:, 1, :], in_=src_ap[:, 512:1024]), src_keys, ["st6b"])
        V(lambda e: e.bn_aggr(out=sm[:, 0:2], in_=st6[:].rearrange("p a b -> p (a b)")), ["st6a", "st6b"], ["sm_mv"])
        V(lambda e: e.tensor_scalar(out=sm[:, 2:3], in0=sm[:, 1:2], scalar1=LN_EPS, scalar2=None, op0=ALU.add),
          ["sm_mv"], ["sm_rstd"])
        ACT(lambda e: e.activation(out=sm[:, 2:3], in_=sm[:, 2:3], func=AF.Sqrt), ["sm_rstd"], ["sm_rstd"])
        V(lambda e: e.reciprocal(out=sm[:, 2:3], in_=sm[:, 2:3]), ["sm_rstd"], ["sm_rstd"])
        V(lambda e: e.scalar_tensor_tensor(out=sm[:, 3:4], in0=sm[:, 0:1], scalar=-1.0, in1=sm[:, 2:3],
                                           op0=ALU.mult, op1=ALU.mult), ["sm_mv", "sm_rstd"], ["sm_nmr"])
        return sm[:, 2:3], sm[:, 3:4]

    def ln_mod(t, which, sh_off, sc_off, dst_bf, dst_key, tmp, tmp_key):
        rstd, nmr = ln_stats(x[:, t, :], [f"x{t}"], "a")
        ACT(lambda e: e.activation(out=tmp[:, 0:1024], in_=x[:, t, :], func=AF.Identity, bias=nmr, scale=rstd),
            [f"x{t}", "sm_rstd", "sm_nmr"], [tmp_key])
        V(lambda e: e.tensor_tensor(out=tmp[:, 0:1024], in0=tmp[:, 0:1024], in1=mod[:, which, sc_off:sc_off + 1024],
                                    op=ALU.mult), [tmp_key, "mod"], [tmp_key])
        V(lambda e: e.tensor_tensor(out=dst_bf, in0=tmp[:, 0:1024], in1=mod[:, which, sh_off:sh_off + 1024],
                                    op=ALU.add), [tmp_key, "mod"], [dst_key])

    def transpose8(src_bf, src_key, dst, dst_key):
        def tr(e):
            ins = None
            for k in range(8):
                ins = e.transpose(out=psT[:, k * 128:(k + 1) * 128], in_=src_bf[:, k * 128:(k + 1) * 128],
                                  identity=ident_b[:])
            return ins
        PE(tr, [src_key, "ident_b"], ["psT"])
        ACT(lambda e: e.copy(out=dst.rearrange("p k t -> p (k t)"), in_=psT[:, :]), ["psT"], [dst_key])

    def mod_half(l, half, wb, wb_key, bb, bb_key):
        for j in range(24):
            c0 = half * 3072 + j * 128
            b = j % 2
            DMA("pool", wb[b], d_wmod[l][:, c0:c0 + 128].rearrange("(k p) n -> p k n", p=128), w=[f"{wb_key}{b}"])
            DMA("sp", bb[b], bcast_row(h_bmod, l * 6144 + c0, 128), w=[f"{bb_key}{b}"])
            for which in range(2):
                def mm(e, which=which, b=b):
                    ins = None
                    for k in range(8):
                        ins = e.matmul(psA[:, which * 512: which * 512 + 128],
                                       lhsT=silu_b[:, which, k:k + 1].to_broadcast([128, 128]),
                                       rhs=wb[b][:, k, :], start=(k == 0), stop=(k == 7))
                    return ins
                PE(mm, ["silu_b", f"{wb_key}{b}"], [f"psA{which}"])
                addc = 1.0 if 8 <= j < 16 else 0.0
                V(lambda e, which=which, b=b, j=j, addc=addc: e.scalar_tensor_tensor(
                    out=mod[:, which, j * 128:(j + 1) * 128], in0=psA[:, which * 512: which * 512 + 128],
                    scalar=addc, in1=bb[b], op0=ALU.add, op1=ALU.add),
                  [f"psA{which}", f"{bb_key}{b}"], ["mod"])

    def dbg(name, ap, keys):
        if name in dbg_out:
            DMA("pool", dbg_out[name], ap, r=keys)

    def attn_pv(blocks, vfn, vkey, out_ps, out_key, first_group_start=True):
        n = len(blocks)
        for r0 in range(0, n, 8):
            grp = list(range(r0, min(n, r0 + 8)))

            def tr(e, grp=grp, r0=r0):
                ins = None
                for n_ in grp:
                    ins = e.transpose(out=psT[:, (n_ - r0) * 128:(n_ - r0 + 1) * 128],
                                      in_=w_bf[:, n_ * 128:(n_ + 1) * 128], identity=ident_b[:])
                return ins
            PE(tr, ["w_bf", "ident_b"], ["psT"])
            ACT(lambda e, grp=grp, r0=r0: e.copy(out=wT[:, r0:r0 + len(grp), :].rearrange("p b q -> p (b q)"),
                                                 in_=psT[:, 0:len(grp) * 128]), ["psT"], ["wT"])

        def pv(e):
            ins = None
            for n_, jb in enumerate(blocks):
                ins = e.matmul(out_ps, lhsT=wT[:, n_, :], rhs=vfn(jb), start=(n_ == 0), stop=(n_ == n - 1))
            return ins
        PE(pv, ["wT", vkey], [out_key])

    def post_ln(t, src0, src0_key, src1, src1_key, which, tmp, tmp_key):
        V(lambda e: e.tensor_tensor(out=tmp[:, 0:512], in0=src0, in1=mod[:, which, 2048:2560], op=ALU.mult),
          [src0_key, "mod"], [tmp_key])
        V(lambda e: e.tensor_tensor(out=tmp[:, 512:1024], in0=src1, in1=mod[:, which, 2560:3072], op=ALU.mult),
          [src1_key, "mod"], [tmp_key])
        V(lambda e: e.scalar_tensor_tensor(out=tmp[:, 0:1024], in0=x[:, t, :], scalar=ALPHA, in1=tmp[:, 0:1024],
                                           op0=ALU.mult, op1=ALU.add), [f"x{t}", tmp_key], [tmp_key])
        rstd, nmr = ln_stats(tmp, [tmp_key], "p")
        ACT(lambda e: e.activation(out=tmp[:, 0:1024], in_=tmp[:, 0:1024], func=AF.Identity, bias=nmr, scale=rstd),
            [tmp_key, "sm_rstd", "sm_nmr"], [tmp_key])
        V(lambda e: e.tensor_tensor(out=tmp[:, 0:1024], in0=tmp[:, 0:1024], in1=lnp[:, 0, :], op=ALU.mult),
          [tmp_key, "lnp"], [tmp_key])
        V(lambda e: e.tensor_tensor(out=x[:, t, :], in0=tmp[:, 0:1024], in1=lnp[:, 1, :], op=ALU.add),
          [tmp_key, "lnp"], [f"x{t}"])

    def ck(name):
        if CFG["stop"] == name:
            raise _Stop()

    try:
      ck("init")
      for l in range(CFG["depth"]):
          lam_init = 0.8 - 0.6 * math.exp(-0.3 * l)
          for k in range(8):
              DMA("pool", w_in_sb[:, k, :], d_win[l][k * 128:(k + 1) * 128, :], w=["w_in"])
          for k in range(0, 8, 2):
              DMA("pool", w_out_sb[:, k:k + 2, :],
                  d_wout[l][k * 128:(k + 2) * 128, :].rearrange("(k p) n -> p k n", p=128), w=["w_out"])
          DMA("pool", cwT[:], d_cwT[l], w=["cwT"])
          DMA("pool", poolw[:], d_poolw[l], w=["poolw"])
          DMA("pool", kb_b[:], d_kb[l], w=["kb_b"])
          DMA("sp", cbT[:], d_cbT[l], w=["cbT"])
          DMA("sp", sink[:], bcast_row(h_sink, l * 4, 4), w=["sink"])
          DMA("sp", lamt[:, 0, :], bcast_row(h_lamq, l * 64, 64), w=["lamt"])
          DMA("sp", lamt[:, 1, :], bcast_row(h_lamk, l * 64, 64), w=["lamt"])
          DMA("sp", subg[:], bcast_row(h_subg, l * 64, 64), w=["subg"])
          DMA("sp", pscale[:], bcast_row(h_pscale, l * 256, 256), w=["pscale"])
          DMA("sp", lnp[:, 0, :], bcast_row(h_lng, (l * 2 + 0) * D, D), w=["lnp"])
          DMA("sp", lnp[:, 1, :], bcast_row(h_lnb, (l * 2 + 0) * D, D), w=["lnp"])
          for j in range(2):
              V(lambda e, j=j: e.tensor_tensor(out=lamt[:, 0, j * 32:(j + 1) * 32], in0=lamt[:, 0, j * 32:(j + 1) * 32],
                                               in1=lamt[:, 1, j * 32:(j + 1) * 32], op=ALU.mult), ["lamt"], ["lamt"])
              V(lambda e, j=j: e.reduce_sum(out=lamv[:, j:j + 1], in_=lamt[:, 0, j * 32:(j + 1) * 32], axis=AX.X),
                ["lamt"], ["lamv"])
          ACT(lambda e: e.activation(out=lamv[:, 2:4], in_=lamv[:, 0:2], func=AF.Exp), ["lamv"], ["lamve"])
          V(lambda e, lam_init=lam_init: e.scalar_tensor_tensor(out=lamv[:, 5:6], in0=lamv[:, 3:4], scalar=-lam_init, in1=lamv[:, 2:3],
                                             op0=ALU.add, op1=ALU.subtract), ["lamve"], ["nlam"])
          V(lambda e, lam_init=lam_init: e.tensor_scalar(out=subg[:], in0=subg[:], scalar1=1.0 - lam_init, scalar2=None, op0=ALU.mult),
            ["subg"], ["subg"])
          mod_half(l, 0, wmb, "wmb", bmb, "bmb")
          NE_ = CFG["nexp"]
          for q_ in range(0, NE_, 1024):
              rr = min(1024, NE_ - q_)
              DMA("pool", d_tab16[q_:q_ + rr, :], d_puv[l * NE_ + q_:l * NE_ + q_ + rr, :], w=[f"tab16_{q_ // 1024}"])
          if l == 0:
              dbg("mod0", mod[:, 0, :], ["mod"])
          ck("mod")

          seqs = [(0, 2, False), (2, 2, False), (4, 8, True)]
          for (t0, nb, lat) in seqs:
              which = 1 if lat else 0
              nkb = nb + (2 if lat else 0)
              if lat:
                  DMA("pool", kTb_all[:, 1024:1280], d_cwkT[l], w=["kTb_all"])
                  DMA("pool", vb_all[:, 8:10, :], d_cwv[l].rearrange("(b p) c -> p b c", p=128), w=["vb_all"])
                  DMA("pool", kTc_all[:, :, 1024:1280], d_cdkT[l], w=["kTc_all"])
                  DMA("pool", vc_all[:, 8:10, :], d_cdv[l].rearrange("(b p) c -> p b c", p=128), w=["vc_all"])

              def rope(src_ps, src_key, n_ch, parts, tmp32, tmp_key, cs, cskey, rmat, rkey, dst, dst_key, lat=lat):
                  if not lat:
                      ACT(lambda e: e.copy(out=dst, in_=src_ps), [src_key], [dst_key])
                      return
                  V(lambda e: e.tensor_copy(out=tmp32, in_=src_ps), [src_key], [tmp_key])

                  def mm(e):
                      ins = None
                      for c in range(n_ch):
                          ins = e.matmul(src_ps[:, c, :], lhsT=rmat, rhs=tmp32[:, c, :], start=True, stop=True)
                      return ins
                  PE(mm, [tmp_key, rkey], [src_key])
                  V(lambda e: e.tensor_tensor(out=tmp32, in0=tmp32, in1=cs[:, 0:1, :].to_broadcast([parts, n_ch, 128]),
                                              op=ALU.mult), [tmp_key, cskey], [tmp_key])
                  V(lambda e: e.tensor_tensor(out=src_ps, in0=src_ps, in1=cs[:, 1:2, :].to_broadcast([parts, n_ch, 128]),
                                              op=ALU.mult), [src_key, cskey], [src_key])
                  V(lambda e: e.tensor_tensor(out=dst, in0=tmp32, in1=src_ps, op=ALU.add), [tmp_key, src_key], [dst_key])

              def load_cs(i):
                  tk = i * 128
                  DMA("sp", csB[:, 0, :], d_cosb[:, tk:tk + 128], w=["csB"])
                  DMA("sp", csB[:, 1, :], d_sinb[:, tk:tk + 128], w=["csB"])
                  DMA("sp", csC[:, 0, :], d_cosc[:, tk:tk + 128], w=["csC"])
                  DMA("sp", csC[:, 1, :], d_sinc[:, tk:tk + 128], w=["csC"])

              for i in range(nb):
                  t = t0 + i
                  ln_mod(t, which, 0, 1024, h_bf, "h_bf", scr0, "scr0")
                  ck("ln")
                  transpose8(h_bf, "h_bf", hT, "hT")
                  ck("tr")
                  if lat:
                      load_cs(i)

                  def tm(e):
                      ins = None
                      for (c_lo, n, po) in ((C_BK, 256, 0), (C_CK, 256, 512), (C_CV, 512, 1024)):
                          for k in range(8):
                              ins = e.matmul(psA[:, po:po + n], lhsT=hT[:, k, :], rhs=w_in_sb[:, k, c_lo:c_lo + n],
                                             start=(k == 0), stop=(k == 7))
                      return ins
                  PE(tm, ["hT", "w_in"], ["psA0", "psA1", "psA2"])

                  def fm(e):
                      ins = None
                      for k in range(8):
                          ins = e.matmul(psC[:, 0:128], lhsT=w_in_sb[:, k, C_BK:C_BK + 128], rhs=hT[:, k, :],
                                         start=(k == 0), stop=(k == 7))
                      for c in range(4):
                          for k in range(8):
                              ins = e.matmul(psB[0:64, c * 128:(c + 1) * 128],
                                             lhsT=w_in_sb[:, k, C_CK + c * 64:C_CK + (c + 1) * 64], rhs=hT[:, k, :],
                                             start=(k == 0), stop=(k == 7))
                      return ins
                  PE(fm, ["hT", "w_in"], ["psC", "psB0"])
                  ck("mm")
                  ACT(lambda e, i=i: e.copy(out=vb_all[:, i, :], in_=psA[:, 128:256]), ["psA0"], ["vb_all"])
                  ACT(lambda e, i=i: e.copy(out=vc_all[:, i, :], in_=psA[:, 1024:1280]), ["psA2"], ["vc_all"])
                  ACT(lambda e, i=i: e.copy(out=z_all[:, i, :], in_=psA[:, 1280:1536]), ["psA2"], ["z_all"])
                  if not lat:
                      sq = t0 // 2
                      V(lambda e: e.tensor_copy(out=scr1[:, 0:256], in_=psA[:, 0:256]), ["psA0"], ["scr1o"])
                      V(lambda e: e.tensor_copy(out=scr1[:, 256:512], in_=psA[:, 512:768]), ["psA1"], ["scr1o"])
                      V(lambda e: e.tensor_copy(out=scr1[:, 512:768], in_=psA[:, 1024:1280]), ["psA2"], ["scr1o"])
                      r0 = i * 128
                      DMA("sp", o_wk[sq, l, r0:r0 + 128, :], scr1[:, 0:128], r=["scr1o"])
                      DMA("sp", o_wv[sq, l, r0:r0 + 128, :], scr1[:, 128:256], r=["scr1o"])
                      DMA("sp", o_dk[sq, l, r0:r0 + 128, :], scr1[:, 256:512], r=["scr1o"])
                      DMA("sp", o_dv[sq, l, r0:r0 + 128, :], scr1[:, 512:768], r=["scr1o"])
                  ck("ev")
                  tk = i * 128
                  fmB32 = scr1[:, 768:896].rearrange("p (c t) -> p c t", t=128)
                  fmC32 = scr1[0:64, 896:1408].rearrange("p (c t) -> p c t", t=128)
                  rope(psC[:, 0:128].rearrange("p (c t) -> p c t", t=128), "psC", 1, 128, fmB32, "scr1o", csB, "csB",
                       rb_f[:], "rb_f", kTb_all[:, tk:tk + 128].rearrange("p (c t) -> p c t", t=128), "kTb_all")
                  rope(psB[0:64, 0:512].rearrange("p (c t) -> p c t", t=128), "psB0", 4, 64, fmC32, "scr1o", csC, "csC",
                       rc_f[:], "rc_f", kTc_all[:, :, tk:tk + 128], "kTc_all")
              if l == 0 and t0 == 0:
                  dbg("kTb", kTb_all[:, 0:256], ["kTb_all"])
                  dbg("z", z_all[:, 0:2, :], ["z_all"])
              ck("p1")

              for i in range(nb):
                  t = t0 + i
                  ln_mod(t, which, 0, 1024, h_bf, "h_bf", scr0, "scr0")
                  transpose8(h_bf, "h_bf", hT, "hT")
                  if lat:
                      load_cs(i)

                  def tmA(e):
                      ins = None
                      for k in range(8):
                          ins = e.matmul(psA[:, 0:512], lhsT=hT[:, k, :], rhs=w_in_sb[:, k, 0:512],
                                         start=(k == 0), stop=(k == 7))
                      return ins
                  PE(tmA, ["hT", "w_in"], ["psA0"])

                  def fmq(e):
                      ins = None
                      for g in range(2):
                          for kv in range(2):
                              c0 = C_BQ + (kv * 2 + g) * 64
                              for k in range(8):
                                  ins = e.matmul(psC[kv * 64:(kv + 1) * 64, g * 128:(g + 1) * 128],
                                                 lhsT=w_in_sb[:, k, c0:c0 + 64], rhs=hT[:, k, :],
                                                 start=(k == 0), stop=(k == 7))
                      for c in range(4):
                          for k in range(8):
                              ins = e.matmul(psB[0:64, c * 128:(c + 1) * 128],
                                             lhsT=w_in_sb[:, k, C_CQ + c * 64:C_CQ + (c + 1) * 64], rhs=hT[:, k, :],
                                             start=(k == 0), stop=(k == 7))
                      return ins
                  PE(fmq, ["hT", "w_in"], ["psC", "psB0"])
                  ACT(lambda e: e.activation(out=ua, in_=psA[:, 0:512], func=AF.Gelu_apprx_tanh), ["psA0"], ["ua"])
                  fmB32 = scr1[:, 768:1024].rearrange("p (c t) -> p c t", t=128)
                  fmC32 = scr1[0:64, 0:512].rearrange("p (c t) -> p c t", t=128)
                  rope(psC[:, 0:256].rearrange("p (c t) -> p c t", t=128), "psC", 2, 128, fmB32, "scr1o", csB, "csB",
                       rb_f[:], "rb_f", qTb, "qTb")
                  rope(psB[0:64, 0:512].rearrange("p (c t) -> p c t", t=128), "psB0", 4, 64, fmC32, "scr1o", csC, "csC",
                       rc_f[:], "rc_f", qTc, "qTc")

                  def mmA(e):
                      ins = None
                      for hh in range(4):
                          ins = e.matmul(psA[:, 1536 + hh * 64:1536 + (hh + 1) * 64], lhsT=cwT[:, hh, :],
                                         rhs=ua[:, 256 + hh * 64:256 + (hh + 1) * 64], start=True, stop=True)
                      return ins
                  PE(mmA, ["ua", "cwT"], ["psA3"])
                  for hh in range(4):
                      V(lambda e, hh=hh: e.scalar_tensor_tensor(
                          out=o_cat[:, hh * 64:(hh + 1) * 64], in0=psA[:, 1536 + hh * 64:1536 + (hh + 1) * 64],
                          scalar=cbT[:, hh:hh + 1], in1=ua[:, hh * 64:(hh + 1) * 64], op0=ALU.add, op1=ALU.mult),
                        ["psA3", "cbT", "ua"], ["o_cat"])

                  rels = []
                  if i > 0:
                      rels.append((i - 1, 3))
                  rels.append((i, 0 if i == 0 else (2 if i == nb - 1 else 1)))
                  if i < nb - 1:
                      rels.append((i + 1, 4))

                  def mmD(e, rels=rels):
                      ins = None
                      for g in range(4):
                          for n_, (j, v) in enumerate(rels):
                              ins = e.matmul(psB[0:64, 512 + g * 128:512 + (g + 1) * 128],
                                             lhsT=z_all[:, j, g * 64:(g + 1) * 64], rhs=poolat[:, v, g, :],
                                             start=(n_ == 0), stop=(n_ == len(rels) - 1))
                      return ins
                  PE(mmD, ["z_all", "poolat"], ["psB1"])
                  ACT(lambda e: e.copy(out=pooledT.rearrange("p g t -> p (g t)"), in_=psB[0:64, 512:1024]),
                      ["psB1"], ["pooledT"])

                  def mmD2(e):
                      ins = None
                      for g in range(4):
                          ins = e.matmul(psA[:, 1792 + g * 64:1792 + (g + 1) * 64], lhsT=pooledT[:, g, :],
                                         rhs=poolw[:, g, :], start=True, stop=True)
                      return ins
                  PE(mmD2, ["pooledT", "poolw"], ["psA3"])
                  V(lambda e: e.tensor_tensor(out=o_cat[:, 768:1024], in0=psA[:, 1792:2048], in1=pscale[:], op=ALU.mult),
                    ["psA3", "pscale"], ["o_cat"])

                  if lat:
                      kblocks = [min(max(j, 0), nb - 1) for j in (i - 1, i, i + 1)] + [8, 9]
                      bvar = 0 if i == 0 else (2 if i == nb - 1 else 1)
                      cblocks = list(range(10))
                  else:
                      kblocks = [0, 1]
                      bvar = 0
                      cblocks = [0, 1]
                  NKB = len(kblocks) * 128
                  NKC = len(cblocks) * 128
                  csc = 32.0 ** -0.5
                  pieces = [(p0, min(512, NKC - p0)) for p0 in range(0, NKC, 512)]
                  w_bfB = hT.rearrange("p k t -> p (k t)")
                  wTB = h_bf[:, 0:640].rearrange("p (b q) -> p b q", q=128)

                  def pv_gen(blocks, wsrc, wkey, wTd, wTkey, vfn, vkey, out_ps, out_key):
                      n = len(blocks)
                      for r0 in range(0, n, 8):
                          grp = list(range(r0, min(n, r0 + 8)))

                          def tr(e, grp=grp, r0=r0):
                              ins = None
                              for n_ in grp:
                                  ins = e.transpose(out=psT[:, (n_ - r0) * 128:(n_ - r0 + 1) * 128],
                                                    in_=wsrc[:, n_ * 128:(n_ + 1) * 128], identity=ident_b[:])
                              return ins
                          PE(tr, [wkey, "ident_b"], ["psT"])
                          ACT(lambda e, grp=grp, r0=r0: e.copy(
                              out=wTd[:, r0:r0 + len(grp), :].rearrange("p b q -> p (b q)"),
                              in_=psT[:, 0:len(grp) * 128]), ["psT"], [wTkey])
                          yield

                      def pv(e):
                          ins = None
                          for n_, jb in enumerate(blocks):
                              ins = e.matmul(out_ps, lhsT=wTd[:, n_, :], rhs=vfn(jb), start=(n_ == 0), stop=(n_ == n - 1))
                          return ins
                      PE(pv, [wTkey, vkey], [out_key])
                      yield

                  def chainB(kblocks=kblocks, bvar=bvar, NKB=NKB, lat=lat):
                      for hh in range(4):
                          kv, g = hh // 2, hh % 2

                          def mmS(e, kv=kv, g=g):
                              ins = None
                              for n_, jb in enumerate(kblocks):
                                  dst = psA[:, 1536 + n_ * 128:1536 + (n_ + 1) * 128] if n_ < 4 else psB[:, 512:640]
                                  ins = e.matmul(dst, lhsT=qTb[kv * 64:(kv + 1) * 64, g, :],
                                                 rhs=kTb_all[kv * 64:(kv + 1) * 64, jb * 128:(jb + 1) * 128],
                                                 start=True, stop=True)
                              return ins
                          PE(mmS, ["qTb", "kTb_all"], ["psA3", "psB1"] if lat else ["psA3"])
                          yield
                          if lat:
                              V(lambda e: e.scalar_tensor_tensor(out=scrB[:, 0:384], in0=psA[:, 1536:1920], scalar=0.125,
                                                                 in1=band[:, bvar, :], op0=ALU.mult, op1=ALU.add),
                                ["psA3", "band"], ["scrB"])
                              V(lambda e: e.tensor_scalar(out=scrB[:, 384:512], in0=psA[:, 1920:2048], scalar1=0.125,
                                                          scalar2=None, op0=ALU.mult), ["psA3"], ["scrB"])
                              V(lambda e: e.tensor_scalar(out=scrB[:, 512:640], in0=psB[:, 512:640], scalar1=0.125,
                                                          scalar2=None, op0=ALU.mult), ["psB1"], ["scrB"])
                          else:
                              V(lambda e: e.tensor_scalar(out=scrB[:, 0:256], in0=psA[:, 1536:1792], scalar1=0.125,
                                                          scalar2=None, op0=ALU.mult), ["psA3"], ["scrB"])
                          yield
                          V(lambda e: e.reduce_max(out=sm[:, 8:9], in_=scrB[:, 0:NKB], axis=AX.X), ["scrB"], ["sm8"])
                          V(lambda e, hh=hh: e.tensor_scalar(out=sm[:, 9:10], in0=sm[:, 8:9], scalar1=sink[:, hh:hh + 1],
                                                             scalar2=-1.0, op0=ALU.max, op1=ALU.mult), ["sm8", "sink"], ["sm9"])
                          yield
                          ACT(lambda e: e.activation(out=scrB[:, 0:NKB], in_=scrB[:, 0:NKB], func=AF.Exp,
                                                     bias=sm[:, 9:10], accum_out=sm[:, 10:11]),
                              ["scrB", "sm9"], ["scrB", "sm10"])
                          ACT(lambda e, hh=hh: e.activation(out=sm[:, 11:12], in_=sink[:, hh:hh + 1], func=AF.Exp,
                                                            bias=sm[:, 9:10]), ["sink", "sm9"], ["sm11"])
                          yield
                          V(lambda e: e.tensor_tensor(out=sm[:, 12:13], in0=sm[:, 10:11], in1=sm[:, 11:12], op=ALU.add),
                            ["sm10", "sm11"], ["sm12"])
                          V(lambda e: e.reciprocal(out=sm[:, 13:14], in_=sm[:, 12:13]), ["sm12"], ["sm13"])
                          V(lambda e: e.tensor_scalar(out=w_bfB[:, 0:NKB], in0=scrB[:, 0:NKB], scalar1=sm[:, 13:14],
                                                      scalar2=None, op0=ALU.mult), ["scrB", "sm13"], ["hT"])
                          yield
                          yield from pv_gen(kblocks, w_bfB, "hT", wTB, "h_bf",
                                            lambda jb, kv=kv: vb_all[:, jb, kv * 64:(kv + 1) * 64], "vb_all",
                                            psC[:, 256 + hh * 64:256 + (hh + 1) * 64], "psC")
                      V(lambda e: e.tensor_copy(out=o_cat[:, 256:512], in_=psC[:, 256:512]), ["psC"], ["o_cat"])
                      yield

                  def chainC(cblocks=cblocks, NKC=NKC, pieces=pieces):
                      for hh in range(4):
                          for j in range(2):
                              ej = scr0 if j == 0 else scr1

                              def mmC(e, hh=hh, j=j):
                                  ins = None
                                  for (p0, pn) in pieces:
                                      ins = e.matmul(psA[:, p0:p0 + pn], lhsT=qTc[j * 32:(j + 1) * 32, hh, :],
                                                     rhs=kTc_all[j * 32:(j + 1) * 32, hh, p0:p0 + pn], start=True, stop=True)
                                  return ins
                              pk = ["psA0", "psA1", "psA2"][:len(pieces)]
                              PE(mmC, ["qTc", "kTc_all"], pk)
                              yield
                              V(lambda e, j=j: e.reduce_max(out=sm[:, 14 + j:15 + j], in_=psA[:, 0:NKC], axis=AX.X),
                                pk, [f"smx{j}"])
                              V(lambda e, j=j: e.tensor_scalar(out=sm[:, 16 + j:17 + j], in0=sm[:, 14 + j:15 + j],
                                                               scalar1=-csc, scalar2=None, op0=ALU.mult),
                                [f"smx{j}"], [f"smn{j}"])
                              yield
                              ACT(lambda e, j=j, ej=ej: e.activation(out=ej[:, 0:NKC], in_=psA[:, 0:NKC], func=AF.Exp,
                                                                     bias=sm[:, 16 + j:17 + j], scale=csc,
                                                                     accum_out=sm[:, 18 + j:19 + j]),
                                  pk + [f"smn{j}"], ["scr0" if j == 0 else "scr1o", f"sms{j}"])
                              yield
                          V(lambda e: e.reciprocal(out=sm[:, 20:22], in_=sm[:, 18:20]), ["sms0", "sms1"], ["smr"])
                          V(lambda e: e.tensor_tensor(out=sm[:, 22:23], in0=sm[:, 21:22], in1=lamv[:, 5:6], op=ALU.mult),
                            ["smr", "nlam"], ["smc1"])
                          yield
                          V(lambda e: e.tensor_scalar(out=scr0[:, 0:NKC], in0=scr0[:, 0:NKC], scalar1=sm[:, 20:21],
                                                      scalar2=None, op0=ALU.mult), ["scr0", "smr"], ["scr0"])
                          yield
                          V(lambda e: e.scalar_tensor_tensor(out=w_bf[:, 0:NKC], in0=scr1[:, 0:NKC], scalar=sm[:, 22:23],
                                                             in1=scr0[:, 0:NKC], op0=ALU.mult, op1=ALU.add),
                            ["scr0", "scr1o", "smc1"], ["w_bf"])
                          yield
                          yield from pv_gen(cblocks, w_bf, "w_bf", wT, "wT",
                                            lambda jb, hh=hh: vc_all[:, jb, hh * 64:(hh + 1) * 64], "vc_all",
                                            psC[:, hh * 64:(hh + 1) * 64], "psC")
                      oc = scr0[:, 0:256]
                      V(lambda e: e.tensor_copy(out=oc, in_=psC[:, 0:256]), ["psC"], ["scr0"])
                      V(lambda e: e.tensor_tensor(out=scr0[:, 256:512], in0=oc, in1=oc, op=ALU.mult), ["scr0"], ["scr0"])
                      V(lambda e: e.reduce_sum(out=sm[:, 24:28], in_=scr0[:, 256:512].rearrange("p (h d) -> p h d", d=64),
                                               axis=AX.X), ["scr0"], ["smq"])
                      V(lambda e: e.tensor_scalar(out=sm[:, 24:28], in0=sm[:, 24:28], scalar1=1.0 / 64.0, scalar2=LN_EPS,
                                                  op0=ALU.mult, op1=ALU.add), ["smq"], ["smq"])
                      yield
                      ACT(lambda e: e.activation(out=sm[:, 24:28], in_=sm[:, 24:28], func=AF.Sqrt), ["smq"], ["smq"])
                      V(lambda e: e.reciprocal(out=sm[:, 24:28], in_=sm[:, 24:28]), ["smq"], ["smq"])
                      V(lambda e: e.tensor_tensor(out=scr0[:, 0:256].rearrange("p (h d) -> p h d", d=64),
                                                  in0=scr0[:, 0:256].rearrange("p (h d) -> p h d", d=64),
                                                  in1=sm[:, 24:28].unsqueeze(2).to_broadcast([128, 4, 64]), op=ALU.mult),
                        ["scr0", "smq"], ["scr0"])
                      V(lambda e: e.tensor_tensor(out=o_cat[:, 512:768].rearrange("p (h d) -> p h d", d=64),
                                                  in0=scr0[:, 0:256].rearrange("p (h d) -> p h d", d=64),
                                                  in1=subg[:].unsqueeze(1).to_broadcast([128, 4, 64]), op=ALU.mult),
                        ["scr0", "subg"], ["o_cat"])
                      yield

                  chains = [chainC(), chainB()] if CFG.get('ilv', 1) else []
                  if not CFG.get('ilv', 1):
                      for _ in chainC():
                          pass
                      for _ in chainB():
                          pass
                  while chains:
                      for g_ in list(chains):
                          try:
                              next(g_)
                          except StopIteration:
                              chains.remove(g_)
                  if l == 0 and t == 0:
                      dbg("ocat", o_cat, ["o_cat"])

                  transpose8(o_cat, "o_cat", hT, "hT")

                  def mmO(e):
                      ins = None
                      for n_ in range(2):
                          for k in range(8):
                              ins = e.matmul(psA[:, n_ * 512:(n_ + 1) * 512], lhsT=hT[:, k, :],
                                             rhs=w_out_sb[:, k, n_ * 512:(n_ + 1) * 512], start=(k == 0), stop=(k == 7))
                      return ins
                  PE(mmO, ["hT", "w_out"], ["psA0", "psA1"])
                  post_ln(t, psA[:, 0:512], "psA0", psA[:, 512:1024], "psA1", which, scr0, "scr0")
          if l == 0:
              dbg("x1", x[:, 0, :], ["x0"])

          ck("mix")
          barrier()
          for k in range(0, 8, 2):
              DMA("pool", wq_sb[:, k:k + 2, :], d_wq[l][k * 128:(k + 2) * 128, :].rearrange("(k p) n -> p k n", p=128),
                  w=["wq"])
          DMA("sp", lnp[:, 0, :], bcast_row(h_lng, (l * 2 + 1) * D, D), w=["lnp"])
          DMA("sp", lnp[:, 1, :], bcast_row(h_lnb, (l * 2 + 1) * D, D), w=["lnp"])
          for i in range(2):
              V(lambda e, i=i: e.memset(bigf[i], 0.0), [], [f"bg{i}_{q_}" for q_ in range(64)])
          mod_half(l, 1, p_wmb, "wmb", p_bmb, "bmb")

          def stage_A(i):
              t = i
              which = 0 if t < 4 else 1
              hb = h2[i % 2]
              hkey = f"h2_{i % 2}"
              slot = i % 2
              ln_mod(t, which, 0, 1024, hb, hkey, e_scr0, "scr0")
              yield
              transpose8(hb, hkey, h2T, "h2T")
              yield
              for half in range(2):
                  def mq(e, half=half):
                      ins = None
                      for c in range(4):
                          cc = half * 4 + c
                          for k in range(8):
                              ins = e.matmul(psA[:, c * 128:(c + 1) * 128], lhsT=wq_sb[:, k, cc * 128:(cc + 1) * 128],
                                             rhs=h2T[:, k, :], start=(k == 0), stop=(k == 7))
                      return ins
                  PE(mq, ["h2T", "wq"], ["psA0"])
                  ACT(lambda e, half=half: e.copy(out=qT_sb[:, half * 4:half * 4 + 4, :].rearrange("p c t -> p (c t)"),
                                                  in_=psA[:, 0:512]), ["psA0"], ["qT_sb"])
                  yield
              V(lambda e: e.memset(idxf, 0.0), [], ["idxf"])
              for hq in range(4):
                  def ms(e, hq=hq):
                      ins = None
                      for hh in range(2):
                          ins = e.matmul(psA[:, hh * 256:(hh + 1) * 256], lhsT=qT_sb[:, 2 * hq + hh, :],
                                         rhs=kb_b[:], start=True, stop=True)
                      return ins
                  PE(ms, ["qT_sb", "kb_b"], ["psA0"])
                  ACT(lambda e: e.copy(out=sc_sb, in_=psA[:, 0:512]), ["psA0"], ["sc_sb"])
                  yield
                  for g in range(4):
                      scg = sc_sb[:, g * 128:(g + 1) * 128]
                      V(lambda e, g=g, scg=scg: e.max(out=sv[:, g, 0:8], in_=scg), ["sc_sb"], ["sv"])
                      V(lambda e, g=g, scg=scg: e.max_index(out=si_u[:, g, 0:8], in_max=sv[:, g, 0:8], in_values=scg),
                        ["sc_sb", "sv"], ["si_u"])
                      V(lambda e, g=g, scg=scg: e.match_replace(out=scr_mr[:, 0:128], in_to_replace=sv[:, g, 0:8],
                                                                in_values=scg, imm_value=-1e30), ["sc_sb", "sv"], ["scr_mr"])
                      V(lambda e, g=g: e.max(out=sv[:, g, 8:16], in_=scr_mr[:, 0:128]), ["scr_mr"], ["sv"])
                      V(lambda e, g=g: e.max_index(out=si_u[:, g, 8:16], in_max=sv[:, g, 8:16], in_values=scr_mr[:, 0:128]),
                        ["scr_mr", "sv"], ["si_u"])
                      yield
                  V(lambda e: e.tensor_copy(out=sif, in_=si_u[:]), ["si_u"], ["sif"])
                  NC_ = 112
                  for hh in range(2):
                      s0, s1 = sv[:, 2 * hh, :], sv[:, 2 * hh + 1, :]
                      f0, f1 = sif[:, 2 * hh, :], sif[:, 2 * hh + 1, :]
                      cA = cand[:, hh, 0:64].rearrange("p (a b) -> p a b", b=16)
                      cB = cand[:, hh, 64:112].rearrange("p (a b) -> p a b", b=4)
                      eA = eid[:, hh, 0:64].rearrange("p (a b) -> p a b", b=16)
                      eB = eid[:, hh, 64:112].rearrange("p (a b) -> p a b", b=4)
                      V(lambda e, cA=cA, s0=s0, s1=s1: e.tensor_tensor(
                          out=cA, in0=s0[:, 0:4].unsqueeze(2).to_broadcast([128, 4, 16]),
                          in1=s1.unsqueeze(1).to_broadcast([128, 4, 16]), op=ALU.add), ["sv"], ["cand"])
                      V(lambda e, cB=cB, s0=s0, s1=s1: e.tensor_tensor(
                          out=cB, in0=s0[:, 4:16].unsqueeze(2).to_broadcast([128, 12, 4]),
                          in1=s1[:, 0:4].unsqueeze(1).to_broadcast([128, 12, 4]), op=ALU.add), ["sv"], ["cand"])
                      V(lambda e, eA=eA, f0=f0, f1=f1: e.scalar_tensor_tensor(
                          out=eA, in0=f0[:, 0:4].unsqueeze(2).to_broadcast([128, 4, 16]), scalar=128.0,
                          in1=f1.unsqueeze(1).to_broadcast([128, 4, 16]), op0=ALU.mult, op1=ALU.add), ["sif"], ["eid"])
                      V(lambda e, eB=eB, f0=f0, f1=f1: e.scalar_tensor_tensor(
                          out=eB, in0=f0[:, 4:16].unsqueeze(2).to_broadcast([128, 12, 4]), scalar=128.0,
                          in1=f1[:, 0:4].unsqueeze(1).to_broadcast([128, 12, 4]), op0=ALU.mult, op1=ALU.add),
                        ["sif"], ["eid"])
                      yield
                  for hh in range(2):
                      H = 2 * hq + hh
                      V(lambda e, hh=hh, H=H: e.max(out=cvv[:, H, 0:8], in_=cand[:, hh, 0:NC_]), ["cand"], ["cvv"])
                      V(lambda e, hh=hh, H=H: e.match_replace(out=scr_mr[:, 0:NC_], in_to_replace=cvv[:, H, 0:8],
                                                              in_values=cand[:, hh, 0:NC_], imm_value=-1e30),
                        ["cand", "cvv"], ["scr_mr"])
                      V(lambda e, H=H: e.max(out=cvv[:, H, 8:16], in_=scr_mr[:, 0:NC_]), ["scr_mr"], ["cvv"])
                      for k in range(16):
                          V(lambda e, hh=hh, H=H, k=k: e.scalar_tensor_tensor(
                              out=e_scr1[:, 0:NC_], in0=cand[:, hh, 0:NC_], scalar=cvv[:, H, k:k + 1],
                              in1=eid[:, hh, 0:NC_], op0=ALU.is_equal, op1=ALU.mult,
                              accum_out=idxf[:, H * 16 + k:H * 16 + k + 1]),
                            ["cand", "eid", "cvv"], ["idxf"])
                          if k % 4 == 3:
                              yield
              yield
              V(lambda e: e.tensor_tensor(out=gat, in0=cvv, in1=cvv[:, :, 0:1].to_broadcast([128, 8, 16]), op=ALU.subtract),
                ["cvv"], ["gat"])
              ACT(lambda e: e.activation(out=gat, in_=gat, func=AF.Exp), ["gat"], ["gat"])
              V(lambda e: e.reduce_sum(out=sm[:, 8:16], in_=gat, axis=AX.X), ["gat"], ["smg"])
              V(lambda e: e.reciprocal(out=sm[:, 8:16], in_=sm[:, 8:16]), ["smg"], ["smg"])
              V(lambda e: e.tensor_tensor(out=gat, in0=gat, in1=sm[:, 8:16].unsqueeze(2).to_broadcast([128, 8, 16]),
                                          op=ALU.mult), ["gat", "smg"], ["gat"])
              V(lambda e: e.tensor_scalar(out=idxf, in0=idxf, scalar1=float(NEXP - 1), scalar2=0.0, op0=ALU.min, op1=ALU.max),
                ["idxf"], ["idxf"])
              V(lambda e, l=l: e.tensor_scalar(out=idxf, in0=idxf, scalar1=float(CFG.get("oob_add", 0)), scalar2=None, op0=ALU.add),
                ["idxf"], ["idxf"])

              def trI(e):
                  e.transpose(out=psA[:, 0:128], in_=idxf, identity=ident_f[:])
                  return e.transpose(out=psA[:, 128:256], in_=gat.rearrange("p h k -> p (h k)"), identity=ident_f[:])
              PE(trI, ["idxf", "gat", "ident_f"], ["psA0"])
              V(lambda e: e.tensor_copy(out=idx_i[:, slot, :], in_=psA[:, 0:128]), ["psA0"], [f"idx{slot}"])
              V(lambda e: e.tensor_copy(out=gT[slot], in_=psA[:, 128:256]), ["psA0"], [f"gT{slot}"])

          gcount = [0]
          ac = aT[0].rearrange("p a t -> p (a t)")[:, 0:32].rearrange("p (j c) -> p j c", c=4)
          bigdiag = [bigf[k_].rearrange("p (a b) -> p a b", b=65)[:, :, 0] for k_ in range(2)]

          tokst = {}
          XB = [(psB[:, 0:512], "psB0"), (psB[:, 512:1024], "psB1"), (psA[:, 512:1024], "psA1"), (psA[:, 1536:2048], "psA3")]

          def stage_S1(i, tt, tab=d_tab16):
              slot = i % 2
              hb, hkey = h2[i % 2], f"h2_{i % 2}"
              n_ = gcount[0]
              gcount[0] += 1
              b = n_ % NGB
              j = n_ % 8
              tokst[(i, tt)] = (b, j)
              S.dma("pool", lambda e: e.indirect_dma_start(
                  out=gbuf[b], out_offset=None, in_=tab,
                  in_offset=bass.IndirectOffsetOnAxis(ap=idx_i[:, slot, tt:tt + 1], axis=0),
                  bounds_check=REGS["bc"], oob_is_err=False), [f"idx{slot}"] + [f"tab16_{q_}" for q_ in range((CFG["nexp"] + 1023) // 1024)], [f"gb{b}"], SEM_U[b])
              for hf in range(2):
                  xb, xk = XB[2 * (n_ % 2) + hf]
                  PE(lambda e, hf=hf, xb=xb: e.matmul(xb, lhsT=ident_b[:, tt:tt + 1].to_broadcast([128, 128]),
                                                      rhs=hb[:, hf * 512:(hf + 1) * 512], start=True, stop=True),
                     [hkey, "ident_b"], [xk])
                  V(lambda e, hf=hf, xb=xb: e.scalar_tensor_tensor(
                      out=junk[:, hf * 512:(hf + 1) * 512], in0=gbuf[b][:, hf * 512:(hf + 1) * 512], scalar=1.0,
                      in1=xb, op0=ALU.mult, op1=ALU.mult,
                      accum_out=ac[:, j, hf:hf + 1]), [f"gb{b}", xk], [f"ac{j}_{hf}"])
              ACT(lambda e: e.activation(out=ac[:, j, 3:4], in_=ac[:, j, 0:1], func=AF.Gelu_apprx_tanh,
                                         bias=ac[:, j, 1:2]), [f"ac{j}_0", f"ac{j}_1"], [f"ac{j}"])
              sbk, t6 = tt // 64, tt % 64
              ACT(lambda e: e.activation(out=bigdiag[sbk][:, t6:t6 + 1], in_=ac[:, j, 3:4], func=AF.Copy,
                                         scale=gT[slot][:, tt:tt + 1]), [f"ac{j}", f"gT{slot}"], [f"bg{sbk}_{t6}"])

          def stage_S23(i, tt):
              slot = i % 2
              b, j = tokst.pop((i, tt))
              sbk, t6 = tt // 64, tt % 64
              bkey = f"bg{sbk}_{t6}"
              lhs = bigf[sbk][:, 0:4096].rearrange("p (t m) -> p t m", m=64)[:, t6, :]

              def mv(e):
                  e.matmul(psC[sbk * 64:(sbk + 1) * 64, :], lhsT=lhs, rhs=gbuf[b][:, 1024:1536],
                           start=(t6 == 0), stop=(t6 == 63))
                  return e.matmul(psA[sbk * 64:(sbk + 1) * 64, 1024:1536], lhsT=lhs, rhs=gbuf[b][:, 1536:2048],
                                  start=(t6 == 0), stop=(t6 == 63))
              PE(mv, [bkey, f"gb{b}"], ["psC", "psA2"])

          def stage_E(i):
              which = 0 if i < 4 else 1
              post_ln(i, psC[:, 0:512], "psC", psA[:, 1024:1536], "psA2", which, e_scr0, "scr0")

          ck("mod2")
          for _ in stage_A(0):
              pass
          ck("A")
          toks = [(i, tt) for i in range(NT) for tt in range(128)]
          stage_S1(*toks[0])
          stage_S1(*toks[1])
          genA = iter(())
          for n_t, (i, tt) in enumerate(toks):
              if tt == 0:
                  genA = stage_A(i + 1) if i + 1 < NT else iter(())
              if tt == 110:
                  for _ in genA:
                      pass
              if n_t + 2 < len(toks):
                  stage_S1(*toks[n_t + 2])
              stage_S23(i, tt)
              next(genA, None)
              if tt == 127:
                  stage_E(i)
          if l == 0:
              dbg("x2", x[:, 0, :], ["x0"])
          barrier()

    except _Stop:
        pass

    for t in range(NT):
        DMA("sp", o_y[t], x[:, t, :], r=[f"x{t}"])

    import contextlib
    sems = {}
    es = contextlib.ExitStack()
    for e_ in S.ENG:
        sems[("e", e_)] = es.enter_context(nc.semaphore(f"s_{e_}"))
    for i_ in range(len(S.dma_n)):
        sems[("d", i_)] = es.enter_context(nc.semaphore(f"d_{i_}"))
    with nc.Block() as block:

        def sem_of(key):
            return sems[key]

        def replay(name, eng):
            for (waits, fn, me, inc) in S.q[name]:
                for (s, v) in waits:
                    eng.wait_ge(sem_of(s), v)
                ins = fn(eng)
                ins.then_inc(sem_of(me[0]), inc)
            for (s, v) in S.final_waits(name):
                eng.wait_ge(sem_of(s), v)

        @block.tensor
        def _(e):
            replay("pe", e)

        @block.scalar
        def _(e):
            replay("act", e)

        @block.vector
        def _(e):
            replay("dve", e)

        @block.gpsimd
        def _(e):
            REGS["bc"] = e.alloc_register("bc")
            e.reg_mov(REGS["bc"], NEXP - 1)
            replay("pool", e)

        @block.sync
        def _(e):
            replay("sp", e)
    es.close()
    return nc


def _prep_inputs(inp):
    f = lambda a: np.ascontiguousarray(np.asarray(a, dtype=np.float32))
    x_prompt, x_sample, c = f(inp["x_prompt"]), f(inp["x_sample"]), f(inp["c"])
    c_ctx = f(inp["c_ctx"])
    cosb, sinb = _rope_tables(64)
    cosc, sinc = _rope_tables(32)
    shared = {
        "w_mod": f(inp["w_mod"]), "b_mod": f(inp["b_mod"]), "w_in": f(inp["w_in"]), "w_out": f(inp["w_out"]),
        "chunk_wT": f(np.transpose(f(inp["chunk_w"]), (0, 3, 1, 2))),
        "chunk_bT": f(np.transpose(f(inp["chunk_b"]), (0, 2, 1))),
        "win_sink": f(inp["win_sink"]),
        "lam_q": f(f(inp["diff_lam_q"]).reshape(DEPTH, 64)), "lam_k": f(f(inp["diff_lam_k"]).reshape(DEPTH, 64)),
        "subln_g": f(inp["diff_subln_g"]),
        "pool_wr": f(np.transpose(f(inp["pool_w"]), (0, 2, 1, 3))),
        "pool_scale": f(inp["pool_scale"]), "ln_g": f(inp["ln_g"]), "ln_b": f(inp["ln_b"]),
        "peer_wq": f(inp["peer_wq"]),
        "peer_uv": np.concatenate([f(inp["peer_u"]).reshape(DEPTH * NEXP, D), f(inp["peer_v"]).reshape(DEPTH * NEXP, D)], axis=1),
        "c_ident": np.eye(128, dtype=np.float32), "c_rb": _rot_matrix(64, 2), "c_rc": _rot_matrix(32, 2),
        "c_cosb": f(np.tile(cosb.T, (2, 1))), "c_sinb": f(np.tile(sinb.T, (2, 1))),
        "c_cosc": f(np.tile(cosc.T, (2, 1))), "c_sinc": f(np.tile(sinc.T, (2, 1))),
        "c_band": _band_bias(), "c_poolat": f(np.transpose(_pool_mats(), (2, 0, 1, 3))),
    }
    keys = f(inp["peer_keys"])
    kb = np.zeros((DEPTH, 128, 256), np.float32)
    for j in range(2):
        kb[:, j * 64:(j + 1) * 64, j * 128:(j + 1) * 128] = np.transpose(keys[:, j], (0, 2, 1))
    shared["peer_kb"] = kb
    cwk, cwv = f(inp["cache_win_k"]), f(inp["cache_win_v"])
    cdk, cdv = f(inp["cache_diff_k"]), f(inp["cache_diff_v"])
    maps = []
    for core in range(8):
        b = core // 4
        xin = np.concatenate([x_prompt[2 * core].reshape(2, 128, D), x_prompt[2 * core + 1].reshape(2, 128, D),
                              x_sample[b].reshape(8, 128, D)], 0)
        cvec = np.stack([c_ctx.reshape(8, 128).T, c[b].reshape(8, 128).T], 1)
        m = dict(shared)
        m["xin"] = f(xin)
        m["cvec"] = f(cvec)
        m["cwkT"] = f(np.transpose(cwk[b].reshape(DEPTH, 256, 128), (0, 2, 1)))
        m["cwv"] = f(cwv[b].reshape(DEPTH, 256, 128))
        m["cdkT"] = f(np.transpose(cdk[b].reshape(DEPTH, 256, 4, 64), (0, 3, 2, 1)))
        m["cdv"] = f(cdv[b].reshape(DEPTH, 256, 256))
        maps.append(m)
    return maps


_NC_CACHE = {}


def kernel(**inputs):
    maps = _prep_inputs(inputs)
    if "nc" not in _NC_CACHE:
        _NC_CACHE["nc"] = build_program()
    nc = _NC_CACHE["nc"]
    res = run_bass_kernel_spmd(nc, maps, core_ids=list(range(8)))
    R = res.results
    y_p = np.zeros((16, 256, D), np.float32)
    y_s = np.zeros((2, 1024, D), np.float32)
    nwk = np.zeros((16, DEPTH, 256, 2, 64), np.float32)
    nwv = np.zeros((16, DEPTH, 256, 2, 64), np.float32)
    ndk = np.zeros((16, DEPTH, 256, 4, 2, 32), np.float32)
    ndv = np.zeros((16, DEPTH, 256, 4, 64), np.float32)
    for core in range(8):
        y = np.asarray(R[core]["y"])
        for s in range(2):
            y_p[2 * core + s] = y[2 * s:2 * s + 2].reshape(256, D)
            nwk[2 * core + s] = np.asarray(R[core]["o_wk"])[s].reshape(DEPTH, 256, 2, 64)
            nwv[2 * core + s] = np.asarray(R[core]["o_wv"])[s].reshape(DEPTH, 256, 2, 64)
            ndk[2 * core + s] = np.asarray(R[core]["o_dk"])[s].reshape(DEPTH, 256, 4, 2, 32)
            ndv[2 * core + s] = np.asarray(R[core]["o_dv"])[s].reshape(DEPTH, 256, 4, 64)
        if core % 4 == 0:
            y_s[core // 4] = y[4:12].reshape(1024, D)
    kernel.last_results = R
    return (y_p, y_s, nwk, nwv, ndk, ndv)
```

```python
import math
import numpy as np
import concourse.bass as bass
import concourse.mybir as mybir
from concourse.bass_utils import run_bass_kernel_spmd

F32 = mybir.dt.float32
BF16 = mybir.dt.bfloat16
I32 = mybir.dt.int32
U32 = mybir.dt.uint32
AF = mybir.ActivationFunctionType
ALU = mybir.AluOpType
AX = mybir.AxisListType

DEPTH = 4
D = 1024
NT = 12
NEXP = 16384
ALPHA = (2 * DEPTH) ** 0.25
LN_EPS = 1e-5
NEG = -30000.0
NGB = 8
DEBUG = {}
CFG = {"depth": DEPTH, "stop": None, "nexp": NEXP}


class _Stop(Exception):
    pass


C_AU, C_AV, C_BQ, C_BK, C_BV, C_CQ, C_CK, C_CV, C_DZ = 0, 256, 512, 768, 896, 1024, 1280, 1536, 1792


class Sched:
    ENG = ("pe", "act", "dve", "pool", "sp")

    def __init__(self, n_dma_sems):
        self.q = {e: [] for e in self.ENG}
        self.cnt = {e: 0 for e in self.ENG}
        self.seen = {e: {} for e in self.ENG}
        self.last_w = {}
        self.readers = {}
        self.dma_n = [0] * n_dma_sems
        self.rr = 0
        self.n_general = n_dma_sems

    def _collect(self, eng, reads, writes, extra=(), raw=None):
        deps = {}
        own = ("e", eng)
        raw = set(reads if raw is None else raw)

        def add(d, same_ok):
            if d is None:
                return
            s, v = d
            if s == own and not same_ok:
                return
            if deps.get(s, 0) < v:
                deps[s] = v
        same_raw = eng != "pe"
        for k in reads:
            add(self.last_w.get(k), same_raw)
        for k in writes:
            add(self.last_w.get(k), same_raw and k in raw)
            for d in self.readers.get(k, {}).items():
                add(d, False)
        for d in extra:
            add(d, True)
        waits = []
        sn = self.seen[eng]
        for s, v in deps.items():
            if sn.get(s, 0) < v:
                sn[s] = v
                waits.append((s, v))
        return waits

    def _commit(self, me, reads, writes):
        for k in writes:
            self.last_w[k] = me
            self.readers[k] = {}
        for k in reads:
            r = self.readers.setdefault(k, {})
            if r.get(me[0], 0) < me[1]:
                r[me[0]] = me[1]

    @staticmethod
    def _norm(reads, writes):
        ps = [k for k in reads if k.startswith("ps")]
        if ps:
            reads = [k for k in reads if not k.startswith("ps")]
            writes = list(writes) + [k for k in ps if k not in writes]
        return reads, writes

    def op(self, eng, fn, reads=(), writes=(), extra=()):
        raw = list(reads)
        reads, writes = self._norm(reads, writes)
        waits = self._collect(eng, reads, writes, extra, raw)
        self.cnt[eng] += 1
        me = (("e", eng), self.cnt[eng])
        self.q[eng].append((waits, fn, me, 1))
        self._commit(me, reads, writes)

    def snapshot(self):
        snap = [(("e", e), self.cnt[e]) for e in self.ENG if self.cnt[e]]
        snap += [(("d", i), 16 * n) for i, n in enumerate(self.dma_n) if n]
        return snap

    def dma(self, eng, fn, reads=(), writes=(), sem=None):
        raw = list(reads)
        reads, writes = self._norm(reads, writes)
        if sem is None:
            sem = self.rr
            self.rr = (self.rr + 1) % self.n_general
        prev = (("d", sem), 16 * self.dma_n[sem]) if self.dma_n[sem] else None
        waits = self._collect(eng, reads, writes, extra=(prev,), raw=raw)
        self.dma_n[sem] += 1
        me = (("d", sem), 16 * self.dma_n[sem])
        self.q[eng].append((waits, fn, me, 16))
        self._commit(me, reads, writes)

    def final_waits(self, eng):
        waits = []
        for i, n in enumerate(self.dma_n):
            if n:
                waits.append((("d", i), 16 * n))
        for e in self.ENG:
            if e != eng and self.cnt[e]:
                waits.append((("e", e), self.cnt[e]))
        return waits


def _rope_tables(dim):
    n_tok, grid = 1024, 64
    rows = n_tok // grid
    row = np.repeat(np.arange(rows, dtype=np.float32), grid)
    col = np.tile(np.arange(grid, dtype=np.float32), rows)
    nf = dim // 4
    inv = (np.float32(10000.0) ** (-np.arange(nf, dtype=np.float32) / np.float32(nf))).astype(np.float32)
    ar = row[:, None] * inv
    ac = col[:, None] * inv
    ang = np.concatenate([ar, ar, ac, ac], axis=-1).astype(np.float32)
    return np.cos(ang).astype(np.float32), np.sin(ang).astype(np.float32)


def _rot_matrix(dim, reps):
    q = dim // 4
    r = np.zeros((dim, dim), np.float32)
    for a in range(q):
        r[q + a, a] = -1.0
        r[a, q + a] = 1.0
        r[3 * q + a, 2 * q + a] = -1.0
        r[2 * q + a, 3 * q + a] = 1.0
    out = np.zeros((dim * reps, dim * reps), np.float32)
    for i in range(reps):
        out[i * dim:(i + 1) * dim, i * dim:(i + 1) * dim] = r
    return out


def _pool_mats():
    S = 384
    out = np.zeros((5, 4, 128, 128), np.float32)
    for g, w in enumerate((2, 4, 8, 16)):
        def mat(seq_lo, seq_hi):
            m = np.zeros((S, S), np.float32)
            for t in range(128, 256):
                lo = min(max(t - w // 2, seq_lo), seq_hi)
                hi = min(max(t + w // 2, seq_lo), seq_hi)
                m[t, lo:hi] = 1.0 / float(hi - lo)
                m[t, t] -= 1.0
            return m
        mid = mat(0, S)
        first = mat(128, S)
        last = mat(0, 256)
        out[0, g] = first[128:256, 128:256].T
        out[1, g] = mid[128:256, 128:256].T
        out[2, g] = last[128:256, 128:256].T
        out[3, g] = mid[128:256, 0:128].T
        out[4, g] = mid[128:256, 256:384].T
    return out


def _band_bias():
    qi = np.arange(128)[:, None]
    ki = np.arange(128)[None, :]
    prev = np.where(ki >= qi, 0.0, NEG).astype(np.float32)
    nxt = np.where(ki <= qi, 0.0, NEG).astype(np.float32)
    z = np.zeros((128, 128), np.float32)
    full = np.full((128, 128), NEG, np.float32)
    return np.stack([np.concatenate([full, z, nxt], 1), np.concatenate([prev, z, nxt], 1),
                     np.concatenate([prev, z, full], 1)], 0)


def build_program():
    nc = bass.Bass("TRN2", target_bir_lowering=False)

    def din(name, shape, dt=F32):
        return nc.dram_tensor(name, list(shape), dt, kind="ExternalInput")

    def dout(name, shape, dt=F32):
        return nc.dram_tensor(name, list(shape), dt, kind="ExternalOutput")

    d_x = din("xin", [NT, 128, D]).ap()
    d_cvec = din("cvec", [128, 2, 8]).ap()
    d_cwkT = din("cwkT", [DEPTH, 128, 256]).ap()
    d_cwv = din("cwv", [DEPTH, 256, 128]).ap()
    d_cdkT = din("cdkT", [DEPTH, 64, 4, 256]).ap()
    d_cdv = din("cdv", [DEPTH, 256, 256]).ap()
    d_wmod = din("w_mod", [CFG["depth"], D, 6 * D]).ap()
    h_bmod = din("b_mod", [DEPTH, 6 * D])
    d_win = din("w_in", [CFG["depth"], D, 2048]).ap()
    d_wout = din("w_out", [CFG["depth"], D, D]).ap()
    d_cwT = din("chunk_wT", [DEPTH, 128, 4, 128]).ap()
    d_cbT = din("chunk_bT", [DEPTH, 128, 4]).ap()
    h_sink = din("win_sink", [DEPTH, 4])
    h_lamq = din("lam_q", [DEPTH, 64])
    h_lamk = din("lam_k", [DEPTH, 64])
    h_subg = din("subln_g", [DEPTH, 64])
    d_poolw = din("pool_wr", [DEPTH, 64, 4, 64]).ap()
    h_pscale = din("pool_scale", [DEPTH, 256])
    h_lng = din("ln_g", [DEPTH, 2, D])
    h_lnb = din("ln_b", [DEPTH, 2, D])
    d_wq = din("peer_wq", [CFG["depth"], D, D]).ap()
    d_kb = din("peer_kb", [DEPTH, 128, 256]).ap()
    d_puv = din("peer_uv", [CFG["depth"] * CFG["nexp"], 2 * D]).ap()
    d_ident = din("c_ident", [128, 128]).ap()
    d_rb = din("c_rb", [128, 128]).ap()
    d_rc = din("c_rc", [64, 64]).ap()
    d_cosb = din("c_cosb", [128, 1024]).ap()
    d_sinb = din("c_sinb", [128, 1024]).ap()
    d_cosc = din("c_cosc", [64, 1024]).ap()
    d_sinc = din("c_sinc", [64, 1024]).ap()
    d_band = din("c_band", [3, 128, 384]).ap()
    d_poolat = din("c_poolat", [128, 5, 4, 128]).ap()

    d_tab16 = nc.dram_tensor("tab16", [NEXP, 2 * D], BF16, kind="Internal").ap()
    o_y = dout("y", [NT, 128, D]).ap()
    o_wk = dout("o_wk", [2, DEPTH, 256, 128]).ap()
    o_wv = dout("o_wv", [2, DEPTH, 256, 128]).ap()
    o_dk = dout("o_dk", [2, DEPTH, 256, 256]).ap()
    o_dv = dout("o_dv", [2, DEPTH, 256, 256]).ap()
    dbg_out = {}
    for name, shp in DEBUG.items():
        dbg_out[name] = dout("dbg_" + name, shp).ap()

    def bcast_row(h, off, n, parts=128):
        return bass.AP(h, off, [[0, parts], [1, n]])

    def sb(name, shape, dt):
        return nc.alloc_sbuf_tensor(name, list(shape), dt)

    x = sb("x", [128, NT, D], F32)
    mod = sb("mod", [128, 2, 3072], F32)
    reg1 = sb("reg1", [128, 24576], BF16)
    A16N, A32N = 21760, 6784
    a16 = sb("a16", [128, A16N], BF16)
    a32 = sb("a32", [128, A32N], F32)
    idx_i = sb("idx_i", [128, 3, 128], I32)
    si_u = sb("si_u", [128, 4, 16], U32)
    ident_f = sb("ident_f", [128, 128], F32)
    ident_b = sb("ident_b", [128, 128], BF16)
    rb_f = sb("rb_f", [128, 128], F32)
    rc_f = sb("rc_f", [64, 64], F32)
    band = sb("band", [128, 3, 384], BF16)
    poolat = sb("poolat", [128, 5, 4, 128], BF16)
    silu_b = sb("silu_b", [128, 2, 8], BF16)
    cvec = sb("cvec_sb", [128, 2, 8], F32)
    cwT = sb("cwT", [128, 4, 128], BF16)
    cbT = sb("cbT", [128, 4], F32)
    sink = sb("sink", [128, 4], F32)
    lamt = sb("lamt", [128, 2, 64], F32)
    lamv = sb("lamv", [128, 8], F32)
    subg = sb("subg", [128, 64], F32)
    poolw = sb("poolw", [64, 4, 64], BF16)
    pscale = sb("pscale", [128, 256], F32)
    kb_b = sb("kb_b", [128, 256], BF16)
    st6 = sb("st6", [128, 2, 6], F32)
    sm = sb("sm", [128, 32], F32)
    scrB = sb("scrB", [128, 640], F32)

    psA = nc.alloc_psum_tensor("psA", [128, 2048], F32)
    psB = nc.alloc_psum_tensor("psB", [128, 1024], F32)
    psC = nc.alloc_psum_tensor("psC", [128, 512], F32)
    psT = nc.alloc_psum_tensor("psT", [128, 1024], BF16)

    class Carver:
        def __init__(self, t, n):
            self.t, self.n, self.o = t, n, 0

        def take(self, size, pat=None, parts=128, **kw):
            assert self.o + size <= self.n, (self.o, size, self.n)
            ap = self.t[0:parts, self.o:self.o + size]
            self.o += size
            if pat:
                ap = ap.rearrange(pat, **kw)
            return ap

    w_in_sb = reg1[:, 0:16384].rearrange("p (k n) -> p k n", n=2048)
    w_out_sb = reg1[:, 16384:24576].rearrange("p (k n) -> p k n", n=1024)
    wq_sb = reg1[:, 0:8192].rearrange("p (k n) -> p k n", n=1024)
    gbuf = [reg1[:, 8192 + i * 2048: 8192 + (i + 1) * 2048] for i in range(NGB)]

    c16 = Carver(a16, A16N)
    h_bf = c16.take(1024)
    hT = c16.take(1024, "p (k t) -> p k t", t=128)
    z_all = c16.take(2048, "p (b c) -> p b c", c=256)
    kTb_all = c16.take(1280)
    vb_all = c16.take(1280, "p (b c) -> p b c", c=128)
    kTc_all = c16.take(5120, "p (h s) -> p h s", parts=64, s=1280)
    vc_all = c16.take(2560, "p (b c) -> p b c", c=256)
    ua = c16.take(512)
    qTb = c16.take(256, "p (c t) -> p c t", t=128)
    qTc = c16.take(512, "p (c t) -> p c t", parts=64, t=128)
    w_bf = c16.take(1280)
    wT = c16.take(1280, "p (b q) -> p b q", q=128)
    o_cat = c16.take(1024)
    pooledT = c16.take(512, "p (g t) -> p g t", parts=64, t=128)
    wmb = [c16.take(1024, "p (k n) -> p k n", n=128) for _ in range(2)]
    c32 = Carver(a32, A32N)
    scr0 = c32.take(1280)
    scr1 = c32.take(1408)
    csB = c32.take(256, "p (a t) -> p a t", t=128)
    csC = c32.take(256, "p (a t) -> p a t", parts=64, t=128)
    bmb = [c32.take(128) for _ in range(2)]
    assert c32.o == 3456
    lnp = c32.take(2048, "p (a n) -> p a n", n=1024)
    p16 = Carver(a16, A16N)
    h2 = [p16.take(1024) for _ in range(2)]
    h2T = p16.take(1024, "p (k t) -> p k t", t=128)
    qT_sb = p16.take(1024, "p (c t) -> p c t", t=128)
    junk = p16.take(1024)
    w_pb = [p16.take(128) for _ in range(2)]
    BIGN = 64 * 65
    bigf = [p16.take(BIGN) for _ in range(2)]
    p_wmb = [p16.take(1024, "p (k n) -> p k n", n=128) for _ in range(6)]
    p32 = Carver(a32, A32N)
    e_scr0 = p32.take(1024)
    e_scr1 = p32.take(256)
    sc_sb = p32.take(512)
    scr_mr = p32.take(256)
    sv = p32.take(64, "p (g k) -> p g k", k=16)
    sif = p32.take(64, "p (g k) -> p g k", k=16)
    cand = p32.take(512, "p (h j) -> p h j", j=256)
    eid = p32.take(512, "p (h j) -> p h j", j=256)
    cvv = p32.take(128, "p (h k) -> p h k", k=16)
    gat = p32.take(128, "p (h k) -> p h k", k=16)
    assert p32.o == 3456
    p32.o = 5504
    idxf = p32.take(128)
    gT = [p32.take(128) for _ in range(3)]
    aT = [p32.take(256, "p (a t) -> p a t", t=128) for _ in range(2)]
    p_bmb = [p32.take(128) for _ in range(2)]

    n_general = 24
    REGS = {}
    S = Sched(n_general + 2 * NGB + 4)
    SEM_U = [n_general + i for i in range(NGB)]
    SEM_V = [n_general + NGB + i for i in range(NGB)]

    def V(fn, r=(), w=()):
        S.op("dve", fn, r, w)

    def ACT(fn, r=(), w=()):
        S.op("act", fn, r, w)

    def PE(fn, r=(), w=()):
        S.op("pe", fn, r, w)

    def POOL(fn, r=(), w=()):
        S.op("pool", fn, r, w)

    def DMA(eng, out, in_, r=(), w=(), sem=None):
        S.dma(eng, lambda e: e.dma_start(out=out, in_=in_), r, w, sem)

    def barrier():
        snap = S.snapshot()
        for e in S.ENG:
            S.op(e, lambda en: en.nop(), extra=snap)
        S.readers = {}
        S.last_w = {}

    DMA("sp", ident_f[:], d_ident, w=["ident_f"])
    DMA("sp", rb_f[:], d_rb, w=["rb_f"])
    DMA("sp", rc_f[:], d_rc, w=["rc_f"])
    DMA("pool", band[:], d_band.rearrange("v q k -> q v k"), w=["band"])
    DMA("sp", cvec[:], d_cvec, w=["cvec"])
    DMA("pool", ident_b[:], d_ident, w=["ident_b"])
    DMA("pool", poolat[:], d_poolat, w=["poolat"])
    for t in range(NT):
        DMA("sp", x[:, t, :], d_x[t], w=[f"x{t}"])
    ACT(lambda e: e.activation(out=silu_b[:], in_=cvec[:], func=AF.Silu), ["cvec"], ["silu_b"])
    for i in range(2):
        V(lambda e, i=i: e.memset(bigf[i], 0.0), [], [f"big{i}"])

    def ln_stats(src_ap, src_keys, tag):
        V(lambda e: e.bn_stats(out=st6[:, 0, :], in_=src_ap[:, 0:512]), src_keys, ["st6a"])
        V(lambda e: e.bn_stats(out=st6[:, 1, :], in_=src_ap[:, 512:1024]), src_keys, ["st6b"])
        V(lambda e: e.bn_aggr(out=sm[:, 0:2], in_=st6[:].rearrange("p a b -> p (a b)")), ["st6a", "st6b"], ["sm_mv"])
        V(lambda e: e.tensor_scalar(out=sm[:, 2:3], in0=sm[:, 1:2], scalar1=LN_EPS, scalar2=None, op0=ALU.add),
          ["sm_mv"], ["sm_rstd"])
        ACT(lambda e: e.activation(out=sm[:, 2:3], in_=sm[:, 2:3], func=AF.Sqrt), ["sm_rstd"], ["sm_rstd"])
        V(lambda e: e.reciprocal(out=sm[:, 2:3], in_=sm[:, 2:3]), ["sm_rstd"], ["sm_rstd"])
        V(lambda e: e.scalar_tensor_tensor(out=sm[:, 3:4], in0=sm[:, 0:1], scalar=-1.0, in1=sm[:, 2:3],
                                           op0=ALU.mult, op1=ALU.mult), ["sm_mv", "sm_rstd"], ["sm_nmr"])
        return sm[:, 2:3], sm[:, 3:4]

    def ln_mod(t, which, sh_off, sc_off, dst_bf, dst_key, tmp, tmp_key):
        rstd, nmr = ln_stats(x[:, t, :], [f"x{t}"], "a")
        ACT(lambda e: e.activation(out=tmp[:, 0:1024], in_=x[:, t, :], func=AF.Identity, bias=nmr, scale=rstd),
            [f"x{t}", "sm_rstd", "sm_nmr"], [tmp_key])
        V(lambda e: e.tensor_tensor(out=tmp[:, 0:1024], in0=tmp[:, 0:1024], in1=mod[:, which, sc_off:sc_off + 1024],
                                    op=ALU.mult), [tmp_key, "mod"], [tmp_key])
        V(lambda e: e.tensor_tensor(out=dst_bf, in0=tmp[:, 0:1024], in1=mod[:, which, sh_off:sh_off + 1024],
                                    op=ALU.add), [tmp_key, "mod"], [dst_key])

    def transpose8(src_bf, src_key, dst, dst_key):
        def tr(e):
            ins = None
            for k in range(8):
                ins = e.transpose(out=psT[:, k * 128:(k + 1) * 128], in_=src_bf[:, k * 128:(k + 1) * 128],
                                  identity=ident_b[:])
            return ins
        PE(tr, [src_key, "ident_b"], ["psT"])
        ACT(lambda e: e.copy(out=dst.rearrange("p k t -> p (k t)"), in_=psT[:, :]), ["psT"], [dst_key])

    def mod_half(l, half, wb, wb_key, bb, bb_key):
        for j in range(24):
            c0 = half * 3072 + j * 128
            b = j % len(wb)
            b2 = j % len(bb)
            DMA("pool", wb[b], d_wmod[l][:, c0:c0 + 128].rearrange("(k p) n -> p k n", p=128), w=[f"{wb_key}{b}"])
            DMA("sp", bb[b2], bcast_row(h_bmod, l * 6144 + c0, 128), w=[f"{bb_key}{b2}"])
            for which in range(2):
                def mm(e, which=which, b=b):
                    ins = None
                    for k in range(8):
                        ins = e.matmul(psA[:, which * 512: which * 512 + 128],
                                       lhsT=silu_b[:, which, k:k + 1].to_broadcast([128, 128]),
                                       rhs=wb[b][:, k, :], start=(k == 0), stop=(k == 7))
                    return ins
                PE(mm, ["silu_b", f"{wb_key}{b}"], [f"psA{which}"])
                addc = 1.0 if 8 <= j < 16 else 0.0
                V(lambda e, which=which, b2=b2, j=j, addc=addc: e.scalar_tensor_tensor(
                    out=mod[:, which, j * 128:(j + 1) * 128], in0=psA[:, which * 512: which * 512 + 128],
                    scalar=addc, in1=bb[b2], op0=ALU.add, op1=ALU.add),
                  [f"psA{which}", f"{bb_key}{b2}"], ["mod"])

    def dbg(name, ap, keys):
        if name in dbg_out:
            DMA("pool", dbg_out[name], ap, r=keys)

    def attn_pv(blocks, vfn, vkey, out_ps, out_key, first_group_start=True):
        n = len(blocks)
        for r0 in range(0, n, 8):
            grp = list(range(r0, min(n, r0 + 8)))

            def tr(e, grp=grp, r0=r0):
                ins = None
                for n_ in grp:
                    ins = e.transpose(out=psT[:, (n_ - r0) * 128:(n_ - r0 + 1) * 128],
                                      in_=w_bf[:, n_ * 128:(n_ + 1) * 128], identity=ident_b[:])
                return ins
            PE(tr, ["w_bf", "ident_b"], ["psT"])
            ACT(lambda e, grp=grp, r0=r0: e.copy(out=wT[:, r0:r0 + len(grp), :].rearrange("p b q -> p (b q)"),
                                                 in_=psT[:, 0:len(grp) * 128]), ["psT"], ["wT"])

        def pv(e):
            ins = None
            for n_, jb in enumerate(blocks):
                ins = e.matmul(out_ps, lhsT=wT[:, n_, :], rhs=vfn(jb), start=(n_ == 0), stop=(n_ == n - 1))
            return ins
        PE(pv, ["wT", vkey], [out_key])

    def post_ln(t, src0, src0_key, src1, src1_key, which, tmp, tmp_key):
        V(lambda e: e.tensor_tensor(out=tmp[:, 0:512], in0=src0, in1=mod[:, which, 2048:2560], op=ALU.mult),
          [src0_key, "mod"], [tmp_key])
        V(lambda e: e.tensor_tensor(out=tmp[:, 512:1024], in0=src1, in1=mod[:, which, 2560:3072], op=ALU.mult),
          [src1_key, "mod"], [tmp_key])
        V(lambda e: e.scalar_tensor_tensor(out=tmp[:, 0:1024], in0=x[:, t, :], scalar=ALPHA, in1=tmp[:, 0:1024],
                                           op0=ALU.mult, op1=ALU.add), [f"x{t}", tmp_key], [tmp_key])
        rstd, nmr = ln_stats(tmp, [tmp_key], "p")
        ACT(lambda e: e.activation(out=tmp[:, 0:1024], in_=tmp[:, 0:1024], func=AF.Identity, bias=nmr, scale=rstd),
            [tmp_key, "sm_rstd", "sm_nmr"], [tmp_key])
        V(lambda e: e.tensor_tensor(out=tmp[:, 0:1024], in0=tmp[:, 0:1024], in1=lnp[:, 0, :], op=ALU.mult),
          [tmp_key, "lnp"], [tmp_key])
        V(lambda e: e.tensor_tensor(out=x[:, t, :], in0=tmp[:, 0:1024], in1=lnp[:, 1, :], op=ALU.add),
          [tmp_key, "lnp"], [f"x{t}"])

    def ck(name):
        if CFG["stop"] == name:
            raise _Stop()

    try:
      ck("init")
      for l in range(CFG["depth"]):
          lam_init = 0.8 - 0.6 * math.exp(-0.3 * l)
          for k in range(8):
              DMA("pool", w_in_sb[:, k, :], d_win[l][k * 128:(k + 1) * 128, :], w=["w_in"])
          for k in range(0, 8, 2):
              DMA("pool", w_out_sb[:, k:k + 2, :],
                  d_wout[l][k * 128:(k + 2) * 128, :].rearrange("(k p) n -> p k n", p=128), w=["w_out"])
          DMA("pool", cwT[:], d_cwT[l], w=["cwT"])
          DMA("pool", poolw[:], d_poolw[l], w=["poolw"])
          DMA("pool", kb_b[:], d_kb[l], w=["kb_b"])
          DMA("sp", cbT[:], d_cbT[l], w=["cbT"])
          DMA("sp", sink[:], bcast_row(h_sink, l * 4, 4), w=["sink"])
          DMA("sp", lamt[:, 0, :], bcast_row(h_lamq, l * 64, 64), w=["lamt"])
          DMA("sp", lamt[:, 1, :], bcast_row(h_lamk, l * 64, 64), w=["lamt"])
          DMA("sp", subg[:], bcast_row(h_subg, l * 64, 64), w=["subg"])
          DMA("sp", pscale[:], bcast_row(h_pscale, l * 256, 256), w=["pscale"])
          DMA("sp", lnp[:, 0, :], bcast_row(h_lng, (l * 2 + 0) * D, D), w=["lnp"])
          DMA("sp", lnp[:, 1, :], bcast_row(h_lnb, (l * 2 + 0) * D, D), w=["lnp"])
          for j in range(2):
              V(lambda e, j=j: e.tensor_tensor(out=lamt[:, 0, j * 32:(j + 1) * 32], in0=lamt[:, 0, j * 32:(j + 1) * 32],
                                               in1=lamt[:, 1, j * 32:(j + 1) * 32], op=ALU.mult), ["lamt"], ["lamt"])
              V(lambda e, j=j: e.reduce_sum(out=lamv[:, j:j + 1], in_=lamt[:, 0, j * 32:(j + 1) * 32], axis=AX.X),
                ["lamt"], ["lamv"])
          ACT(lambda e: e.activation(out=lamv[:, 2:4], in_=lamv[:, 0:2], func=AF.Exp), ["lamv"], ["lamve"])
          V(lambda e, lam_init=lam_init: e.scalar_tensor_tensor(out=lamv[:, 5:6], in0=lamv[:, 3:4], scalar=-lam_init, in1=lamv[:, 2:3],
                                             op0=ALU.add, op1=ALU.subtract), ["lamve"], ["nlam"])
          V(lambda e, lam_init=lam_init: e.tensor_scalar(out=subg[:], in0=subg[:], scalar1=1.0 - lam_init, scalar2=None, op0=ALU.mult),
            ["subg"], ["subg"])
          mod_half(l, 0, wmb, "wmb", bmb, "bmb")
          NE_ = CFG["nexp"]
          for q_ in range(0, NE_, 1024):
              rr = min(1024, NE_ - q_)
              DMA("pool", d_tab16[q_:q_ + rr, :], d_puv[l * NE_ + q_:l * NE_ + q_ + rr, :], w=[f"tab16_{q_ // 1024}"])
          if l == 0:
              dbg("mod0", mod[:, 0, :], ["mod"])
          ck("mod")

          seqs = [(0, 2, False), (2, 2, False), (4, 8, True)]
          for (t0, nb, lat) in seqs:
              which = 1 if lat else 0
              nkb = nb + (2 if lat else 0)
              if lat:
                  DMA("pool", kTb_all[:, 1024:1280], d_cwkT[l], w=["kTb_all"])
                  DMA("pool", vb_all[:, 8:10, :], d_cwv[l].rearrange("(b p) c -> p b c", p=128), w=["vb_all"])
                  DMA("pool", kTc_all[:, :, 1024:1280], d_cdkT[l], w=["kTc_all"])
                  DMA("pool", vc_all[:, 8:10, :], d_cdv[l].rearrange("(b p) c -> p b c", p=128), w=["vc_all"])

              def rope(src_ps, src_key, n_ch, parts, tmp32, tmp_key, cs, cskey, rmat, rkey, dst, dst_key, lat=lat):
                  if not lat:
                      ACT(lambda e: e.copy(out=dst, in_=src_ps), [src_key], [dst_key])
                      return
                  V(lambda e: e.tensor_copy(out=tmp32, in_=src_ps), [src_key], [tmp_key])

                  def mm(e):
                      ins = None
                      for c in range(n_ch):
                          ins = e.matmul(src_ps[:, c, :], lhsT=rmat, rhs=tmp32[:, c, :], start=True, stop=True)
                      return ins
                  PE(mm, [tmp_key, rkey], [src_key])
                  V(lambda e: e.tensor_tensor(out=tmp32, in0=tmp32, in1=cs[:, 0:1, :].to_broadcast([parts, n_ch, 128]),
                                              op=ALU.mult), [tmp_key, cskey], [tmp_key])
                  V(lambda e: e.tensor_tensor(out=src_ps, in0=src_ps, in1=cs[:, 1:2, :].to_broadcast([parts, n_ch, 128]),
                                              op=ALU.mult), [src_key, cskey], [src_key])
                  V(lambda e: e.tensor_tensor(out=dst, in0=tmp32, in1=src_ps, op=ALU.add), [tmp_key, src_key], [dst_key])

              def load_cs(i):
                  tk = i * 128
                  DMA("sp", csB[:, 0, :], d_cosb[:, tk:tk + 128], w=["csB"])
                  DMA("sp", csB[:, 1, :], d_sinb[:, tk:tk + 128], w=["csB"])
                  DMA("sp", csC[:, 0, :], d_cosc[:, tk:tk + 128], w=["csC"])
                  DMA("sp", csC[:, 1, :], d_sinc[:, tk:tk + 128], w=["csC"])

              for i in range(nb):
                  t = t0 + i
                  ln_mod(t, which, 0, 1024, h_bf, "h_bf", scr0, "scr0")
                  ck("ln")
                  transpose8(h_bf, "h_bf", hT, "hT")
                  ck("tr")
                  if lat:
                      load_cs(i)

                  def tm(e):
                      ins = None
                      for (c_lo, n, po) in ((C_BK, 256, 0), (C_CK, 256, 512), (C_CV, 512, 1024)):
                          for k in range(8):
                              ins = e.matmul(psA[:, po:po + n], lhsT=hT[:, k, :], rhs=w_in_sb[:, k, c_lo:c_lo + n],
                                             start=(k == 0), stop=(k == 7))
                      return ins
                  PE(tm, ["hT", "w_in"], ["psA0", "psA1", "psA2"])

                  def fm(e):
                      ins = None
                      for k in range(8):
                          ins = e.matmul(psC[:, 0:128], lhsT=w_in_sb[:, k, C_BK:C_BK + 128], rhs=hT[:, k, :],
                                         start=(k == 0), stop=(k == 7))
                      for c in range(4):
                          for k in range(8):
                              ins = e.matmul(psB[0:64, c * 128:(c + 1) * 128],
                                             lhsT=w_in_sb[:, k, C_CK + c * 64:C_CK + (c + 1) * 64], rhs=hT[:, k, :],
                                             start=(k == 0), stop=(k == 7))
                      return ins
                  PE(fm, ["hT", "w_in"], ["psC", "psB0"])
                  ck("mm")
                  ACT(lambda e, i=i: e.copy(out=vb_all[:, i, :], in_=psA[:, 128:256]), ["psA0"], ["vb_all"])
                  ACT(lambda e, i=i: e.copy(out=vc_all[:, i, :], in_=psA[:, 1024:1280]), ["psA2"], ["vc_all"])
                  ACT(lambda e, i=i: e.copy(out=z_all[:, i, :], in_=psA[:, 1280:1536]), ["psA2"], ["z_all"])
                  if not lat:
                      sq = t0 // 2
                      V(lambda e: e.tensor_copy(out=scr1[:, 0:256], in_=psA[:, 0:256]), ["psA0"], ["scr1o"])
                      V(lambda e: e.tensor_copy(out=scr1[:, 256:512], in_=psA[:, 512:768]), ["psA1"], ["scr1o"])
                      V(lambda e: e.tensor_copy(out=scr1[:, 512:768], in_=psA[:, 1024:1280]), ["psA2"], ["scr1o"])
                      r0 = i * 128
                      DMA("sp", o_wk[sq, l, r0:r0 + 128, :], scr1[:, 0:128], r=["scr1o"])
                      DMA("sp", o_wv[sq, l, r0:r0 + 128, :], scr1[:, 128:256], r=["scr1o"])
                      DMA("sp", o_dk[sq, l, r0:r0 + 128, :], scr1[:, 256:512], r=["scr1o"])
                      DMA("sp", o_dv[sq, l, r0:r0 + 128, :], scr1[:, 512:768], r=["scr1o"])
                  ck("ev")
                  tk = i * 128
                  fmB32 = scr1[:, 768:896].rearrange("p (c t) -> p c t", t=128)
                  fmC32 = scr1[0:64, 896:1408].rearrange("p (c t) -> p c t", t=128)
                  rope(psC[:, 0:128].rearrange("p (c t) -> p c t", t=128), "psC", 1, 128, fmB32, "scr1o", csB, "csB",
                       rb_f[:], "rb_f", kTb_all[:, tk:tk + 128].rearrange("p (c t) -> p c t", t=128), "kTb_all")
                  rope(psB[0:64, 0:512].rearrange("p (c t) -> p c t", t=128), "psB0", 4, 64, fmC32, "scr1o", csC, "csC",
                       rc_f[:], "rc_f", kTc_all[:, :, tk:tk + 128], "kTc_all")
              if l == 0 and t0 == 0:
                  dbg("kTb", kTb_all[:, 0:256], ["kTb_all"])
                  dbg("z", z_all[:, 0:2, :], ["z_all"])
              ck("p1")

              for i in range(nb):
                  t = t0 + i
                  ln_mod(t, which, 0, 1024, h_bf, "h_bf", scr0, "scr0")
                  transpose8(h_bf, "h_bf", hT, "hT")
                  if lat:
                      load_cs(i)

                  def tmA(e):
                      ins = None
                      for k in range(8):
                          ins = e.matmul(psA[:, 0:512], lhsT=hT[:, k, :], rhs=w_in_sb[:, k, 0:512],
                                         start=(k == 0), stop=(k == 7))
                      return ins
                  PE(tmA, ["hT", "w_in"], ["psA0"])

                  def fmq(e):
                      ins = None
                      for g in range(2):
                          for kv in range(2):
                              c0 = C_BQ + (kv * 2 + g) * 64
                              for k in range(8):
                                  ins = e.matmul(psC[kv * 64:(kv + 1) * 64, g * 128:(g + 1) * 128],
                                                 lhsT=w_in_sb[:, k, c0:c0 + 64], rhs=hT[:, k, :],
                                                 start=(k == 0), stop=(k == 7))
                      for c in range(4):
                          for k in range(8):
                              ins = e.matmul(psB[0:64, c * 128:(c + 1) * 128],
                                             lhsT=w_in_sb[:, k, C_CQ + c * 64:C_CQ + (c + 1) * 64], rhs=hT[:, k, :],
                                             start=(k == 0), stop=(k == 7))
                      return ins
                  PE(fmq, ["hT", "w_in"], ["psC", "psB0"])
                  ACT(lambda e: e.activation(out=ua, in_=psA[:, 0:512], func=AF.Gelu_apprx_tanh), ["psA0"], ["ua"])
                  fmB32 = scr1[:, 768:1024].rearrange("p (c t) -> p c t", t=128)
                  fmC32 = scr1[0:64, 0:512].rearrange("p (c t) -> p c t", t=128)
                  rope(psC[:, 0:256].rearrange("p (c t) -> p c t", t=128), "psC", 2, 128, fmB32, "scr1o", csB, "csB",
                       rb_f[:], "rb_f", qTb, "qTb")
                  rope(psB[0:64, 0:512].rearrange("p (c t) -> p c t", t=128), "psB0", 4, 64, fmC32, "scr1o", csC, "csC",
                       rc_f[:], "rc_f", qTc, "qTc")

                  def mmA(e):
                      ins = None
                      for hh in range(4):
                          ins = e.matmul(psA[:, 1536 + hh * 64:1536 + (hh + 1) * 64], lhsT=cwT[:, hh, :],
                                         rhs=ua[:, 256 + hh * 64:256 + (hh + 1) * 64], start=True, stop=True)
                      return ins
                  PE(mmA, ["ua", "cwT"], ["psA3"])
                  for hh in range(4):
                      V(lambda e, hh=hh: e.scalar_tensor_tensor(
                          out=o_cat[:, hh * 64:(hh + 1) * 64], in0=psA[:, 1536 + hh * 64:1536 + (hh + 1) * 64],
                          scalar=cbT[:, hh:hh + 1], in1=ua[:, hh * 64:(hh + 1) * 64], op0=ALU.add, op1=ALU.mult),
                        ["psA3", "cbT", "ua"], ["o_cat"])

                  rels = []
                  if i > 0:
                      rels.append((i - 1, 3))
                  rels.append((i, 0 if i == 0 else (2 if i == nb - 1 else 1)))
                  if i < nb - 1:
                      rels.append((i + 1, 4))

                  def mmD(e, rels=rels):
                      ins = None
                      for g in range(4):
                          for n_, (j, v) in enumerate(rels):
                              ins = e.matmul(psB[0:64, 512 + g * 128:512 + (g + 1) * 128],
                                             lhsT=z_all[:, j, g * 64:(g + 1) * 64], rhs=poolat[:, v, g, :],
                                             start=(n_ == 0), stop=(n_ == len(rels) - 1))
                      return ins
                  PE(mmD, ["z_all", "poolat"], ["psB1"])
                  ACT(lambda e: e.copy(out=pooledT.rearrange("p g t -> p (g t)"), in_=psB[0:64, 512:1024]),
                      ["psB1"], ["pooledT"])

                  def mmD2(e):
                      ins = None
                      for g in range(4):
                          ins = e.matmul(psA[:, 1792 + g * 64:1792 + (g + 1) * 64], lhsT=pooledT[:, g, :],
                                         rhs=poolw[:, g, :], start=True, stop=True)
                      return ins
                  PE(mmD2, ["pooledT", "poolw"], ["psA3"])
                  V(lambda e: e.tensor_tensor(out=o_cat[:, 768:1024], in0=psA[:, 1792:2048], in1=pscale[:], op=ALU.mult),
                    ["psA3", "pscale"], ["o_cat"])

                  if lat:
                      kblocks = [min(max(j, 0), nb - 1) for j in (i - 1, i, i + 1)] + [8, 9]
                      bvar = 0 if i == 0 else (2 if i == nb - 1 else 1)
                      cblocks = list(range(10))
                  else:
                      kblocks = [0, 1]
                      bvar = 0
                      cblocks = [0, 1]
                  NKB = len(kblocks) * 128
                  NKC = len(cblocks) * 128
                  csc = 32.0 ** -0.5
                  pieces = [(p0, min(512, NKC - p0)) for p0 in range(0, NKC, 512)]
                  w_bfB = hT.rearrange("p k t -> p (k t)")
                  wTB = h_bf[:, 0:640].rearrange("p (b q) -> p b q", q=128)

                  def pv_gen(blocks, wsrc, wkey, wTd, wTkey, vfn, vkey, out_ps, out_key):
                      n = len(blocks)
                      for r0 in range(0, n, 8):
                          grp = list(range(r0, min(n, r0 + 8)))

                          def tr(e, grp=grp, r0=r0):
                              ins = None
                              for n_ in grp:
                                  ins = e.transpose(out=psT[:, (n_ - r0) * 128:(n_ - r0 + 1) * 128],
                                                    in_=wsrc[:, n_ * 128:(n_ + 1) * 128], identity=ident_b[:])
                              return ins
                          PE(tr, [wkey, "ident_b"], ["psT"])
                          ACT(lambda e, grp=grp, r0=r0: e.copy(
                              out=wTd[:, r0:r0 + len(grp), :].rearrange("p b q -> p (b q)"),
                              in_=psT[:, 0:len(grp) * 128]), ["psT"], [wTkey])
                          yield

                      def pv(e):
                          ins = None
                          for n_, jb in enumerate(blocks):
                              ins = e.matmul(out_ps, lhsT=wTd[:, n_, :], rhs=vfn(jb), start=(n_ == 0), stop=(n_ == n - 1))
                          return ins
                      PE(pv, [wTkey, vkey], [out_key])
                      yield

                  def chainB(kblocks=kblocks, bvar=bvar, NKB=NKB, lat=lat):
                      for hh in range(4):
                          kv, g = hh // 2, hh % 2

                          def mmS(e, kv=kv, g=g):
                              ins = None
                              for n_, jb in enumerate(kblocks):
                                  dst = psA[:, 1536 + n_ * 128:1536 + (n_ + 1) * 128] if n_ < 4 else psB[:, 512:640]
                                  ins = e.matmul(dst, lhsT=qTb[kv * 64:(kv + 1) * 64, g, :],
                                                 rhs=kTb_all[kv * 64:(kv + 1) * 64, jb * 128:(jb + 1) * 128],
                                                 start=True, stop=True)
                              return ins
                          PE(mmS, ["qTb", "kTb_all"], ["psA3", "psB1"] if lat else ["psA3"])
                          yield
                          if lat:
                              V(lambda e: e.scalar_tensor_tensor(out=scrB[:, 0:384], in0=psA[:, 1536:1920], scalar=0.125,
                                                                 in1=band[:, bvar, :], op0=ALU.mult, op1=ALU.add),
                                ["psA3", "band"], ["scrB"])
                              V(lambda e: e.tensor_scalar(out=scrB[:, 384:512], in0=psA[:, 1920:2048], scalar1=0.125,
                                                          scalar2=None, op0=ALU.mult), ["psA3"], ["scrB"])
                              V(lambda e: e.tensor_scalar(out=scrB[:, 512:640], in0=psB[:, 512:640], scalar1=0.125,
                                                          scalar2=None, op0=ALU.mult), ["psB1"], ["scrB"])
                          else:
                              V(lambda e: e.tensor_scalar(out=scrB[:, 0:256], in0=psA[:, 1536:1792], scalar1=0.125,
                                                          scalar2=None, op0=ALU.mult), ["psA3"], ["scrB"])
                          yield
                          V(lambda e: e.reduce_max(out=sm[:, 8:9], in_=scrB[:, 0:NKB], axis=AX.X), ["scrB"], ["sm8"])
                          V(lambda e, hh=hh: e.tensor_scalar(out=sm[:, 9:10], in0=sm[:, 8:9], scalar1=sink[:, hh:hh + 1],
                                                             scalar2=-1.0, op0=ALU.max, op1=ALU.mult), ["sm8", "sink"], ["sm9"])
                          yield
                          ACT(lambda e: e.activation(out=scrB[:, 0:NKB], in_=scrB[:, 0:NKB], func=AF.Exp,
                                                     bias=sm[:, 9:10], accum_out=sm[:, 10:11]),
                              ["scrB", "sm9"], ["scrB", "sm10"])
                          ACT(lambda e, hh=hh: e.activation(out=sm[:, 11:12], in_=sink[:, hh:hh + 1], func=AF.Exp,
                                                            bias=sm[:, 9:10]), ["sink", "sm9"], ["sm11"])
                          yield
                          V(lambda e: e.tensor_tensor(out=sm[:, 12:13], in0=sm[:, 10:11], in1=sm[:, 11:12], op=ALU.add),
                            ["sm10", "sm11"], ["sm12"])
                          V(lambda e: e.reciprocal(out=sm[:, 13:14], in_=sm[:, 12:13]), ["sm12"], ["sm13"])
                          V(lambda e: e.tensor_scalar(out=w_bfB[:, 0:NKB], in0=scrB[:, 0:NKB], scalar1=sm[:, 13:14],
                                                      scalar2=None, op0=ALU.mult), ["scrB", "sm13"], ["hT"])
                          yield
                          yield from pv_gen(kblocks, w_bfB, "hT", wTB, "h_bf",
                                            lambda jb, kv=kv: vb_all[:, jb, kv * 64:(kv + 1) * 64], "vb_all",
                                            psC[:, 256 + hh * 64:256 + (hh + 1) * 64], "psC")
                      V(lambda e: e.tensor_copy(out=o_cat[:, 256:512], in_=psC[:, 256:512]), ["psC"], ["o_cat"])
                      yield

                  def chainC(cblocks=cblocks, NKC=NKC, pieces=pieces):
                      for hh in range(4):
                          for j in range(2):
                              ej = scr0 if j == 0 else scr1

                              def mmC(e, hh=hh, j=j):
                                  ins = None
                                  for (p0, pn) in pieces:
                                      ins = e.matmul(psA[:, p0:p0 + pn], lhsT=qTc[j * 32:(j + 1) * 32, hh, :],
                                                     rhs=kTc_all[j * 32:(j + 1) * 32, hh, p0:p0 + pn], start=True, stop=True)
                                  return ins
                              pk = ["psA0", "psA1", "psA2"][:len(pieces)]
                              PE(mmC, ["qTc", "kTc_all"], pk)
                              yield
                              V(lambda e, j=j: e.reduce_max(out=sm[:, 14 + j:15 + j], in_=psA[:, 0:NKC], axis=AX.X),
                                pk, [f"smx{j}"])
                              V(lambda e, j=j: e.tensor_scalar(out=sm[:, 16 + j:17 + j], in0=sm[:, 14 + j:15 + j],
                                                               scalar1=-csc, scalar2=None, op0=ALU.mult),
                                [f"smx{j}"], [f"smn{j}"])
                              yield
                              ACT(lambda e, j=j, ej=ej: e.activation(out=ej[:, 0:NKC], in_=psA[:, 0:NKC], func=AF.Exp,
                                                                     bias=sm[:, 16 + j:17 + j], scale=csc,
                                                                     accum_out=sm[:, 18 + j:19 + j]),
                                  pk + [f"smn{j}"], ["scr0" if j == 0 else "scr1o", f"sms{j}"])
                              yield
                          V(lambda e: e.reciprocal(out=sm[:, 20:22], in_=sm[:, 18:20]), ["sms0", "sms1"], ["smr"])
                          V(lambda e: e.tensor_tensor(out=sm[:, 22:23], in0=sm[:, 21:22], in1=lamv[:, 5:6], op=ALU.mult),
                            ["smr", "nlam"], ["smc1"])
                          yield
                          V(lambda e: e.tensor_scalar(out=scr0[:, 0:NKC], in0=scr0[:, 0:NKC], scalar1=sm[:, 20:21],
                                                      scalar2=None, op0=ALU.mult), ["scr0", "smr"], ["scr0"])
                          yield
                          V(lambda e: e.scalar_tensor_tensor(out=w_bf[:, 0:NKC], in0=scr1[:, 0:NKC], scalar=sm[:, 22:23],
                                                             in1=scr0[:, 0:NKC], op0=ALU.mult, op1=ALU.add),
                            ["scr0", "scr1o", "smc1"], ["w_bf"])
                          yield
                          yield from pv_gen(cblocks, w_bf, "w_bf", wT, "wT",
                                            lambda jb, hh=hh: vc_all[:, jb, hh * 64:(hh + 1) * 64], "vc_all",
                                            psC[:, hh * 64:(hh + 1) * 64], "psC")
                      oc = scr0[:, 0:256]
                      V(lambda e: e.tensor_copy(out=oc, in_=psC[:, 0:256]), ["psC"], ["scr0"])
                      V(lambda e: e.tensor_tensor(out=scr0[:, 256:512], in0=oc, in1=oc, op=ALU.mult), ["scr0"], ["scr0"])
                      V(lambda e: e.reduce_sum(out=sm[:, 24:28], in_=scr0[:, 256:512].rearrange("p (h d) -> p h d", d=64),
                                               axis=AX.X), ["scr0"], ["smq"])
                      V(lambda e: e.tensor_scalar(out=sm[:, 24:28], in0=sm[:, 24:28], scalar1=1.0 / 64.0, scalar2=LN_EPS,
                                                  op0=ALU.mult, op1=ALU.add), ["smq"], ["smq"])
                      yield
                      ACT(lambda e: e.activation(out=sm[:, 24:28], in_=sm[:, 24:28], func=AF.Sqrt), ["smq"], ["smq"])
                      V(lambda e: e.reciprocal(out=sm[:, 24:28], in_=sm[:, 24:28]), ["smq"], ["smq"])
                      V(lambda e: e.tensor_tensor(out=scr0[:, 0:256].rearrange("p (h d) -> p h d", d=64),
                                                  in0=scr0[:, 0:256].rearrange("p (h d) -> p h d", d=64),
                                                  in1=sm[:, 24:28].unsqueeze(2).to_broadcast([128, 4, 64]), op=ALU.mult),
                        ["scr0", "smq"], ["scr0"])
                      V(lambda e: e.tensor_tensor(out=o_cat[:, 512:768].rearrange("p (h d) -> p h d", d=64),
                                                  in0=scr0[:, 0:256].rearrange("p (h d) -> p h d", d=64),
                                                  in1=subg[:].unsqueeze(1).to_broadcast([128, 4, 64]), op=ALU.mult),
                        ["scr0", "subg"], ["o_cat"])
                      yield

                  chains = [chainC(), chainB()] if CFG.get('ilv', 1) else []
                  if not CFG.get('ilv', 1):
                      for _ in chainC():
                          pass
                      for _ in chainB():
                          pass
                  while chains:
                      for g_ in list(chains):
                          try:
                              next(g_)
                          except StopIteration:
                              chains.remove(g_)
                  if l == 0 and t == 0:
                      dbg("ocat", o_cat, ["o_cat"])

                  transpose8(o_cat, "o_cat", hT, "hT")

                  def mmO(e):
                      ins = None
                      for n_ in range(2):
                          for k in range(8):
                              ins = e.matmul(psA[:, n_ * 512:(n_ + 1) * 512], lhsT=hT[:, k, :],
                                             rhs=w_out_sb[:, k, n_ * 512:(n_ + 1) * 512], start=(k == 0), stop=(k == 7))
                      return ins
                  PE(mmO, ["hT", "w_out"], ["psA0", "psA1"])
                  post_ln(t, psA[:, 0:512], "psA0", psA[:, 512:1024], "psA1", which, scr0, "scr0")
          if l == 0:
              dbg("x1", x[:, 0, :], ["x0"])

          ck("mix")
          barrier()
          for k in range(0, 8, 2):
              DMA("pool", wq_sb[:, k:k + 2, :], d_wq[l][k * 128:(k + 2) * 128, :].rearrange("(k p) n -> p k n", p=128),
                  w=["wq"])
          DMA("sp", lnp[:, 0, :], bcast_row(h_lng, (l * 2 + 1) * D, D), w=["lnp"])
          DMA("sp", lnp[:, 1, :], bcast_row(h_lnb, (l * 2 + 1) * D, D), w=["lnp"])
          for i in range(2):
              V(lambda e, i=i: e.memset(bigf[i], 0.0), [], [f"bg{i}_{q_}" for q_ in range(64)])
          mod_half(l, 1, p_wmb, "wmb", p_bmb, "bmb")

          def stage_A(i):
              t = i
              which = 0 if t < 4 else 1
              hb = h2[i % 2]
              hkey = f"h2_{i % 2}"
              slot = i % 2
              ln_mod(t, which, 0, 1024, hb, hkey, e_scr0, "scr0")
              yield
              transpose8(hb, hkey, h2T, "h2T")
              yield
              for half in range(2):
                  def mq(e, half=half):
                      ins = None
                      for c in range(4):
                          cc = half * 4 + c
                          for k in range(8):
                              ins = e.matmul(psA[:, c * 128:(c + 1) * 128], lhsT=wq_sb[:, k, cc * 128:(cc + 1) * 128],
                                             rhs=h2T[:, k, :], start=(k == 0), stop=(k == 7))
                      return ins
                  PE(mq, ["h2T", "wq"], ["psA0"])
                  ACT(lambda e, half=half: e.copy(out=qT_sb[:, half * 4:half * 4 + 4, :].rearrange("p c t -> p (c t)"),
                                                  in_=psA[:, 0:512]), ["psA0"], ["qT_sb"])
                  yield
              V(lambda e: e.memset(idxf, 0.0), [], ["idxf"])
              for hq in range(4):
                  def ms(e, hq=hq):
                      ins = None
                      for hh in range(2):
                          ins = e.matmul(psA[:, hh * 256:(hh + 1) * 256], lhsT=qT_sb[:, 2 * hq + hh, :],
                                         rhs=kb_b[:], start=True, stop=True)
                      return ins
                  PE(ms, ["qT_sb", "kb_b"], ["psA0"])
                  ACT(lambda e: e.copy(out=sc_sb, in_=psA[:, 0:512]), ["psA0"], ["sc_sb"])
                  yield
                  for g in range(4):
                      scg = sc_sb[:, g * 128:(g + 1) * 128]
                      V(lambda e, g=g, scg=scg: e.max(out=sv[:, g, 0:8], in_=scg), ["sc_sb"], ["sv"])
                      V(lambda e, g=g, scg=scg: e.max_index(out=si_u[:, g, 0:8], in_max=sv[:, g, 0:8], in_values=scg),
                        ["sc_sb", "sv"], ["si_u"])
                      V(lambda e, g=g, scg=scg: e.match_replace(out=scr_mr[:, 0:128], in_to_replace=sv[:, g, 0:8],
                                                                in_values=scg, imm_value=-1e30), ["sc_sb", "sv"], ["scr_mr"])
                      V(lambda e, g=g: e.max(out=sv[:, g, 8:16], in_=scr_mr[:, 0:128]), ["scr_mr"], ["sv"])
                      V(lambda e, g=g: e.max_index(out=si_u[:, g, 8:16], in_max=sv[:, g, 8:16], in_values=scr_mr[:, 0:128]),
                        ["scr_mr", "sv"], ["si_u"])
                      yield
                  V(lambda e: e.tensor_copy(out=sif, in_=si_u[:]), ["si_u"], ["sif"])
                  NC_ = 112
                  for hh in range(2):
                      s0, s1 = sv[:, 2 * hh, :], sv[:, 2 * hh + 1, :]
                      f0, f1 = sif[:, 2 * hh, :], sif[:, 2 * hh + 1, :]
                      cA = cand[:, hh, 0:64].rearrange("p (a b) -> p a b", b=16)
                      cB = cand[:, hh, 64:112].rearrange("p (a b) -> p a b", b=4)
                      eA = eid[:, hh, 0:64].rearrange("p (a b) -> p a b", b=16)
                      eB = eid[:, hh, 64:112].rearrange("p (a b) -> p a b", b=4)
                      V(lambda e, cA=cA, s0=s0, s1=s1: e.tensor_tensor(
                          out=cA, in0=s0[:, 0:4].unsqueeze(2).to_broadcast([128, 4, 16]),
                          in1=s1.unsqueeze(1).to_broadcast([128, 4, 16]), op=ALU.add), ["sv"], ["cand"])
                      V(lambda e, cB=cB, s0=s0, s1=s1: e.tensor_tensor(
                          out=cB, in0=s0[:, 4:16].unsqueeze(2).to_broadcast([128, 12, 4]),
                          in1=s1[:, 0:4].unsqueeze(1).to_broadcast([128, 12, 4]), op=ALU.add), ["sv"], ["cand"])
                      V(lambda e, eA=eA, f0=f0, f1=f1: e.scalar_tensor_tensor(
                          out=eA, in0=f0[:, 0:4].unsqueeze(2).to_broadcast([128, 4, 16]), scalar=128.0,
                          in1=f1.unsqueeze(1).to_broadcast([128, 4, 16]), op0=ALU.mult, op1=ALU.add), ["sif"], ["eid"])
                      V(lambda e, eB=eB, f0=f0, f1=f1: e.scalar_tensor_tensor(
                          out=eB, in0=f0[:, 4:16].unsqueeze(2).to_broadcast([128, 12, 4]), scalar=128.0,
                          in1=f1[:, 0:4].unsqueeze(1).to_broadcast([128, 12, 4]), op0=ALU.mult, op1=ALU.add),
                        ["sif"], ["eid"])
                      yield
                  for hh in range(2):
                      H = 2 * hq + hh
                      V(lambda e, hh=hh, H=H: e.max(out=cvv[:, H, 0:8], in_=cand[:, hh, 0:NC_]), ["cand"], ["cvv"])
                      V(lambda e, hh=hh, H=H: e.match_replace(out=scr_mr[:, 0:NC_], in_to_replace=cvv[:, H, 0:8],
                                                              in_values=cand[:, hh, 0:NC_], imm_value=-1e30),
                        ["cand", "cvv"], ["scr_mr"])
                      V(lambda e, H=H: e.max(out=cvv[:, H, 8:16], in_=scr_mr[:, 0:NC_]), ["scr_mr"], ["cvv"])
                      for k in range(16):
                          V(lambda e, hh=hh, H=H, k=k: e.scalar_tensor_tensor(
                              out=e_scr1[:, 0:NC_], in0=cand[:, hh, 0:NC_], scalar=cvv[:, H, k:k + 1],
                              in1=eid[:, hh, 0:NC_], op0=ALU.is_equal, op1=ALU.mult,
                              accum_out=idxf[:, H * 16 + k:H * 16 + k + 1]),
                            ["cand", "eid", "cvv"], ["idxf"])
                          if k % 4 == 3:
                              yield
              yield
              V(lambda e: e.tensor_tensor(out=gat, in0=cvv, in1=cvv[:, :, 0:1].to_broadcast([128, 8, 16]), op=ALU.subtract),
                ["cvv"], ["gat"])
              ACT(lambda e: e.activation(out=gat, in_=gat, func=AF.Exp), ["gat"], ["gat"])
              V(lambda e: e.reduce_sum(out=sm[:, 8:16], in_=gat, axis=AX.X), ["gat"], ["smg"])
              V(lambda e: e.reciprocal(out=sm[:, 8:16], in_=sm[:, 8:16]), ["smg"], ["smg"])
              V(lambda e: e.tensor_tensor(out=gat, in0=gat, in1=sm[:, 8:16].unsqueeze(2).to_broadcast([128, 8, 16]),
                                          op=ALU.mult), ["gat", "smg"], ["gat"])
              V(lambda e: e.tensor_scalar(out=idxf, in0=idxf, scalar1=float(NEXP - 1), scalar2=0.0, op0=ALU.min, op1=ALU.max),
                ["idxf"], ["idxf"])
              V(lambda e, l=l: e.tensor_scalar(out=idxf, in0=idxf, scalar1=float(CFG.get("oob_add", 0)), scalar2=None, op0=ALU.add),
                ["idxf"], ["idxf"])

              def trI(e):
                  e.transpose(out=psA[:, 0:128], in_=idxf, identity=ident_f[:])
                  return e.transpose(out=psA[:, 128:256], in_=gat.rearrange("p h k -> p (h k)"), identity=ident_f[:])
              PE(trI, ["idxf", "gat", "ident_f"], ["psA0"])
              V(lambda e: e.tensor_copy(out=idx_i[:, slot, :], in_=psA[:, 0:128]), ["psA0"], [f"idx{slot}"])
              V(lambda e: e.tensor_copy(out=gT[slot], in_=psA[:, 128:256]), ["psA0"], [f"gT{slot}"])

          gcount = [0]
          ac = aT[0].rearrange("p a t -> p (a t)")[:, 0:32].rearrange("p (j c) -> p j c", c=4)
          bigdiag = [bigf[k_].rearrange("p (a b) -> p a b", b=65)[:, :, 0] for k_ in range(2)]

          tokst = {}
          XB = [(psB[:, 0:512], "psB0"), (psB[:, 512:1024], "psB1"), (psA[:, 512:1024], "psA1"), (psA[:, 1536:2048], "psA3")]

          def stage_S1(i, tt, tab=d_tab16):
              slot = i % 2
              hb, hkey = h2[i % 2], f"h2_{i % 2}"
              n_ = gcount[0]
              gcount[0] += 1
              b = n_ % NGB
              j = n_ % 8
              tokst[(i, tt)] = (b, j)
              S.dma("pool", lambda e: e.indirect_dma_start(
                  out=gbuf[b], out_offset=None, in_=tab,
                  in_offset=bass.IndirectOffsetOnAxis(ap=idx_i[:, slot, tt:tt + 1], axis=0),
                  bounds_check=REGS["bc"], oob_is_err=False), [f"idx{slot}"] + [f"tab16_{q_}" for q_ in range((CFG["nexp"] + 1023) // 1024)], [f"gb{b}"], SEM_U[b])
              for hf in range(2):
                  xb, xk = XB[2 * (n_ % 2) + hf]
                  PE(lambda e, hf=hf, xb=xb: e.matmul(xb, lhsT=ident_b[:, tt:tt + 1].to_broadcast([128, 128]),
                                                      rhs=hb[:, hf * 512:(hf + 1) * 512], start=True, stop=True),
                     [hkey, "ident_b"], [xk])
                  V(lambda e, hf=hf, xb=xb: e.scalar_tensor_tensor(
                      out=junk[:, hf * 512:(hf + 1) * 512], in0=gbuf[b][:, hf * 512:(hf + 1) * 512], scalar=1.0,
                      in1=xb, op0=ALU.mult, op1=ALU.mult,
                      accum_out=ac[:, j, hf:hf + 1]), [f"gb{b}", xk], [f"ac{j}_{hf}"])
              ACT(lambda e: e.activation(out=ac[:, j, 3:4], in_=ac[:, j, 0:1], func=AF.Gelu_apprx_tanh,
                                         bias=ac[:, j, 1:2]), [f"ac{j}_0", f"ac{j}_1"], [f"ac{j}"])
              sbk, t6 = tt // 64, tt % 64
              ACT(lambda e: e.activation(out=bigdiag[sbk][:, t6:t6 + 1], in_=ac[:, j, 3:4], func=AF.Copy,
                                         scale=gT[slot][:, tt:tt + 1]), [f"ac{j}", f"gT{slot}"], [f"bg{sbk}_{t6}"])

          def stage_S23(i, tt):
              slot = i % 2
              b, j = tokst.pop((i, tt))
              sbk, t6 = tt // 64, tt % 64
              bkey = f"bg{sbk}_{t6}"
              lhs = bigf[sbk][:, 0:4096].rearrange("p (t m) -> p t m", m=64)[:, t6, :]

              def mv(e):
                  e.matmul(psC[sbk * 64:(sbk + 1) * 64, :], lhsT=lhs, rhs=gbuf[b][:, 1024:1536],
                           start=(t6 == 0), stop=(t6 == 63))
                  return e.matmul(psA[sbk * 64:(sbk + 1) * 64, 1024:1536], lhsT=lhs, rhs=gbuf[b][:, 1536:2048],
                                  start=(t6 == 0), stop=(t6 == 63))
              PE(mv, [bkey, f"gb{b}"], ["psC", "psA2"])

          def stage_E(i):
              which = 0 if i < 4 else 1
              post_ln(i, psC[:, 0:512], "psC", psA[:, 1024:1536], "psA2", which, e_scr0, "scr0")

          ck("mod2")
          for _ in stage_A(0):
              pass
          ck("A")
          toks = [(i, tt) for i in range(NT) for tt in range(128)]
          stage_S1(*toks[0])
          stage_S1(*toks[1])
          genA = iter(())
          for n_t, (i, tt) in enumerate(toks):
              if tt == 0:
                  genA = stage_A(i + 1) if i + 1 < NT else iter(())
              if tt == 110:
                  for _ in genA:
                      pass
              if n_t + 2 < len(toks):
                  stage_S1(*toks[n_t + 2])
              stage_S23(i, tt)
              next(genA, None)
              if tt == 127:
                  stage_E(i)
          if l == 0:
              dbg("x2", x[:, 0, :], ["x0"])
          barrier()

    except _Stop:
        pass

    for t in range(NT):
        DMA("sp", o_y[t], x[:, t, :], r=[f"x{t}"])

    import contextlib
    sems = {}
    es = contextlib.ExitStack()
    for e_ in S.ENG:
        sems[("e", e_)] = es.enter_context(nc.semaphore(f"s_{e_}"))
    for i_ in range(len(S.dma_n)):
        sems[("d", i_)] = es.enter_context(nc.semaphore(f"d_{i_}"))
    with nc.Block() as block:

        def sem_of(key):
            return sems[key]

        def replay(name, eng):
            for (waits, fn, me, inc) in S.q[name]:
                for (s, v) in waits:
                    eng.wait_ge(sem_of(s), v)
                ins = fn(eng)
                ins.then_inc(sem_of(me[0]), inc)
            for (s, v) in S.final_waits(name):
                eng.wait_ge(sem_of(s), v)

        @block.tensor
        def _(e):
            replay("pe", e)

        @block.scalar
        def _(e):
            replay("act", e)

        @block.vector
        def _(e):
            replay("dve", e)

        @block.gpsimd
        def _(e):
            REGS["bc"] = e.alloc_register("bc")
            e.reg_mov(REGS["bc"], NEXP - 1)
            replay("pool", e)

        @block.sync
        def _(e):
            replay("sp", e)
    es.close()
    return nc


def _prep_inputs(inp):
    f = lambda a: np.ascontiguousarray(np.asarray(a, dtype=np.float32))
    x_prompt, x_sample, c = f(inp["x_prompt"]), f(inp["x_sample"]), f(inp["c"])
    c_ctx = f(inp["c_ctx"])
    cosb, sinb = _rope_tables(64)
    cosc, sinc = _rope_tables(32)
    shared = {
        "w_mod": f(inp["w_mod"]), "b_mod": f(inp["b_mod"]), "w_in": f(inp["w_in"]), "w_out": f(inp["w_out"]),
        "chunk_wT": f(np.transpose(f(inp["chunk_w"]), (0, 3, 1, 2))),
        "chunk_bT": f(np.transpose(f(inp["chunk_b"]), (0, 2, 1))),
        "win_sink": f(inp["win_sink"]),
        "lam_q": f(f(inp["diff_lam_q"]).reshape(DEPTH, 64)), "lam_k": f(f(inp["diff_lam_k"]).reshape(DEPTH, 64)),
        "subln_g": f(inp["diff_subln_g"]),
        "pool_wr": f(np.transpose(f(inp["pool_w"]), (0, 2, 1, 3))),
        "pool_scale": f(inp["pool_scale"]), "ln_g": f(inp["ln_g"]), "ln_b": f(inp["ln_b"]),
        "peer_wq": f(inp["peer_wq"]),
        "peer_uv": np.concatenate([f(inp["peer_u"]).reshape(DEPTH * NEXP, D), f(inp["peer_v"]).reshape(DEPTH * NEXP, D)], axis=1),
        "c_ident": np.eye(128, dtype=np.float32), "c_rb": _rot_matrix(64, 2), "c_rc": _rot_matrix(32, 2),
        "c_cosb": f(np.tile(cosb.T, (2, 1))), "c_sinb": f(np.tile(sinb.T, (2, 1))),
        "c_cosc": f(np.tile(cosc.T, (2, 1))), "c_sinc": f(np.tile(sinc.T, (2, 1))),
        "c_band": _band_bias(), "c_poolat": f(np.transpose(_pool_mats(), (2, 0, 1, 3))),
    }
    keys = f(inp["peer_keys"])
    kb = np.zeros((DEPTH, 128, 256), np.float32)
    for j in range(2):
        kb[:, j * 64:(j + 1) * 64, j * 128:(j + 1) * 128] = np.transpose(keys[:, j], (0, 2, 1))
    shared["peer_kb"] = kb
    cwk, cwv = f(inp["cache_win_k"]), f(inp["cache_win_v"])
    cdk, cdv = f(inp["cache_diff_k"]), f(inp["cache_diff_v"])
    maps = []
    for core in range(8):
        b = core // 4
        xin = np.concatenate([x_prompt[2 * core].reshape(2, 128, D), x_prompt[2 * core + 1].reshape(2, 128, D),
                              x_sample[b].reshape(8, 128, D)], 0)
        cvec = np.stack([c_ctx.reshape(8, 128).T, c[b].reshape(8, 128).T], 1)
        m = dict(shared)
        m["xin"] = f(xin)
        m["cvec"] = f(cvec)
        m["cwkT"] = f(np.transpose(cwk[b].reshape(DEPTH, 256, 128), (0, 2, 1)))
        m["cwv"] = f(cwv[b].reshape(DEPTH, 256, 128))
        m["cdkT"] = f(np.transpose(cdk[b].reshape(DEPTH, 256, 4, 64), (0, 3, 2, 1)))
        m["cdv"] = f(cdv[b].reshape(DEPTH, 256, 256))
        maps.append(m)
    return maps


_NC_CACHE = {}


def kernel(**inputs):
    maps = _prep_inputs(inputs)
    if "nc" not in _NC_CACHE:
        _NC_CACHE["nc"] = build_program()
    nc = _NC_CACHE["nc"]
    res = run_bass_kernel_spmd(nc, maps, core_ids=list(range(8)))
    R = res.results
    y_p = np.zeros((16, 256, D), np.float32)
    y_s = np.zeros((2, 1024, D), np.float32)
    nwk = np.zeros((16, DEPTH, 256, 2, 64), np.float32)
    nwv = np.zeros((16, DEPTH, 256, 2, 64), np.float32)
    ndk = np.zeros((16, DEPTH, 256, 4, 2, 32), np.float32)
    ndv = np.zeros((16, DEPTH, 256, 4, 64), np.float32)
    for core in range(8):
        y = np.asarray(R[core]["y"])
        for s in range(2):
            y_p[2 * core + s] = y[2 * s:2 * s + 2].reshape(256, D)
            nwk[2 * core + s] = np.asarray(R[core]["o_wk"])[s].reshape(DEPTH, 256, 2, 64)
            nwv[2 * core + s] = np.asarray(R[core]["o_wv"])[s].reshape(DEPTH, 256, 2, 64)
            ndk[2 * core + s] = np.asarray(R[core]["o_dk"])[s].reshape(DEPTH, 256, 4, 2, 32)
            ndv[2 * core + s] = np.asarray(R[core]["o_dv"])[s].reshape(DEPTH, 256, 4, 64)
        if core % 4 == 0:
            y_s[core // 4] = y[4:12].reshape(1024, D)
    kernel.last_results = R
    return (y_p, y_s, nwk, nwv, ndk, ndv)
```
